# Optimizing a Trainium2 kernel written in Bass

```python
import math
import jax, jax.numpy as jnp
from jax import lax
import numpy as np


D_MODEL = 2048
BATCH = 1
SEQ = 8192
DEPTH = 2
DEC_BATCH = 128
DEC_SEQ = 8
PAST_LEN = 2048
PAGE_SIZE = 128

N_A_LAYERS = DEPTH // 2
N_B_LAYERS = DEPTH - N_A_LAYERS
SSM_GROUP = 16
SSM_GROUPS = D_MODEL // SSM_GROUP
SSM_STATE = 64
HEAD_DIM = 128
N_HEADS = D_MODEL // HEAD_DIM
N_KV_HEADS = 4
Q_PER_KV = N_HEADS // N_KV_HEADS
D_FF = 4 * D_MODEL
PLE_DIM = 256
BLOCK_Q = 128
LN_EPS = 1e-5
DN_ALPHA = (2.0 * DEPTH) ** 0.25
DN_BETA = (8.0 * DEPTH) ** -0.25
DT_MIN = 0.001
DT_MAX = 0.1
SB_BIAS_INIT = -6.0

kernel_name = 'yoco_s5_stickbreaking_decoder_step'


def layer_norm(x, g, b):
    xf = x.astype(jnp.float32)
    mu = jnp.mean(xf, axis=-1, keepdims=True)
    var = jnp.mean(jnp.square(xf - mu), axis=-1, keepdims=True)
    return ((xf - mu) * lax.rsqrt(var + LN_EPS) * g.astype(jnp.float32) + b.astype(jnp.float32)).astype(x.dtype)


def post_norm(x, sub, g, b):
    return layer_norm(DN_ALPHA * x + sub.astype(x.dtype), g, b)


def _ssm_combine(e1, e2):
    a1r, a1i, b1r, b1i = e1
    a2r, a2i, b2r, b2i = e2
    return (a1r * a2r - a1i * a2i,
            a1r * a2i + a1i * a2r,
            a2r * b1r - a2i * b1i + b2r,
            a2r * b1i + a2i * b1r + b2i)


def s5_mixer(x, h0_re, h0_im, lam_re, lam_im, log_dt, b_re, b_im, c_re, c_im, d_skip, w_gate, w_out):
    bsz, t, _ = x.shape
    f32 = jnp.float32
    lam_re = lam_re.astype(f32)
    lam_im = lam_im.astype(f32)
    dt = jnp.exp(log_dt.astype(f32))[:, None]
    mag = jnp.exp(lam_re * dt)
    ang = lam_im * dt
    ab_re = mag * jnp.cos(ang)
    ab_im = mag * jnp.sin(ang)
    den = lam_re * lam_re + lam_im * lam_im
    nr = ab_re - 1.0
    f_re = (nr * lam_re + ab_im * lam_im) / den
    f_im = (ab_im * lam_re - nr * lam_im) / den
    b_re = b_re.astype(f32)
    b_im = b_im.astype(f32)
    bb_re = f_re[..., None] * b_re - f_im[..., None] * b_im
    bb_im = f_re[..., None] * b_im + f_im[..., None] * b_re
    u = x.astype(f32).reshape(bsz, t, SSM_GROUPS, SSM_GROUP)
    bu_re = jnp.einsum('btgc,gpc->btgp', u, bb_re)
    bu_im = jnp.einsum('btgc,gpc->btgp', u, bb_im)
    a_re = jnp.broadcast_to(ab_re, bu_re.shape)
    a_im = jnp.broadcast_to(ab_im, bu_im.shape)
    acum_re, acum_im, h_re, h_im = lax.associative_scan(
        _ssm_combine, (a_re, a_im, bu_re, bu_im), axis=1)
    h0r = h0_re.astype(f32)[:, None]
    h0i = h0_im.astype(f32)[:, None]
    h_re, h_im = (h_re + acum_re * h0r - acum_im * h0i,
                  h_im + acum_re * h0i + acum_im * h0r)
    y = (jnp.einsum('btgp,gcp->btgc', h_re, c_re.astype(f32))
         - jnp.einsum('btgp,gcp->btgc', h_im, c_im.astype(f32)))
    y = y.reshape(bsz, t, D_MODEL) + d_skip.astype(f32) * x.astype(f32)
    g = jax.nn.gelu(y).astype(x.dtype)
    out = (g * jax.nn.sigmoid(g @ w_gate)) @ w_out
    return out, h_re[:, -1], h_im[:, -1]


def _sb_block(q, k, v, q_pos, k_pos, bias):
    bsz, qb, _, _ = q.shape
    f32 = jnp.float32
    qg = q.astype(f32).reshape(bsz, qb, N_KV_HEADS, Q_PER_KV, HEAD_DIM)
    z = jnp.einsum('bqhgd,bkhd->bhgqk', qg, k.astype(f32)) * (HEAD_DIM ** -0.5)
    z = z + bias.astype(f32).reshape(N_KV_HEADS, Q_PER_KV)[None, :, :, None, None]
    causal = k_pos[None, :] < q_pos[:, None]
    log_stay = jnp.where(causal, jax.nn.log_sigmoid(-z), 0.0)
    log_w = jax.nn.log_sigmoid(z) + lax.cumsum(log_stay, axis=4, reverse=True) - log_stay
    w = jnp.where(causal, jnp.exp(log_w), 0.0)
    o = jnp.einsum('bhgqk,bkhd->bqhgd', w, v.astype(f32))
    return o.reshape(bsz, qb, N_HEADS, HEAD_DIM).astype(v.dtype)


def stick_breaking_attention(q, k, v, q_pos, k_pos, bias):
    bsz, t, h, d = q.shape
    qb = BLOCK_Q if t % BLOCK_Q == 0 else t
    nb = t // qb
    q_blocks = q.reshape(bsz, nb, qb, h, d).transpose(1, 0, 2, 3, 4)
    pos_blocks = q_pos.reshape(nb, qb)
    out = lax.map(lambda a: _sb_block(a[0], k, v, a[1], k_pos, bias), (q_blocks, pos_blocks))
    return out.transpose(1, 0, 2, 3, 4).reshape(bsz, t, h, d)


def squared_relu_mlp(x, w1, w2):
    return jnp.square(jax.nn.relu(x @ w1)) @ w2


def per_layer_embedding(x, p, w_proj, w_gate):
    return (p.astype(x.dtype) @ w_proj) * jax.nn.sigmoid(x @ w_gate)


def run_trunk(x, p, h0_re, h0_im, k_past, v_past, pos0, P):
    bsz, t, _ = x.shape
    q_pos = pos0 + jnp.arange(t)
    k_pos = jnp.arange(pos0 + t)
    new_re, new_im = [], []
    k_new = v_new = k_all = v_all = None
    for i in range(DEPTH):
        if i < N_A_LAYERS:
            mix, hr, hi = s5_mixer(x, h0_re[i], h0_im[i], P['s5_lam_re'][i], P['s5_lam_im'][i],
                                   P['s5_log_dt'][i], P['s5_b_re'][i], P['s5_b_im'][i],
                                   P['s5_c_re'][i], P['s5_c_im'][i], P['s5_d'][i],
                                   P['s5_w_gate'][i], P['s5_w_out'][i])
            new_re.append(hr)
            new_im.append(hi)
        else:
            if i == N_A_LAYERS:
                k_new = (x @ P['w_k']).reshape(bsz, t, N_KV_HEADS, HEAD_DIM)
                v_new = (x @ P['w_v']).reshape(bsz, t, N_KV_HEADS, HEAD_DIM)
                k_all = jnp.concatenate([k_past.astype(k_new.dtype), k_new], axis=1)
                v_all = jnp.concatenate([v_past.astype(v_new.dtype), v_new], axis=1)
            j = i - N_A_LAYERS
            q = (x @ P['sb_w_q'][j]).reshape(bsz, t, N_HEADS, HEAD_DIM)
            o = stick_breaking_attention(q, k_all, v_all, q_pos, k_pos, P['sb_bias'][j])
            mix = o.reshape(bsz, t, N_HEADS * HEAD_DIM) @ P['sb_w_o'][j]
        x = post_norm(x, mix, P['ln_mix_g'][i], P['ln_mix_b'][i])
        x = post_norm(x, squared_relu_mlp(x, P['mlp_w1'][i], P['mlp_w2'][i]),
                      P['ln_mlp_g'][i], P['ln_mlp_b'][i])
        x = post_norm(x, per_layer_embedding(x, p[i], P['ple_w_proj'][i], P['ple_w_gate'][i]),
                      P['ln_ple_g'][i], P['ln_ple_b'][i])
    return x, jnp.stack(new_re), jnp.stack(new_im), k_new, v_new


def setup_inputs(seed: int = 0) -> dict:
    key = jax.random.key(seed)
    ks = jax.random.split(key, 40)
    f32 = jnp.float32

    def dense(k, shape, fan_in, scale=1.0):
        return jax.random.normal(k, shape, f32) * (scale * fan_in ** -0.5)

    n_pages = PAST_LEN // PAGE_SIZE
    n_used = DEC_BATCH * n_pages
    n_phys = (5 * n_used + 3) // 4
    page_table = jax.random.permutation(ks[8], n_phys)[:n_used].reshape(DEC_BATCH, n_pages).astype(jnp.int32)
    state_shape = (N_A_LAYERS, DEC_BATCH, SSM_GROUPS, SSM_STATE)
    gp = (N_A_LAYERS, SSM_GROUPS, SSM_STATE)
    return {
        'x_prompt': jax.random.normal(ks[0], (BATCH, SEQ, D_MODEL), f32),
        'x_sample': jax.random.normal(ks[1], (DEC_BATCH, DEC_SEQ, D_MODEL), f32),
        'p_prompt': jax.random.normal(ks[2], (DEPTH, BATCH, SEQ, PLE_DIM), f32),
        'p_sample': jax.random.normal(ks[3], (DEPTH, DEC_BATCH, DEC_SEQ, PLE_DIM), f32),
        'state_ssm_re': 0.1 * jax.random.normal(ks[4], state_shape, f32),
        'state_ssm_im': 0.1 * jax.random.normal(ks[5], state_shape, f32),
        'cache_k': jax.random.normal(ks[6], (n_phys, PAGE_SIZE, N_KV_HEADS, HEAD_DIM), f32),
        'cache_v': DN_BETA * jax.random.normal(ks[7], (n_phys, PAGE_SIZE, N_KV_HEADS, HEAD_DIM), f32),
        'page_table': page_table,
        's5_lam_re': -0.5 + 0.01 * jax.random.normal(ks[9], gp, f32),
        's5_lam_im': math.pi * jnp.arange(SSM_STATE, dtype=f32) + 0.01 * jax.random.normal(ks[10], gp, f32),
        's5_log_dt': jax.random.uniform(ks[11], (N_A_LAYERS, SSM_GROUPS), f32,
                                        math.log(DT_MIN), math.log(DT_MAX)),
        's5_b_re': dense(ks[12], (N_A_LAYERS, SSM_GROUPS, SSM_STATE, SSM_GROUP), 2 * SSM_GROUP),
        's5_b_im': dense(ks[13], (N_A_LAYERS, SSM_GROUPS, SSM_STATE, SSM_GROUP), 2 * SSM_GROUP),
        's5_c_re': dense(ks[14], (N_A_LAYERS, SSM_GROUPS, SSM_GROUP, SSM_STATE), SSM_STATE),
        's5_c_im': dense(ks[15], (N_A_LAYERS, SSM_GROUPS, SSM_GROUP, SSM_STATE), SSM_STATE),
        's5_d': jax.random.normal(ks[16], (N_A_LAYERS, D_MODEL), f32),
        's5_w_gate': dense(ks[17], (N_A_LAYERS, D_MODEL, D_MODEL), D_MODEL),
        's5_w_out': dense(ks[18], (N_A_LAYERS, D_MODEL, D_MODEL), D_MODEL, DN_BETA),
        'w_k': dense(ks[19], (D_MODEL, N_KV_HEADS * HEAD_DIM), D_MODEL),
        'w_v': dense(ks[20], (D_MODEL, N_KV_HEADS * HEAD_DIM), D_MODEL, DN_BETA),
        'sb_w_q': dense(ks[21], (N_B_LAYERS, D_MODEL, N_HEADS * HEAD_DIM), D_MODEL),
        'sb_w_o': dense(ks[22], (N_B_LAYERS, N_HEADS * HEAD_DIM, D_MODEL), N_HEADS * HEAD_DIM, DN_BETA),
        'sb_bias': SB_BIAS_INIT + 0.1 * jax.random.normal(ks[33], (N_B_LAYERS, N_HEADS), f32),
        'ln_mix_g': 1.0 + 0.02 * jax.random.normal(ks[23], (DEPTH, D_MODEL), f32),
        'ln_mix_b': 0.02 * jax.random.normal(ks[24], (DEPTH, D_MODEL), f32),
        'mlp_w1': dense(ks[25], (DEPTH, D_MODEL, D_FF), D_MODEL),
        'mlp_w2': dense(ks[26], (DEPTH, D_FF, D_MODEL), D_FF, DN_BETA),
        'ln_mlp_g': 1.0 + 0.02 * jax.random.normal(ks[27], (DEPTH, D_MODEL), f32),
        'ln_mlp_b': 0.02 * jax.random.normal(ks[28], (DEPTH, D_MODEL), f32),
        'ple_w_proj': dense(ks[29], (DEPTH, PLE_DIM, D_MODEL), PLE_DIM, DN_BETA),
        'ple_w_gate': dense(ks[30], (DEPTH, D_MODEL, D_MODEL), D_MODEL),
        'ln_ple_g': 1.0 + 0.02 * jax.random.normal(ks[31], (DEPTH, D_MODEL), f32),
        'ln_ple_b': 0.02 * jax.random.normal(ks[32], (DEPTH, D_MODEL), f32),
    }


def reference(x_prompt, x_sample, p_prompt, p_sample, state_ssm_re, state_ssm_im, cache_k, cache_v,
              page_table, s5_lam_re, s5_lam_im, s5_log_dt, s5_b_re, s5_b_im, s5_c_re, s5_c_im, s5_d,
              s5_w_gate, s5_w_out, w_k, w_v, sb_w_q, sb_w_o, sb_bias, ln_mix_g, ln_mix_b, mlp_w1, mlp_w2,
              ln_mlp_g, ln_mlp_b, ple_w_proj, ple_w_gate, ln_ple_g, ln_ple_b):
    params = dict(s5_lam_re=s5_lam_re, s5_lam_im=s5_lam_im, s5_log_dt=s5_log_dt, s5_b_re=s5_b_re,
                  s5_b_im=s5_b_im, s5_c_re=s5_c_re, s5_c_im=s5_c_im, s5_d=s5_d, s5_w_gate=s5_w_gate,
                  s5_w_out=s5_w_out, w_k=w_k, w_v=w_v, sb_w_q=sb_w_q, sb_w_o=sb_w_o, sb_bias=sb_bias,
                  ln_mix_g=ln_mix_g, ln_mix_b=ln_mix_b, mlp_w1=mlp_w1, mlp_w2=mlp_w2,
                  ln_mlp_g=ln_mlp_g, ln_mlp_b=ln_mlp_b, ple_w_proj=ple_w_proj, ple_w_gate=ple_w_gate,
                  ln_ple_g=ln_ple_g, ln_ple_b=ln_ple_b)
    n_dec, n_pages = page_table.shape
    past_len = n_pages * cache_k.shape[1]
    k_past = cache_k[page_table].reshape(n_dec, past_len, N_KV_HEADS, HEAD_DIM)
    v_past = cache_v[page_table].reshape(n_dec, past_len, N_KV_HEADS, HEAD_DIM)
    bsz = x_prompt.shape[0]
    h0 = jnp.zeros((N_A_LAYERS, bsz, SSM_GROUPS, SSM_STATE), jnp.float32)
    empty = jnp.zeros((bsz, 0, N_KV_HEADS, HEAD_DIM), x_prompt.dtype)
    y_prompt, sre_p, sim_p, k_p, v_p = run_trunk(x_prompt, p_prompt, h0, h0, empty, empty, 0, params)
    y_sample, sre_s, sim_s, k_s, v_s = run_trunk(x_sample, p_sample, state_ssm_re, state_ssm_im,
                                                 k_past, v_past, past_len, params)
    return (y_prompt, y_sample, sre_p, sim_p, k_p, v_p, sre_s, sim_s, k_s, v_s)
```

```python
import math
from contextlib import ExitStack
import numpy as np
import ml_dtypes
import concourse.bass as bass
import concourse.mybir as mybir
from concourse.bass_utils import run_bass_kernel_spmd

F32 = mybir.dt.float32
BF16 = mybir.dt.bfloat16
I32 = mybir.dt.int32
AF = mybir.ActivationFunctionType
ALU = mybir.AluOpType

NC = 8
TP = 1024
TS = 128
T = TP + TS
TT = 384
NTT = 3
D = 2048
KC = 16
DFF = 8192
ALPHA = 4.0 ** 0.25
EPS = 1e-5
NEG = -30000.0
STAGE = 2
SAME_ENG_INORDER = ("pe", "dve", "act")


class _Op:
    __slots__ = ("eng", "fn", "deps", "dma", "inc", "val", "waits", "ep", "strict")


class Prog:
    ENGS = ("pe", "act", "dve", "pool", "sp")

    def __init__(self, nc):
        self.nc = nc
        self.ops = []
        self.last_w = {}
        self.readers = {}
        self.epoch = 0
        self.strict = True

    def add(self, eng, fn, reads=(), writes=(), dma=None):
        i = len(self.ops)
        deps = set()
        for t in reads:
            w = self.last_w.get(t)
            if w is not None:
                deps.add(w)
        for t in writes:
            w = self.last_w.get(t)
            if w is not None:
                deps.add(w)
            r = self.readers.get(t)
            if r:
                deps.update(r)
        op = _Op()
        op.eng, op.fn, op.deps, op.dma = eng, fn, deps, dma
        op.inc, op.val, op.waits = False, 0, None
        op.ep = self.epoch
        op.strict = self.strict
        for t in writes:
            self.last_w[t] = i
            self.readers[t] = []
        for t in reads:
            self.readers.setdefault(t, []).append(i)
        self.ops.append(op)
        return i

    def barrier(self, new_epoch=False):
        toks = list(self.last_w.keys() | self.readers.keys())
        idxs = []
        for e in self.ENGS:
            idxs.append(self.add(e, None, reads=(), writes=toks))
        if new_epoch:
            self.epoch += 1
        return idxs

    def finalize(self):
        ops = self.ops
        for op in ops:
            for d in op.deps:
                dep = ops[d]
                if dep.dma is None:
                    if dep.eng == op.eng and op.dma is None and (dep.eng == "pe" or (dep.eng in SAME_ENG_INORDER and not (op.strict or dep.strict))):
                        continue
                    dep.inc = True
        cnt = {}
        scnt = {}
        for op in ops:
            if op.dma is None:
                if op.inc:
                    k_ = (op.eng, op.ep)
                    cnt[k_] = cnt.get(k_, 0) + 1
                    op.val = cnt[k_]
            else:
                scnt[op.dma] = scnt.get(op.dma, 0) + 16
                op.val = scnt[op.dma]
        seen = {e: {} for e in self.ENGS}
        for op in ops:
            need = {}
            for d in op.deps:
                dep = ops[d]
                if dep.dma is None:
                    if dep.eng == op.eng and op.dma is None and (dep.eng == "pe" or (dep.eng in SAME_ENG_INORDER and not (op.strict or dep.strict))):
                        continue
                    key = ("e", dep.eng, dep.ep)
                else:
                    key = ("d", dep.dma)
                if dep.val > need.get(key, 0):
                    need[key] = dep.val
            s = seen[op.eng]
            w = []
            for key, v in need.items():
                if s.get(key, 0) < v:
                    s[key] = v
                    w.append((key, v))
            op.waits = w
        self.streams = sorted(scnt.keys(), key=str)
        self.nep = self.epoch + 1
        print("ops", len(ops), "epochs", self.nep, "streams", len(self.streams), "max eng cnt", max(cnt.values()), "max dma cnt", max(scnt.values()))

    def emit(self):
        nc = self.nc
        self.finalize()
        NSET = self.nep
        pool_ = [{e: nc.alloc_semaphore(name=f"se_{e}_{k}") for e in self.ENGS} for k in range(NSET)]
        sems = {}
        for ep in range(self.nep):
            for e in self.ENGS:
                sems[("e", e, ep)] = pool_[ep % NSET][e]
        for k, s in enumerate(self.streams):
            sems[("d", s)] = nc.alloc_semaphore(name=f"sd_{k}")
        ops = self.ops

        def run(engname, e):
            for op in ops:
                if op.eng != engname:
                    continue
                for key, v in op.waits:
                    e.wait_ge(sems[key], v)
                if op.fn is None:
                    if op.inc:
                        e.nop().then_inc(sems[("e", engname, op.ep)], 1)
                    continue
                ins = op.fn(e)
                if op.dma is not None:
                    ins.then_inc(sems[("d", op.dma)], 16)
                elif op.inc:
                    ins.then_inc(sems[("e", engname, op.ep)], 1)

        with nc.Block() as block:
            @block.tensor
            def _(e):
                run("pe", e)

            @block.scalar
            def _(e):
                run("act", e)

            @block.vector
            def _(e):
                run("dve", e)

            @block.gpsimd
            def _(e):
                run("pool", e)

            @block.sync
            def _(e):
                run("sp", e)


class Builder:
    def __init__(self, nc):
        self.nc = nc
        self.P = Prog(nc)
        self.dram = {}

    def mm(self, out, lhsT, rhs, start, stop, r, w):
        self.P.add("pe", lambda e: e.matmul(out, lhsT, rhs, start=start, stop=stop), r, w)

    def tr(self, out, in_, ident, r, w):
        self.P.add("pe", lambda e: e.transpose(out, in_, ident), r, w)

    def act(self, out, in_, func, r, w, bias=None, scale=None):
        kw = {}
        if bias is not None:
            kw["bias"] = bias
        if scale is not None:
            kw["scale"] = scale
        self.P.add("act", lambda e: e.activation(out, in_, func, **kw), r, w)

    def tt(self, eng, out, in0, in1, op, r, w):
        self.P.add(eng, lambda e: e.tensor_tensor(out, in0, in1, op), r, w)

    def ts(self, eng, out, in0, s1, s2, op0, op1, r, w):
        if s2 is None:
            self.P.add(eng, lambda e: e.tensor_scalar(out, in0, s1, None, op0), r, w)
        else:
            self.P.add(eng, lambda e: e.tensor_scalar(out, in0, s1, s2, op0, op1), r, w)

    def stt(self, eng, out, in0, scalar, in1, op0, op1, r, w):
        self.P.add(eng, lambda e: e.scalar_tensor_tensor(out, in0, scalar, in1, op0, op1), r, w)

    def cp(self, eng, out, in_, r, w):
        if eng == "act":
            self.P.add("act", lambda e: e.activation(out, in_, AF.Copy), r, w)
        else:
            self.P.add(eng, lambda e: e.tensor_copy(out, in_), r, w)

    def memset(self, eng, ap, val, w):
        self.P.add(eng, lambda e: e.memset(ap, val), (), w)

    def dma(self, q, out, in_, r, w, stream):
        w = list(w)
        if stream == "prm":
            w.append("_prmchain")
        self.P.add(q, lambda e: e.dma_start(out=out, in_=in_), r, w, dma=stream)


def _sig(eng_b, out, in_, r, w, scale=1.0):
    eng_b.act(out, in_, AF.Sigmoid, r, w, scale=scale)


def build_program():
    nc = bass.Bass("TRN2", target_bir_lowering=False)
    B = Builder(nc)
    P = B.P

    def din(name, shape, dt=F32):
        return nc.dram_tensor(name, list(shape), dt, kind="ExternalInput").ap()

    def dout(name, shape, dt=F32):
        return nc.dram_tensor(name, list(shape), dt, kind="ExternalOutput").ap()

    def dint(name, shape, dt=F32):
        return nc.dram_tensor(name, list(shape), dt, kind="Internal").ap()

    xTp = din("xTp", [8 * 128, KC * 1024])
    xTs = din("xTs", [128, KC * 128])
    pTp = [din(f"pTp{l}", [8 * 128, 2 * 1024]) for l in range(2)]
    pTs = [din(f"pTs{l}", [128, 2 * 128]) for l in range(2)]
    h0re = din("h0re", [128, 64 * 16])
    h0im = din("h0im", [128, 64 * 16])
    sLR = din("sLR", [128, 2048])
    sLI = din("sLI", [128, 2048])
    sDT = din("sDT", [128, 2048])
    sBre = din("sBre", [128, 2048])
    sBim = din("sBim", [128, 2048])
    lr2 = din("lr2", [128, 64])
    li2 = din("li2", [128, 64])
    dt2 = din("dt2", [128, 64])
    cpad = din("cpad", [16, 128, 1024])
    vecs = din("vecs", [128, 13 * 16])
    sbb = din("sbb", [128, 16])
    brow = din("brow", [2, 128])
    m01 = din("m01", [2, 2])
    ptab = din("ptab", [128, 256], I32)
    ridx = din("ridx", [128, 8], I32)
    cmask = din("cmask", [128, 8])
    c_ident = din("c_ident", [128, 128])
    c_trineg = din("c_trineg", [128, 128])
    c_onesneg = din("c_onesneg", [128, 128])
    c_onesmean = din("c_onesmean", [128, 128])
    c_trimask = din("c_trimask", [128, 128])
    c_smask = din("c_smask", [8, 128])
    c_dsamp = din("c_dsamp", [128, 128])
    c_m96 = din("c_m96", [128, 1])
    c_iota = din("c_iota", [128, 1])
    w_gate = din("s5_w_gate", [D, D])
    w_out = din("s5_w_out", [D, D])
    w_k = din("w_k", [D, 512])
    w_v = din("w_v", [D, 512])
    w_q = din("sb_w_q", [D, D])
    w_o = din("sb_w_o", [D, D])
    w1 = [din(f"mlp_w1_{l}", [D, DFF]) for l in range(2)]
    w2 = [din(f"mlp_w2_{l}", [DFF, D]) for l in range(2)]
    wpp = [din(f"ple_w_proj_{l}", [256, D]) for l in range(2)]
    wpg = [din(f"ple_w_gate_{l}", [D, D]) for l in range(2)]
    if STAGE == 2:
        cache_k = din("cache_k", [2560 * 128, 512])
        cache_v = din("cache_v", [2560 * 128, 512])

    o_y = dout("o_y", [128, KC * T])
    o_sre = dout("o_sre", [128, 64])
    o_sim = dout("o_sim", [128, 64])
    o_ssre = dout("o_ssre", [128, 64 * 16])
    o_ssim = dout("o_ssim", [128, 64 * 16])
    o_kTp = dout("o_kTp", [8 * 128, 4096])
    o_vp = dout("o_vp", [8 * 128, 4096])
    o_kTs = dout("o_kTs", [128, 512])
    o_vs = dout("o_vs", [128, 512])

    KTa = [dint(f"KTa{k}", [8 * 128, 1024], BF16) for k in range(4)]
    Va = [dint(f"Va{k}", [8 * 128, 1024], BF16) for k in range(4)]
    KTs_d = dint("KTs_d", [128, 512], BF16)
    Vs_d = dint("Vs_d", [128, 512], BF16)
    X1 = [dint(f"X1_{k}", [8 * 128, 8 * T]) for k in range(2)]
    X1s = dint("X1s", [128, KC * 128])
    TAB = [dint(f"TAB{k}", [128, 3 * 512]) for k in range(64)]

    es = ExitStack()
    _u = {"n": 0}

    def sbt(name, shape, dt=F32):
        _u["n"] += 1
        return nc.sbuf_tensor(f"{name}_u{_u['n']}", list(shape), dt)

    def sb(name, shape, dt=F32):
        return es.enter_context(nc.sbuf_tensor(name, list(shape), dt))

    def ps(name, shape, dt=F32):
        return es.enter_context(nc.psum_tensor(name, list(shape), dt))

    with es:
        xres = sb("xres", [128, KC, T])
        xb = sb("xb", [128, KC, T], BF16)
        hA = sb("hA", [128, KC, T], BF16)
        vec = sb("vec", [128, 13 * 16])
        ident = sb("ident", [128, 128], BF16)
        trineg = sb("trineg", [128, 128], BF16)
        onesneg = sb("onesneg", [128, 128], BF16)
        onesmean = sb("onesmean", [128, 128], BF16)
        trimask = sb("trimask", [128, 128], BF16)
        smask = sb("smask", [8, 128], BF16)
        PS = [ps(f"ps{i}", [128, 512]) for i in range(8)]

        def pstok(i):
            return f"ps{i}"

        cst_f = xres[:, 0, 0:128]
        for nm, src, dst, npart in (("ident", c_ident, ident, 128), ("trineg", c_trineg, trineg, 128),
                                    ("onesneg", c_onesneg, onesneg, 128), ("onesmean", c_onesmean, onesmean, 128),
                                    ("trimask", c_trimask, trimask, 128), ("smask", c_smask, smask, 8)):
            B.dma("sp", cst_f[0:npart, :], src, (), ["xres0"], "cst")
            B.cp("dve", dst[0:npart, :], cst_f[0:npart, :], ["xres0"], [nm])
        B.dma("sp", vec[:], vecs, (), ["vec"], "vec")

        def vcol(i, kc):
            return vec[:, i * 16 + kc:i * 16 + kc + 1]


        cfg = {"tts": None, "ncols": 0}

        def load_x(ti):
            nco = cfg["ncols"]
            src = xTp[ti * 128:(ti + 1) * 128, :].rearrange("p (k t) -> p k t", k=KC)
            for g4 in range(4):
                ks = slice(4 * g4, 4 * g4 + 4)
                B.dma("sp", xres[:, ks, 0:1024], src[:, ks, :], (), [f"xres{k}" for k in range(4 * g4, 4 * g4 + 4)], f"xl{g4}")
            if nco > 1024:
                B.dma("sp", xres[:, :, 1024:1152], xTs.rearrange("p (k t) -> p k t", k=KC), (), [f"xres{k}" for k in range(KC)], "xls")
            P.strict = False
            for kc in range(KC):
                B.cp("act", xb[:, kc, 0:nco], xres[:, kc, 0:nco], [f"xres{kc}"], [f"xb{kc}"])
            P.strict = True

        es_ssm = ExitStack()
        es.enter_context(es_ssm)

        def sbs(name, shape, dt=F32):
            return es_ssm.enter_context(sbt(name, list(shape), dt))
        BBT = sbs("BBT", [128, 16, 2, 128], BF16)
        m96t = sbs("m96t", [128, 1])
        sp2 = sbs("sp2", [128, 8, 64])
        LRc, LIc, DTc, Rr, Cc, Sc, Ta, Tb = [sp2[:, i, :] for i in range(8)]
        Hst = sbs("Hst", [128, 2, 64])
        B.dma("sp", m96t[:], c_m96, (), ["m96t"], "prm")
        B.dma("sp", LRc, lr2, (), ["sp2"], "prm")
        B.dma("sp", LIc, li2, (), ["sp2"], "prm")
        B.dma("sp", DTc, dt2, (), ["sp2"], "prm")
        B.memset("dve", Hst[:].rearrange("p a b -> p (a b)"), 0.0, ["Hst"])

        def lam_bar(lr, li, dtt, mag, c, s, t1, t2, tok):
            B.act(dtt, dtt, AF.Exp, tok, tok)
            B.tt("dve", t1, lr, dtt, ALU.mult, tok, tok)
            B.act(mag, t1, AF.Exp, tok, tok)
            B.tt("dve", t1, li, dtt, ALU.mult, tok, tok)
            B.act(s, t1, AF.Sin, tok, tok, scale=1.0 / 16)
            B.ts("dve", t2, t1, 1.0 / 16, math.pi / 2, ALU.mult, ALU.add, tok, tok)
            B.act(c, t2, AF.Sin, tok, tok)
            for _ in range(4):
                B.tt("dve", t1, c, c, ALU.mult, tok, tok)
                B.tt("dve", t2, s, s, ALU.mult, tok, tok)
                B.stt("dve", s, c, 2.0, s, ALU.mult, ALU.mult, tok, tok)
                B.tt("dve", c, t1, t2, ALU.subtract, tok, tok)

        lam_bar(LRc, LIc, DTc, Rr, Cc, Sc, Ta, Tb, ["sp2"])

        with ExitStack() as es3:
            sl = es3.enter_context(sbt("sl", [128, 12, 512], F32))
            for q in range(4):
                cs_ = slice(q * 512, (q + 1) * 512)
                tk = ["sl"]
                for i, src in enumerate((sLR, sLI, sDT, sBre, sBim)):
                    B.dma("sp", sl[:, i, :], src[:, cs_], (), tk, "prm")
                lr, li, dtt, bre_, bim_, mag, c, s, t1, t2, fr, fi = [sl[:, i, :] for i in range(12)]
                lam_bar(lr, li, dtt, mag, c, s, t1, t2, tk)
                B.tt("dve", c, mag, c, ALU.mult, tk, tk)
                B.tt("dve", s, mag, s, ALU.mult, tk, tk)
                B.ts("dve", c, c, -1.0, None, ALU.add, None, tk, tk)
                B.tt("dve", t1, lr, lr, ALU.mult, tk, tk)
                B.tt("dve", t2, li, li, ALU.mult, tk, tk)
                B.tt("dve", mag, t1, t2, ALU.add, tk, tk)
                P.add("dve", lambda e, o=mag: e.reciprocal(o, o), tk, tk)
                B.tt("dve", t1, c, lr, ALU.mult, tk, tk)
                B.tt("dve", t2, s, li, ALU.mult, tk, tk)
                B.tt("dve", t1, t1, t2, ALU.add, tk, tk)
                B.tt("dve", fr, t1, mag, ALU.mult, tk, tk)
                B.tt("dve", t1, s, lr, ALU.mult, tk, tk)
                B.tt("dve", t2, c, li, ALU.mult, tk, tk)
                B.tt("dve", t1, t1, t2, ALU.subtract, tk, tk)
                B.tt("dve", fi, t1, mag, ALU.mult, tk, tk)
                B.tt("dve", t1, fr, bre_, ALU.mult, tk, tk)
                B.tt("dve", t2, fi, bim_, ALU.mult, tk, tk)
                B.tt("dve", BBT[:, 4 * q:4 * q + 4, 0, :], t1.rearrange("p (a b) -> p a b", a=4),
                     t2.rearrange("p (a b) -> p a b", a=4), ALU.subtract, tk, ["BBT"])
                B.tt("dve", t1, fr, bim_, ALU.mult, tk, tk)
                B.tt("dve", t2, fi, bre_, ALU.mult, tk, tk)
                B.tt("dve", BBT[:, 4 * q:4 * q + 4, 1, :], t1.rearrange("p (a b) -> p a b", a=4),
                     t2.rearrange("p (a b) -> p a b", a=4), ALU.add, tk, ["BBT"])
        P.barrier()

        def ssm_phase(ti):
            with ExitStack() as es2:
                def sb2(name, shape, dt=F32):
                    return es2.enter_context(sbt(name, list(shape), dt))
                tabt = [sb2(f"tabt{i}", [128, 3, 512]) for i in range(2)]
                tsel = {"i": 0}
                decs = [sb2(f"dec{i}", [128, 512]) for i in range(2)]
                if ti == 0:
                    dsr = sb2("dsr", [128, 128])
                    dsamp = sb2("dsamp", [128, 128])
                nbb = 1 if ti == 0 else 2
                bres = [sb2(f"bre{i}", [128, 512]) for i in range(nbb)]; bims = [sb2(f"bim{i}", [128, 512]) for i in range(nbb)]
                t1 = sb2("t1", [128, 512]); t2 = sb2("t2", [128, 512])
                rre = sb2("rre", [128, 512]); rim = sb2("rim", [128, 512])
                gres = [sb2(f"gre{i}", [128, 512]) for i in range(2)]; gims = [sb2(f"gim{i}", [128, 512]) for i in range(2)]
                p1 = sb2("p1", [128, 512]); p2 = sb2("p2", [128, 512])
                gsel = {"n": 0}
                hreb = sb2("hreb", [128, 512], BF16); nhimb = sb2("nhimb", [128, 512], BF16)
                cpb = [sb2(f"cpb{b}", [128, 4, 2, 128], BF16) for b in range(2)]
                hl = sb2("hl", [128, 8])
                if ti == 0:
                    h0t = sb2("h0t", [128, 2, 16])
                    hst = sb2("hst", [128, 2, 16])
                nsm = sb2("nsm", [128, 2])
                bbz = sb2("bbz", [128, 2, 128], BF16)
                if ti == 0:
                    B.dma("sp", dsamp[:], c_dsamp, (), ["dsamp"], "prm")
                tk = ["tab0"]

                def cur_tabs():
                    tt_ = tabt[tsel["i"]]
                    return tt_[:, 0, :], tt_[:, 1, :], tt_[:, 2, :]

                def gen_tables(pair):
                    tsel["i"] = pair % 2
                    tk[0] = f"tab{pair % 2}"
                    ct, st, nst = cur_tabs()
                    dec = decs[pair % 2]
                    dtk = f"dec{pair % 2}"
                    if ti > 0:
                        B.dma("sp", tabt[pair % 2][:].rearrange("p a b -> p (a b)"), TAB[pair], ["TAB"], [tk[0]], f"tabl{pair % 2}")
                        B.ts("pool", dec[:], ct, 0.0, Rr[:, pair:pair + 1], ALU.mult, ALU.add, tk + ["sp2"], [dtk])
                        return
                    geng = "pool" if pair % 2 == 0 else "dve"
                    B.cp(geng, ct[:, 0:1], Cc[:, pair:pair + 1], ["sp2"] + tk, tk)
                    B.cp(geng, st[:, 0:1], Sc[:, pair:pair + 1], ["sp2"] + tk, tk)
                    m = 1
                    dk = tk + [dtk]
                    while m < 512:
                        cm = ct[:, m - 1:m]; sm = st[:, m - 1:m]
                        B.ts(geng, nsm[:, 0:1], sm, -1.0, None, ALU.mult, None, tk + ["nsm"], ["nsm"])
                        B.ts(geng, nst[:, 0:m], ct[:, 0:m], cm, None, ALU.mult, None, dk, dk)
                        B.ts(geng, dec[:, 0:m], st[:, 0:m], nsm[:, 0:1], None, ALU.mult, None, dk + ["nsm"], dk)
                        B.tt(geng, ct[:, m:2 * m], nst[:, 0:m], dec[:, 0:m], ALU.add, dk, dk)
                        B.ts(geng, nst[:, 0:m], ct[:, 0:m], sm, None, ALU.mult, None, dk, dk)
                        B.ts(geng, dec[:, 0:m], st[:, 0:m], cm, None, ALU.mult, None, dk, dk)
                        B.tt(geng, st[:, m:2 * m], nst[:, 0:m], dec[:, 0:m], ALU.add, dk, dk)
                        m *= 2
                    B.ts(geng, nst, st, -1.0, None, ALU.mult, None, tk, tk)
                    B.dma("sp", TAB[pair], tabt[pair % 2][:].rearrange("p a b -> p (a b)"), list(tk), ["TAB"], f"tabs{pair % 2}")
                    B.ts(geng, dec[:], ct, 0.0, Rr[:, pair:pair + 1], ALU.mult, ALU.add, tk + ["sp2"], [dtk])
                    if ti == 0:
                        B.ts("pool", dsr[:], dsamp[:], Rr[:, pair:pair + 1], None, ALU.mult, None, ["dsamp", "sp2"], ["dsr"])

                def ssm_tile(pair, kind, init_re, init_im, init_tok):
                    ct, st, nst = cur_tabs()
                    tk = [f"tab{pair % 2}"]
                    gi = gsel["n"] % 2
                    gsel["n"] += 1
                    gre, gim = gres[gi], gims[gi]
                    gtr, gti = f"gre{gi}", f"gim{gi}"
                    reng = "dve" if ti == 0 else "pool"
                    q1, q2 = (t1, t2) if ti == 0 else (p1, p2)
                    q1t, q2t = ("t1", "t2") if ti == 0 else ("p1", "p2")
                    chunk, j = pair // 4, pair % 4
                    rows = slice(32 * j, 32 * j + 32) if j < 3 else slice(64, 128)
                    if kind < 2:
                        N = 512; c0 = 512 * kind
                    else:
                        N = 128; c0 = 1024
                    P.strict = (kind == 2)
                    bi_ = gi % nbb
                    bre, bim = bres[bi_], bims[bi_]
                    bret, bimt = f"bre{bi_}", f"bim{bi_}"
                    pri, pii = (0, 1) if bi_ == 0 else (5, 6)
                    pr_, pi_ = PS[pri], PS[pii]
                    B.mm(pr_[:, 0:N], (BBT[rows, chunk, 0, :] if j < 3 else bbz[rows, 0, :]), xb[rows, chunk, c0:c0 + N], True, True, ["BBT", "bbz", f"xb{chunk}"], [f"ps{pri}"])
                    B.mm(pi_[:, 0:N], (BBT[rows, chunk, 1, :] if j < 3 else bbz[rows, 1, :]), xb[rows, chunk, c0:c0 + N], True, True, ["BBT", "bbz", f"xb{chunk}"], [f"ps{pii}"])
                    B.cp("act", bre[:, 0:N], pr_[:, 0:N], [f"ps{pri}"], [bret])
                    B.cp("act", bim[:, 0:N], pi_[:, 0:N], [f"ps{pii}"], [bimt])

                    def v(ap):
                        return ap[:, 0:N].rearrange("p (a b) -> p a b", b=8) if kind == 2 else ap[:, 0:N]

                    def tb(ap):
                        return ap[:, 0:8].unsqueeze(1).broadcast_to([128, 16, 8]) if kind == 2 else ap[:, 0:N]
                    B.tt("dve", v(t1), v(bre), tb(ct), ALU.mult, [bret] + tk, ["t1"])
                    B.tt("dve", v(t2), v(bim), tb(st), ALU.mult, [bimt] + tk, ["t2"])
                    B.tt("dve", rre[:, 0:N], t1[:, 0:N], t2[:, 0:N], ALU.add, ["t1", "t2"], ["rre"])
                    B.tt("dve", v(t1), v(bim), tb(ct), ALU.mult, [bimt] + tk, ["t1"])
                    B.tt("dve", v(t2), v(bre), tb(st), ALU.mult, [bret] + tk, ["t2"])
                    B.tt("dve", rim[:, 0:N], t1[:, 0:N], t2[:, 0:N], ALU.subtract, ["t1", "t2"], ["rim"])
                    if kind == 2:
                        B.dma("sp", h0t[:, 0, :], h0re[:, pair * 16:(pair + 1) * 16], (), ["h0t"], "h0")
                        B.dma("sp", h0t[:, 1, :], h0im[:, pair * 16:(pair + 1) * 16], (), ["h0t"], "h0")
                        B.stt("dve", v(rre)[:, :, 0], h0t[:, 0, :], Rr[:, pair:pair + 1], v(rre)[:, :, 0], ALU.mult, ALU.add, ["h0t", "sp2", "rre"], ["rre"])
                        B.stt("dve", v(rim)[:, :, 0], h0t[:, 1, :], Rr[:, pair:pair + 1], v(rim)[:, :, 0], ALU.mult, ALU.add, ["h0t", "sp2", "rim"], ["rim"])
                        dec_ = dsr; dtok = "dsr"
                    else:
                        dec_ = decs[pair % 2]; dtok = f"dec{pair % 2}"
                    P.add("dve", lambda e: e.tensor_tensor_scan(gre[:, 0:N], dec_[:, 0:N], rre[:, 0:N], init_re, ALU.mult, ALU.add),
                          ["rre", dtok] + init_tok, [gtr])
                    P.add("dve", lambda e: e.tensor_tensor_scan(gim[:, 0:N], dec_[:, 0:N], rim[:, 0:N], init_im, ALU.mult, ALU.add),
                          ["rim", dtok] + init_tok, [gti])
                    P.strict = True
                    if kind < 2:
                        L = N - 1
                        B.tt("dve", hl[:, 2:3], gre[:, L:N], ct[:, L:N], ALU.mult, [gtr] + tk, ["hl2"])
                        B.stt("dve", hl[:, 0:1], gim[:, L:N], nst[:, L:N], hl[:, 2:3], ALU.mult, ALU.add, [gti, "hl2"] + tk, ["hl"])
                        B.tt("dve", hl[:, 3:4], gre[:, L:N], st[:, L:N], ALU.mult, [gtr] + tk, ["hl3"])
                        B.stt("dve", hl[:, 1:2], gim[:, L:N], ct[:, L:N], hl[:, 3:4], ALU.mult, ALU.add, [gti, "hl3"] + tk, ["hl"])
                    else:
                        B.ts("dve", hst[:, 0, :], v(gre)[:, :, 7], ct[:, 7:8], None, ALU.mult, None, [gtr] + tk, ["hst0"])
                        B.stt("dve", hst[:, 0, :], v(gim)[:, :, 7], nst[:, 7:8], hst[:, 0, :], ALU.mult, ALU.add, [gti, "hst0"] + tk, ["hst0"])
                        B.ts("dve", hst[:, 1, :], v(gre)[:, :, 7], st[:, 7:8], None, ALU.mult, None, [gtr] + tk, ["hst1"])
                        B.stt("dve", hst[:, 1, :], v(gim)[:, :, 7], ct[:, 7:8], hst[:, 1, :], ALU.mult, ALU.add, [gti, "hst1"] + tk, ["hst1"])
                        B.dma("sp", o_ssre[:, pair * 16:(pair + 1) * 16], hst[:, 0, :], ["hst0"], ["o_ssre"], "hso0")
                        B.dma("sp", o_ssim[:, pair * 16:(pair + 1) * 16], hst[:, 1, :], ["hst1"], ["o_ssim"], "hso1")
                    P.strict = (kind == 2)
                    B.tt(reng, v(q1), v(gre), tb(ct), ALU.mult, [gtr] + tk, [q1t])
                    B.tt(reng, v(q2), v(gim), tb(st), ALU.mult, [gti] + tk, [q2t])
                    B.tt(reng, hreb[:, 0:N], q1[:, 0:N], q2[:, 0:N], ALU.subtract, [q1t, q2t], ["hreb"])
                    B.tt(reng, v(q1), v(gre), tb(nst), ALU.mult, [gtr] + tk, [q1t])
                    B.tt(reng, v(q2), v(gim), tb(ct), ALU.mult, [gti] + tk, [q2t])
                    B.tt(reng, nhimb[:, 0:N], q1[:, 0:N], q2[:, 0:N], ALU.subtract, [q1t, q2t], ["nhimb"])
                    py = PS[2 + kind]
                    cb = cpb[chunk % 2]
                    B.mm(py[:, 0:N], cb[:, j, 0, :], hreb[:, 0:N], j == 0, False, [f"cpb{chunk % 2}", "hreb"], [f"ps{2 + kind}"])
                    B.mm(py[:, 0:N], cb[:, j, 1, :], nhimb[:, 0:N], False, j == 3, [f"cpb{chunk % 2}", "nhimb"], [f"ps{2 + kind}"])
                    P.strict = True

                kinds = 3 if ti == 0 else 2
                for chunk in range(16):
                    cb = cpb[chunk % 2]
                    B.dma("pool", cb[:].rearrange("p a b c -> p (a b c)"), cpad[chunk], (), [f"cpb{chunk % 2}"], f"cpb{chunk % 2}")
                    B.ts("dve", bbz[:].rearrange("p a b -> p (a b)"), BBT[:, chunk, :, :].rearrange("p a b -> p (a b)"), m96t[:, 0:1], None,
                         ALU.mult, None, ["BBT", "m96t"], ["bbz"])
                    for j in range(4):
                        pair = chunk * 4 + j
                        gen_tables(pair)
                        ssm_tile(pair, 0, Hst[:, 0, pair:pair + 1], Hst[:, 1, pair:pair + 1], ["Hst"])
                        ssm_tile(pair, 1, hl[:, 0:1], hl[:, 1:2], ["hl"])
                        B.cp("dve", Hst[:, 0, pair:pair + 1], hl[:, 0:1], ["hl"], ["Hst"])
                        B.cp("dve", Hst[:, 1, pair:pair + 1], hl[:, 1:2], ["hl"], ["Hst"])
                        if ti == 0:
                            ssm_tile(pair, 2, 0.0, 0.0, [])
                    for kind in range(kinds):
                        N = 512 if kind < 2 else 128
                        c0 = 512 * kind
                        py = PS[2 + kind]
                        xr = xres[:, chunk, c0:c0 + N]
                        P.strict = (kind == 2)
                        B.stt("dve", rre[:, 0:N], xr, vcol(0, chunk), py[:, 0:N], ALU.mult, ALU.add, [f"xres{chunk}", "vec", f"ps{2 + kind}"], ["rre"])
                        B.act(t1[:, 0:N], rre[:, 0:N], AF.Square, ["rre"], ["t1"])
                        B.ts("dve", t1[:, 0:N], t1[:, 0:N], 0.044715, 1.0, ALU.mult, ALU.add, ["t1"], ["t1"])
                        B.tt("dve", t1[:, 0:N], t1[:, 0:N], rre[:, 0:N], ALU.mult, ["t1", "rre"], ["t1"])
                        B.act(t2[:, 0:N], t1[:, 0:N], AF.Sigmoid, ["t1"], ["t2"], scale=1.5957691216057308)
                        B.tt("dve", hA[:, chunk, c0:c0 + N], rre[:, 0:N], t2[:, 0:N], ALU.mult, ["rre", "t2"], [f"hA{chunk}"])
                        P.strict = True
                if ti == 7:
                    B.dma("sp", o_sre, Hst[:, 0, :], ["Hst"], ["o_sre"], "osp")
                    B.dma("sp", o_sim, Hst[:, 1, :], ["Hst"], ["o_sim"], "osp")
            P.barrier()

        dn = {}
        st_ = {"w": 0, "ps": 0, "e": 0}

        def dense_scope(stack):
            def sb5(name, shape, dt=F32):
                return stack.enter_context(sbt(name, list(shape), dt))
            dn["wbuf"] = [sb5(f"wbuf{i}", [128, 16, 256], BF16) for i in range(3)]
            dn["wps"] = [sb5(f"wps{i}", [128, 2, 256], BF16) for i in range(3)]
            dn["pbt"] = sb5("pbt", [128, 2, T], BF16)
            dn["mean_sb"] = sb5("mean_sb", [128, T])
            dn["rstd_sb"] = sb5("rstd_sb", [128, T])
            dn["etmp"] = [sb5(f"etmp{i}", [128, 512]) for i in range(2)]
            dn["epsb"] = sb5("epsb", [128, 1])
            B.memset("dve", dn["epsb"][:], EPS, ["epsb"])

        def next_ps():
            i = st_["ps"] % 6
            st_["ps"] += 1
            return i

        def next_et():
            i = st_["e"] % 2
            st_["e"] += 1
            return i

        def wview(W, r0, kcn, m0, n):
            return W[r0:r0 + kcn * 128, :].rearrange("(kc p) n -> p kc n", p=128)[:, :, m0:m0 + n]

        WB = {}

        def linear(W, r0, kcn, n_out, in_tile, in_name, epi, extra_load=None, wkey=None):
            wbuf = dn["wbuf"]
            tiles = [(mp * 256, min(256, n_out * 128 - mp * 256)) for mp in range((n_out * 128 + 255) // 256)]
            slots = []
            wmode = cfg.get("wmode") if wkey is not None else None
            if wmode is not None and wkey not in WB:
                WB[wkey] = [dint(f"WB_{wkey}_{i}", [128, kcn * 256], BF16) for i in range(len(tiles))]

            def load(i):
                s = st_["w"] % 3
                st_["w"] += 1
                m0, n = tiles[i]
                if wmode == "use":
                    B.dma("sp", wbuf[s][:, 0:kcn, :].rearrange("p a b -> p (a b)"), WB[wkey][i], [f"WB_{wkey}_{i}"], [f"wbuf{s}"], f"wbuf{s}")
                else:
                    B.dma("pool", wbuf[s][:, 0:kcn, 0:n], wview(W, r0, kcn, m0, n), (), [f"wbuf{s}"], f"wbuf{s}")
                    if wmode == "fill":
                        B.dma("sp", WB[wkey][i], wbuf[s][:, 0:kcn, :].rearrange("p a b -> p (a b)"), [f"wbuf{s}"], [f"WB_{wkey}_{i}"], f"wbs{s}")
                if extra_load is not None:
                    extra_load(s, m0, n)
                slots.append(s)
            for i in range(min(2, len(tiles))):
                load(i)
            for i, (m0, n) in enumerate(tiles):
                s = slots[i]
                for mi in range(n // 128):
                    m = m0 // 128 + mi
                    for (c0, nt) in cfg["tts"]:
                        pi_ = next_ps()
                        P.strict = (nt < 512)
                        for kc in range(kcn):
                            B.mm(PS[pi_][:, 0:nt], wbuf[s][:, kc, mi * 128:(mi + 1) * 128], in_tile[:, kc, c0:c0 + nt],
                                 kc == 0, kc == kcn - 1, [f"wbuf{s}", f"{in_name}{kc}"], [f"ps{pi_}"])
                        epi(m, (c0, nt), PS[pi_][:, 0:nt], f"ps{pi_}", s, mi)
                        P.strict = True
                if i + 2 < len(tiles):
                    load(i + 2)

        def epi_resid(first):
            def f(m, tt, pp, ptok, s, mi):
                c0, nt = tt
                xr = xres[:, m, c0:c0 + nt]
                if first:
                    B.stt("dve", xr, xr, ALPHA, pp, ALU.mult, ALU.add, [f"xres{m}", ptok], [f"xres{m}"])
                else:
                    B.tt("dve", xr, xr, pp, ALU.add, [f"xres{m}", ptok], [f"xres{m}"])
            return f

        def layer_norm(gi, bi):
            nco = cfg["ncols"]
            mean_sb, rstd_sb, epsb = dn["mean_sb"], dn["rstd_sb"], dn["epsb"]
            P.strict = False
            for kc in range(KC):
                B.cp("act", xb[:, kc, 0:nco], xres[:, kc, 0:nco], [f"xres{kc}"], [f"xb{kc}"])
                B.act(hA[:, kc, 0:nco], xres[:, kc, 0:nco], AF.Square, [f"xres{kc}"], [f"hA{kc}"])
            for (c0, nt) in cfg["tts"]:
                P.strict = (nt < 512)
                p1, p2 = next_ps(), next_ps()
                for kc in range(KC):
                    B.mm(PS[p1][:, 0:nt], onesmean[:], xb[:, kc, c0:c0 + nt], kc == 0, kc == KC - 1, ["onesmean", f"xb{kc}"], [f"ps{p1}"])
                for kc in range(KC):
                    B.mm(PS[p2][:, 0:nt], onesmean[:], hA[:, kc, c0:c0 + nt], kc == 0, kc == KC - 1, ["onesmean", f"hA{kc}"], [f"ps{p2}"])
                ms = mean_sb[:, c0:c0 + nt]
                rs = rstd_sb[:, c0:c0 + nt]
                B.cp("dve", ms, PS[p1][:, 0:nt], [f"ps{p1}"], ["mean_sb"])
                B.tt("dve", rs, ms, ms, ALU.mult, ["mean_sb"], ["rstd_sb"])
                B.tt("dve", rs, PS[p2][:, 0:nt], rs, ALU.subtract, [f"ps{p2}", "rstd_sb"], ["rstd_sb"])
                B.act(rs, rs, AF.Sqrt, ["rstd_sb", "epsb"], ["rstd_sb"], bias=epsb[:, 0:1])
                P.add("dve", lambda e, o=rs: e.reciprocal(o, o), ["rstd_sb"], ["rstd_sb"])
            P.strict = False
            for kc in range(KC):
                xr = xres[:, kc, 0:nco]
                B.tt("dve", xr, xr, mean_sb[:, 0:nco], ALU.subtract, [f"xres{kc}", "mean_sb"], [f"xres{kc}"])
                B.tt("dve", xr, xr, rstd_sb[:, 0:nco], ALU.mult, [f"xres{kc}", "rstd_sb"], [f"xres{kc}"])
                B.act(xr, xr, AF.Identity, [f"xres{kc}", "vec"], [f"xres{kc}"], bias=vcol(bi, kc), scale=vcol(gi, kc))
                B.cp("act", xb[:, kc, 0:nco], xr, [f"xres{kc}"], [f"xb{kc}"])
            P.strict = True

        def mlp(l):
            etmp = dn["etmp"]
            for q in range(4):
                def epi1(m, tt, pp, ptok, s, mi):
                    c0, nt = tt
                    k = next_et()
                    B.act(etmp[k][:, 0:nt], pp, AF.Relu, [ptok], [f"etmp{k}"])
                    B.tt("dve", hA[:, m, c0:c0 + nt], etmp[k][:, 0:nt], etmp[k][:, 0:nt], ALU.mult, [f"etmp{k}"], [f"hA{m}"])
                linear(w1[l][:, q * 2048:(q + 1) * 2048], 0, 16, 16, xb, "xb", epi1, wkey=f"w1_{l}_{q}")
                linear(w2[l], q * 2048, 16, 16, hA, "hA", epi_resid(q == 0), wkey=f"w2_{l}_{q}")

        def ple(l, psrc_list):
            etmp, pbt, wps = dn["etmp"], dn["pbt"], dn["wps"]
            for (dst_cols, src_ap) in psrc_list:
                B.dma("pool", pbt[:, :, dst_cols], src_ap, (), ["pbt"], "pbt")

            def xload(s, m0, n):
                B.dma("pool", wps[s][:, :, 0:n], wpp[l].rearrange("(kc p) n -> p kc n", p=128)[:, :, m0:m0 + n], (), [f"wps{s}"], f"wps{s}")

            def epi(m, tt, pp, ptok, s, mi):
                c0, nt = tt
                k = next_et()
                B.act(etmp[k][:, 0:nt], pp, AF.Sigmoid, [ptok], [f"etmp{k}"])
                p2 = next_ps()
                for kc in range(2):
                    B.mm(PS[p2][:, 0:nt], wps[s][:, kc, mi * 128:(mi + 1) * 128], pbt[:, kc, c0:c0 + nt], kc == 0, kc == 1, [f"wps{s}", "pbt"], [f"ps{p2}"])
                B.tt("dve", etmp[k][:, 0:nt], PS[p2][:, 0:nt], etmp[k][:, 0:nt], ALU.mult, [f"ps{p2}", f"etmp{k}"], [f"etmp{k}"])
                xr = xres[:, m, c0:c0 + nt]
                B.stt("dve", xr, xr, ALPHA, etmp[k][:, 0:nt], ALU.mult, ALU.add, [f"xres{m}", f"etmp{k}"], [f"xres{m}"])
            linear(wpg[l], 0, 16, 16, xb, "xb", epi, extra_load=xload, wkey=f"pg{l}")

        def epi_gate(m, tt, pp, ptok, s, mi):
            c0, nt = tt
            etmp = dn["etmp"]
            k = next_et()
            B.act(etmp[k][:, 0:nt], pp, AF.Sigmoid, [ptok], [f"etmp{k}"])
            B.tt("dve", xb[:, m, c0:c0 + nt], hA[:, m, c0:c0 + nt], etmp[k][:, 0:nt], ALU.mult, [f"hA{m}", f"etmp{k}"], [f"xb{m}"])

        def psrc(l, ti, with_s):
            lst = [(slice(0, 1024), pTp[l][ti * 128:(ti + 1) * 128, :].rearrange("p (k t) -> p k t", k=2))]
            if with_s:
                lst.append((slice(1024, 1152), pTs[l].rearrange("p (k t) -> p k t", k=2)))
            return lst

        def kv_proj(ti):
            etmp, wbuf = dn["etmp"], dn["wbuf"]

            def epi_k(m, tt, pp, ptok, s, mi):
                c0, nt = tt
                k = next_et()
                B.cp("dve", etmp[k][:, 0:nt], pp, [ptok], [f"etmp{k}"])
                if c0 < 1024:
                    B.dma("sp", o_kTp[ti * 128:(ti + 1) * 128, m * 1024 + c0:m * 1024 + c0 + nt], etmp[k][:, 0:nt], [f"etmp{k}"], ["o_kTp"], f"okt{k}")
                    B.dma("pool", KTa[m][ti * 128:(ti + 1) * 128, c0:c0 + nt], etmp[k][:, 0:nt], [f"etmp{k}"], ["KTa"], f"okb{k}")
                else:
                    B.dma("sp", o_kTs[:, m * 128:(m + 1) * 128], etmp[k][:, 0:nt], [f"etmp{k}"], ["o_kTs"], f"okt{k}")
                    B.dma("pool", KTs_d[:, m * 128:(m + 1) * 128], etmp[k][:, 0:nt], [f"etmp{k}"], ["KTs_d"], f"okb{k}")
            linear(w_k, 0, 16, 4, xb, "xb", epi_k, wkey="wk")
            for hf in range(2):
                B.dma("pool", wbuf[hf][:], wview(w_v, 0, 16, hf * 256, 256), (), [f"wbuf{hf}"], f"wbuf{hf}")
            nblk = cfg["ncols"] // 128
            P.strict = False
            for tb_ in range(nblk):
                pi_ = next_ps()
                for hf in range(2):
                    for kc in range(KC):
                        B.mm(PS[pi_][:, hf * 256:(hf + 1) * 256], xb[:, kc, tb_ * 128:(tb_ + 1) * 128], wbuf[hf][:, kc, :], kc == 0, kc == KC - 1,
                             [f"xb{kc}", f"wbuf{hf}"], [f"ps{pi_}"])
                k = next_et()
                B.cp("dve", etmp[k][:], PS[pi_][:, :], [f"ps{pi_}"], [f"etmp{k}"])
                if tb_ < 8:
                    B.dma("sp", o_vp[ti * 128:(ti + 1) * 128, tb_ * 512:(tb_ + 1) * 512], etmp[k][:], [f"etmp{k}"], ["o_vp"], f"okt{k}")
                    for kvh in range(4):
                        B.dma("pool", Va[kvh][ti * 128:(ti + 1) * 128, tb_ * 128:(tb_ + 1) * 128], etmp[k][:, kvh * 128:(kvh + 1) * 128],
                              [f"etmp{k}"], ["Va"], f"okb{k}")
                else:
                    B.dma("sp", o_vs, etmp[k][:], [f"etmp{k}"], ["o_vs"], f"okt{k}")
                    B.dma("pool", Vs_d, etmp[k][:], [f"etmp{k}"], ["Vs_d"], f"okb{k}")
            P.strict = True

        NT0 = 8
        for ti in range(NT0):
            if ti == 0:
                cfg["ncols"] = T
                cfg["tts"] = [(0, 512), (512, 512), (1024, 128)]
            else:
                cfg["ncols"] = 1024
                cfg["tts"] = [(0, 512), (512, 512)]
            cfg["wmode"] = "fill" if ti == 0 else "use"
            load_x(ti)
            ssm_phase(ti)
            with ExitStack() as esd:
                dense_scope(esd)
                linear(w_gate, 0, 16, 16, hA, "hA", epi_gate, wkey="wg")
                linear(w_out, 0, 16, 16, xb, "xb", epi_resid(True), wkey="wo")
                layer_norm(1, 2)
                mlp(0)
                layer_norm(3, 4)
                ple(0, psrc(0, ti, ti == 0))
                layer_norm(5, 6)
                kv_proj(ti)
                for hf in range(2):
                    B.dma("sp", X1[hf][ti * 128:(ti + 1) * 128, :], xres[:, 8 * hf:8 * hf + 8, :].rearrange("p a b -> p (a b)"),
                          [f"xres{k}" for k in range(8 * hf, 8 * hf + 8)], ["X1"], f"x1s{hf}")
                if ti == 0:
                    B.dma("sp", X1s.rearrange("p (k t) -> p k t", k=KC), xres[:, :, 1024:1152], [f"xres{k}" for k in range(KC)], ["X1s"], "x1ss")
                P.barrier(new_epoch=True)
        cfg["wmode"] = None
        es_ssm.close()

        cfg["ncols"] = T
        cfg["tts"] = [(0, 512), (512, 512), (1024, 128)]
        ridx_sb = sb("ridx_sb", [128, 8], I32)
        B.dma("sp", ridx_sb[:], ridx, (), ["ridx_sb"], "prm")
        for hf in range(2):
            P.add("pool", lambda e, hf=hf: e.indirect_dma_start(
                out=xres[:, 8 * hf:8 * hf + 8, :].rearrange("p a b -> p (a b)"), out_offset=None, in_=X1[hf],
                in_offset=bass.IndirectOffsetOnAxis(ap=ridx_sb[:, 0:1], axis=0)),
                ["ridx_sb", "X1"], [f"xres{k}" for k in range(8 * hf, 8 * hf + 8)], dma=f"x1g{hf}")
        B.dma("sp", xres[:, :, 1024:1152], X1s.rearrange("p (k t) -> p k t", k=KC), ["X1s"], [f"xres{k}" for k in range(KC)], "x1sl")
        for kc in range(KC):
            B.cp("act", xb[:, kc, :], xres[:, kc, :], [f"xres{kc}"], [f"xb{kc}"])

        with ExitStack() as esd:
            dense_scope(esd)

            def epi_q(m, tt, pp, ptok, s, mi):
                c0, nt = tt
                B.act(hA[:, m, c0:c0 + nt], pp, AF.Copy, [ptok], [f"hA{m}"], scale=128.0 ** -0.5)
            linear(w_q, 0, 16, 16, xb, "xb", epi_q)
            P.barrier()

        if STAGE == 2:
            with ExitStack() as esa:
                def sba(name, shape, dt=F32):
                    return esa.enter_context(sbt(name, list(shape), dt))
                biasT = sba("biasT", [128, 16, 8])
                sbb_sb = sba("sbb_sb", [128, 16])
                cm_sb = sba("cm_sb", [128, 8])
                zerosT = sba("zerosT", [128, 128], BF16)
                def mk_state(nm):
                    return {"nm": nm, "Rf": sba(f"Rf{nm}", [128, 512]), "Rb": [sba(f"Rb{nm}{i}", [128, 512], BF16) for i in range(2)], "r": 0}
                SW = [mk_state("a"), mk_state("b")]
                Ef = [sba(f"Ef{i}", [128, 512]) for i in range(2)]
                Gf = [sba(f"Gf{i}", [128, 512]) for i in range(2)]
                Lpb = [sba(f"Lpb{i}", [128, 512], BF16) for i in range(2)]
                wb = [sba(f"wb{i}", [128, 512], BF16) for i in range(3)]
                B.dma("sp", sbb_sb[:], sbb, (), ["sbb_sb"], "prm")
                B.dma("sp", cm_sb[:], cmask, (), ["cm_sb"], "prm")
                B.memset("dve", zerosT[:], 0.0, ["zerosT"])
                for j in range(8):
                    B.ts("dve", biasT[:, :, j], sbb_sb[:], cm_sb[:, j:j + 1], None, ALU.add, None, ["sbb_sb", "cm_sb"], ["biasT"])
                bc = {"n": 0}

                pend = {"p": None}

                fcnt = {"n": 0}

                def pe_fence():
                    fcnt["n"] += 1
                    tkn = f"fence{fcnt['n']}"
                    B.mm(PS[7][0:8, 0:8], ident[0:8, 0:8], ident[0:8, 0:8], True, True, ["ident"], [tkn])
                    return tkn

                def sb_block(S, nk, N, qk_list, bias_ap, mask_fn, first, roff, wv_list):
                    i = bc["n"] % 2
                    wi = bc["n"] % 3
                    bc["n"] += 1
                    pa = i
                    A = PS[pa]
                    P.strict = (N < 512)
                    nq = len(qk_list)
                    started = False
                    if mask_fn is not None and mask_fn[0] == "pre":
                        for (o_, l_, r_, rt) in mask_fn[1]:
                            B.mm(o_(A), l_, r_, not started, False, rt, [f"ps{pa}"])
                            started = True
                    for qi, (oc, l_, r_, rt) in enumerate(qk_list):
                        last = (qi == nq - 1) and not (mask_fn is not None and mask_fn[0] == "post")
                        B.mm(A[0:nk, oc], l_, r_, not started, last, rt, [f"ps{pa}"])
                        if len(qk_list) == 1:
                            started = True
                    if mask_fn is not None and mask_fn[0] == "post":
                        for (o_, l_, r_, rt) in mask_fn[1]:
                            B.mm(o_(A), l_, r_, False, True, rt, [f"ps{pa}"])
                    E = Ef[i][0:nk, 0:N]; L = Lpb[i][0:nk, 0:N]
                    fz = []
                    if S.get("fence"):
                        fz = [pe_fence()]
                    if bias_ap is not None:
                        B.act(E, A[0:nk, 0:N], AF.Exp, [f"ps{pa}", "biasT"] + fz, [f"Ef{i}"], bias=bias_ap)
                    else:
                        B.act(E, A[0:nk, 0:N], AF.Exp, [f"ps{pa}"] + fz, [f"Ef{i}"])
                    B.act(L, E, AF.Ln, [f"Ef{i}"], [f"Lpb{i}"], bias=1.0)
                    P.strict = True
                    cur = (S, i, nk, N, first, roff, wv_list, wi)
                    if not S.get("pipe", True):
                        stage2(*cur)
                        stage3(*cur)
                        return
                    q = pend.setdefault("q", [])
                    q.append(cur)
                    if len(q) >= 2:
                        stage2(*q[-2])
                    if len(q) >= 4:
                        stage3(*q[0])
                        q.pop(0)

                def flush_blocks():
                    q = pend.setdefault("q", [])
                    if q:
                        stage2(*q[-1])
                    for c_ in q:
                        stage3(*c_)
                    q.clear()

                def stage2(S, i, nk, N, first, roff, wv_list, wi):
                    pb_ = 2 + i
                    Bk = PS[pb_]
                    P.strict = (N < 512)
                    E = Ef[i][0:nk, 0:N]; L = Lpb[i][0:nk, 0:N]; G = Gf[i][0:nk, 0:N]; W = wb[wi][0:nk, 0:N]
                    Rf, Rb, nm = S["Rf"], S["Rb"], S["nm"]
                    rcur = S["r"]
                    B.mm(Bk[0:nk, 0:N], trineg[0:nk, 0:nk], L, True, first, ["trineg", f"Lpb{i}"], [f"ps{pb_}"])
                    if not first:
                        B.mm(Bk[0:nk, 0:N], onesneg[:, 0:nk], Rb[rcur][:, roff:roff + N], False, True, ["onesneg", f"Rb{nm}{rcur}"], [f"ps{pb_}"])
                    rn = 1 - rcur
                    reng_ = S.get("r_eng", "dve")
                    B.tt(reng_, Rf[0:nk, roff:roff + N], Rf[0:nk, roff:roff + N], L, ALU.add, [f"Rf{nm}", f"Lpb{i}"], [f"Rf{nm}"])
                    B.cp(reng_, Rb[rn][:, :], Rf[:, :], [f"Rf{nm}"], [f"Rb{nm}{rn}"])
                    S["r"] = rn
                    fz = []
                    if S.get("fence"):
                        fz = [pe_fence()]
                    B.act(G, Bk[0:nk, 0:N], AF.Exp, [f"ps{pb_}"] + fz, [f"Gf{i}"])
                    B.tt(S.get("w_eng", "pool"), W, E, G, ALU.mult, [f"Ef{i}", f"Gf{i}"], [f"wb{wi}"])
                    P.strict = True

                def stage3(S, i, nk, N, first, roff, wv_list, wi):
                    P.strict = (N < 512)
                    for (po, ptok, l_, wc, st0, sp0, rt) in wv_list:
                        B.mm(po, l_, wb[wi][0:nk, wc], st0, sp0, rt + [f"wb{wi}"], [ptok])
                    P.strict = True

                with ExitStack() as esp:
                    def sbp(name, shape, dt=F32):
                        return esp.enter_context(sbt(name, list(shape), dt))
                    kbuf = [sbp(f"kbuf{i}", [128, 1024], BF16) for i in range(3)]
                    vbuf = [sbp(f"vbuf{i}", [128, 1024], BF16) for i in range(3)]

                    def gather(hj):
                        h_, j_ = hj // 8, hj % 8
                        g_ = hj % 3
                        kvh_ = h_ // 4
                        P.add("pool", lambda e: e.indirect_dma_start(
                            out=kbuf[g_][:], out_offset=None, in_=KTa[kvh_],
                            in_offset=bass.IndirectOffsetOnAxis(ap=ridx_sb[:, j_:j_ + 1], axis=0)),
                            ["ridx_sb", "KTa"], [f"kbuf{g_}"], dma=f"kg{g_}")
                        P.add("pool", lambda e: e.indirect_dma_start(
                            out=vbuf[g_][:], out_offset=None, in_=Va[kvh_],
                            in_offset=bass.IndirectOffsetOnAxis(ap=ridx_sb[:, j_:j_ + 1], axis=0)),
                            ["ridx_sb", "Va"], [f"vbuf{g_}"], dma=f"vg{g_}")
                    gather(0)
                    gcount = {"n": 0}
                    for h in range(16):
                        kvh = h // 4
                        POs = [PS[4], PS[5]]
                        for qt in range(2):
                            B.memset("pool", SW[qt]["Rf"][:], 0.0, [f"Rf{SW[qt]['nm']}"])
                            B.mm(POs[qt][:, 0:512], zerosT[:], hA[:, h, qt * 512:(qt + 1) * 512], True, False, ["zerosT", f"hA{h}"], [f"ps{4 + qt}"])
                        firsts = [True, True]
                        for j in range(8):
                            hj = h * 8 + j
                            g = hj % 3
                            if hj + 1 < 128:
                                gather(hj + 1)
                            for kb in range(7, -1, -1):
                                for qt in ((1, 0) if j == 0 else (0, 1)):
                                    if j == 0 and kb > 4 * qt + 3:
                                        continue
                                    diag = (j == 0 and kb >= 4 * qt)
                                    off = (kb - 4 * qt) * 128 if diag else 0
                                    N = 512 - off
                                    q0 = qt * 512 + off
                                    qk = [(slice(0, N), kbuf[g][:, kb * 128:(kb + 1) * 128], hA[:, h, q0:q0 + N], [f"kbuf{g}", f"hA{h}"])]
                                    mk = None
                                    if diag:
                                        mk = ("post", [(lambda A_: A_[:, 0:128], ident[:], trimask[:], ["ident", "trimask"])])
                                    last_blk = (j == 7 and kb == 0)
                                    wv = [(POs[qt][:, off:512], f"ps{4 + qt}", vbuf[g][:, kb * 128:(kb + 1) * 128], slice(0, N), False, last_blk, [f"vbuf{g}"])]
                                    sb_block(SW[qt], 128, N, qk, biasT[:, h, j:j + 1], mk, firsts[qt], off, wv)
                                    firsts[qt] = False
                        flush_blocks()
                        for qt in range(2):
                            B.cp("act", hA[:, h, qt * 512:(qt + 1) * 512], POs[qt][:, 0:512], [f"ps{4 + qt}"], [f"hA{h}"])
                P.barrier()

                with ExitStack() as ess:
                    def sbq(name, shape, dt=F32):
                        return ess.enter_context(sbt(name, list(shape), dt))
                    xbf = xb[:].rearrange("p a b -> p (a b)")
                    Kraw = xbf[:, 0:8192].rearrange("p (a b) -> p a b", b=512)
                    Vraw = sbq("Vraw", [128, 16, 512], BF16)
                    KTt = xbf[:, 8192:16384].rearrange("p (a b) -> p a b", b=2048)
                    qS = sbq("qS", [128, 16, 128], BF16)
                    kTs = sbq("kTs", [128, 512], BF16)
                    vnew = sbq("vnew", [8, 512], BF16)
                    pt_sb = sbq("pt_sb", [128, 256], I32)
                    idx_all = sbq("idx_all", [128, 256], I32)
                    iota_c = sbq("iota_c", [128, 1])
                    br_f = sbq("br_f", [2, 128]); br_h = sbq("br_h", [2, 128], BF16); br_l = sbq("br_l", [2, 128])
                    br2 = sbq("br2", [2, 128], BF16); m01_sb = sbq("m01_sb", [2, 2]); ones2 = sbq("ones2", [2, 128], BF16)
                    B.dma("sp", pt_sb[:], ptab, (), ["pt_sb"], "prm")
                    B.dma("sp", iota_c[:], c_iota, (), ["iota_c"], "prm")
                    B.dma("sp", br_f[:], brow, (), ["br"], "prm")
                    B.dma("sp", m01_sb[:], m01, (), ["br"], "prm")
                    B.dma("sp", kTs[:], KTs_d, ["KTs_d"], ["kTs"], "prm")
                    B.ts("dve", idx_all[:], pt_sb[:], 128.0, iota_c[:, 0:1], ALU.mult, ALU.add, ["pt_sb", "iota_c"], ["idx_all"])
                    B.cp("dve", br_h[:], br_f[:], ["br"], ["br"])
                    B.tt("dve", br_l[:], br_f[:], br_h[:], ALU.subtract, ["br"], ["br"])
                    B.cp("dve", br_f[:], br_h[:], ["br"], ["br"])
                    B.ts("dve", br_f[:], br_f[:], m01_sb[:, 0:1], None, ALU.mult, None, ["br"], ["br"])
                    B.stt("dve", br2[:], br_l[:], m01_sb[:, 1:2], br_f[:], ALU.mult, ALU.add, ["br"], ["br2"])
                    B.memset("dve", ones2[:], 1.0, ["ones2"])
                    B.cp("dve", qS[:].rearrange("p b (h t) -> p b h t", t=8),
                         hA[:, :, 1024:1152].rearrange("p h (b t) -> p b h t", t=8), [f"hA{h}" for h in range(16)], ["qS"])
                    PTb = PS[6][:].bitcast(BF16)
                    for b in range(16):
                        for pg in range(16):
                            col = b * 16 + pg
                            P.add("pool", lambda e, pg=pg, col=col: e.indirect_dma_start(
                                out=Kraw[:, pg, :], out_offset=None, in_=cache_k,
                                in_offset=bass.IndirectOffsetOnAxis(ap=idx_all[:, col:col + 1], axis=0)),
                                ["idx_all"], [f"Kraw{pg}"], dma=f"ckg{pg % 4}")
                            P.add("pool", lambda e, pg=pg, col=col: e.indirect_dma_start(
                                out=Vraw[:, pg, :], out_offset=None, in_=cache_v,
                                in_offset=bass.IndirectOffsetOnAxis(ap=idx_all[:, col:col + 1], axis=0)),
                                ["idx_all"], [f"Vraw{pg}"], dma=f"cvg{pg % 4}")
                        B.dma("sp", vnew[:], Vs_d[b * 8:(b + 1) * 8, :], ["Vs_d"], ["vnew"], "vnew")
                        for pg in range(16):
                            for kvh in range(4):
                                B.tr(PTb[:, kvh * 128:(kvh + 1) * 128], Kraw[:, pg, kvh * 128:(kvh + 1) * 128], ident[:], [f"Kraw{p_}" for p_ in range(16)] + ["ident"], ["ps6"])
                            ftk = pe_fence()
                            B.cp("dve", KTt[:, :, pg * 128:(pg + 1) * 128], PTb[:, 0:512].rearrange("p (k s) -> p k s", k=4), ["ps6", ftk], ["KTt"])
                        PO = PS[4 + b % 2]
                        potok = f"ps{4 + b % 2}"
                        SS = SW[b % 2]
                        SS["fence"] = False
                        B.memset("pool", SS["Rf"][:], 0.0, [f"Rf{SS['nm']}"])
                        first = True
                        for blk in range(16, -1, -1):
                            nk = 8 if blk == 16 else 128
                            pre = [(lambda A_, nk=nk: A_[0:nk, 0:128], ones2[0:2, 0:nk], br2[0:2, :], ["ones2", "br2"])]
                            if blk == 16:
                                pre.append((lambda A_: A_[0:8, 0:128], ident[0:8, 0:8], smask[0:8, :], ["ident", "smask"]))
                            qk = []
                            wv = []
                            for kvh in range(4):
                                if blk == 16:
                                    kt_ap = kTs[:, kvh * 128 + b * 8:kvh * 128 + b * 8 + 8]
                                    v_ap = vnew[0:8, kvh * 128:(kvh + 1) * 128]
                                    rt = ["kTs"]; vt = ["vnew"]
                                else:
                                    kt_ap = KTt[:, kvh, blk * 128:(blk + 1) * 128]
                                    v_ap = Vraw[:, blk, kvh * 128:(kvh + 1) * 128]
                                    rt = ["KTt"]; vt = [f"Vraw{p_}" for p_ in range(16)]
                                qk.append((slice(kvh * 32, kvh * 32 + 32), kt_ap, qS[:, b, kvh * 32:(kvh + 1) * 32], rt + ["qS"]))
                                wv.append((PO[:, kvh * 32:(kvh + 1) * 32], potok, v_ap, slice(kvh * 32, kvh * 32 + 32), blk == 16, blk == 0, vt))
                            sb_block(SS, nk, 128, qk, None, ("pre", pre), first, 0, wv)
                            first = False
                        flush_blocks()
                        ftk = pe_fence()
                        B.cp("act", hA[:, :, 1024 + b * 8:1024 + b * 8 + 8], PO[:, 0:128].rearrange("p (h t) -> p h t", t=8), [potok, ftk], [f"hA{h}" for h in range(16)])
                P.barrier()

        with ExitStack() as esd:
            dense_scope(esd)
            linear(w_o, 0, 16, 16, hA, "hA", epi_resid(True))
            layer_norm(7, 8)
            mlp(1)
            layer_norm(9, 10)
            pown = esd.enter_context(sbt("pown", [128, 2048], F32))
            P.add("pool", lambda e: e.indirect_dma_start(out=pown[:], out_offset=None, in_=pTp[1],
                                                         in_offset=bass.IndirectOffsetOnAxis(ap=ridx_sb[:, 0:1], axis=0)),
                  ["ridx_sb"], ["pown"], dma="pown")
            B.cp("dve", dn["pbt"][:, :, 0:1024], pown[:].rearrange("p (k t) -> p k t", k=2), ["pown"], ["pbt"])
            ple(1, [(slice(1024, 1152), pTs[1].rearrange("p (k t) -> p k t", k=2))])
            layer_norm(11, 12)
            for kc in range(KC):
                B.dma("sp", o_y[:, kc * T:(kc + 1) * T], xres[:, kc, :], [f"xres{kc}"], ["o_y"], "oy")
        P.add("sp", None, ["o_y", "o_sre", "o_sim", "o_ssre", "o_ssim", "o_kTp", "o_vp", "o_kTs", "o_vs"], ["done"])
        P.emit()
    return nc


_PROG = None


def _get_prog():
    global _PROG
    if _PROG is None:
        _PROG = build_program()
    return _PROG


def _fm(a, nchunk):
    t = a.shape[0]
    return np.ascontiguousarray(a.reshape(t, nchunk, 128).transpose(2, 1, 0).reshape(128, nchunk * t))


def _unfm(a, nchunk, t):
    return np.ascontiguousarray(a.reshape(128, nchunk, t).transpose(2, 1, 0).reshape(t, nchunk * 128))


def kernel(**inp):
    f32 = np.float32
    g = lambda k: np.asarray(inp[k])
    x_prompt = g("x_prompt").astype(f32, copy=False)[0]
    x_sample = g("x_sample").astype(f32, copy=False).reshape(1024, 2048)
    p_prompt = g("p_prompt").astype(f32, copy=False)[:, 0]
    p_sample = g("p_sample").astype(f32, copy=False).reshape(2, 1024, 256)
    sre = g("state_ssm_re")[0]
    sim = g("state_ssm_im")[0]
    ptab_full = g("page_table").astype(np.int32)

    def gp_to_pair(a):
        return np.ascontiguousarray(a.reshape(64, 2, 64).transpose(1, 2, 0).reshape(128, 64)).astype(f32)

    lam_re = g("s5_lam_re")[0]; lam_im = g("s5_lam_im")[0]
    log_dt = g("s5_log_dt")[0]
    ldt_gp = np.repeat(log_dt[:, None], 64, axis=1)
    lr2 = gp_to_pair(lam_re); li2 = gp_to_pair(lam_im); dt2 = gp_to_pair(ldt_gp)

    def feat_layout(a_gp):
        a = a_gp.reshape(16, 4, 2, 64)
        a = a.transpose(1, 2, 0, 3)
        a = np.broadcast_to(a[:, :, None, :, None, :], (4, 2, 16, 16, 2, 64))
        return np.ascontiguousarray(a.reshape(128, 2048)).astype(f32)

    sLR = feat_layout(lam_re); sLI = feat_layout(lam_im); sDT = feat_layout(ldt_gp)

    def bpad(b):
        a = b.reshape(16, 4, 2, 64, 16).transpose(1, 2, 4, 0, 3)
        out = np.zeros((4, 2, 16, 16, 2, 64), f32)
        for g2 in range(2):
            out[:, g2, :, :, g2, :] = a[:, g2]
        return out.reshape(128, 2048)

    sBre = bpad(g("s5_b_re")[0]); sBim = bpad(g("s5_b_im")[0])

    def cpad_fn(cre, cim):
        out = np.zeros((16, 2, 64, 4, 2, 4, 2, 16), f32)
        for ri, c in enumerate((cre, cim)):
            a = c.reshape(16, 4, 2, 16, 64)
            for j in range(4):
                for g2 in range(2):
                    out[:, g2, :, j, ri, j, g2, :] = a[:, j, g2].transpose(0, 2, 1)
        return out.reshape(16, 128, 1024)

    cpad = cpad_fn(g("s5_c_re")[0], g("s5_c_im")[0])

    def pk(v):
        return np.ascontiguousarray(v.reshape(16, 128).T)

    vlist = [g("s5_d")[0]]
    for nm in ("ln_mix", "ln_mlp", "ln_ple"):
        vlist += [g(nm + "_g")[0], g(nm + "_b")[0]]
    for nm in ("ln_mix", "ln_mlp", "ln_ple"):
        vlist += [g(nm + "_g")[1], g(nm + "_b")[1]]
    vecs = np.concatenate([pk(v) for v in vlist], axis=1).astype(f32)

    ii = np.arange(128)
    c_ident = np.eye(128, dtype=f32)
    c_trineg = -(ii[:, None] >= ii[None, :]).astype(f32)
    c_onesneg = -np.ones((128, 128), f32)
    c_onesmean = np.full((128, 128), 1.0 / 2048, f32)
    c_trimask = np.where(ii[:, None] >= ii[None, :], NEG, 0.0).astype(f32)
    c_smask = np.where(np.arange(8)[:, None] >= (np.arange(128) % 8)[None, :], NEG, 0.0).astype(f32)
    c_dsamp = np.broadcast_to((np.arange(128) % 8 != 0).astype(f32)[None, :], (128, 128)).copy()
    sbb = np.broadcast_to(g("sb_bias")[0][None, :], (128, 16)).astype(f32).copy()
    brow = np.broadcast_to(np.repeat(g("sb_bias")[0], 8)[None, :], (2, 128)).astype(f32).copy()
    m01 = np.eye(2, dtype=f32)

    shared = dict(sLR=sLR, sLI=sLI, sDT=sDT, sBre=sBre, sBim=sBim, lr2=lr2, li2=li2, dt2=dt2, cpad=cpad, vecs=vecs,
                  sbb=sbb, brow=brow, m01=m01, c_ident=c_ident, c_trineg=c_trineg, c_onesneg=c_onesneg,
                  c_onesmean=c_onesmean, c_trimask=c_trimask, c_smask=c_smask, c_dsamp=c_dsamp, c_m96=(np.arange(128) >= 96).astype(np.float32).reshape(128, 1),
                  s5_w_gate=g("s5_w_gate")[0], s5_w_out=g("s5_w_out")[0], w_k=g("w_k"), w_v=g("w_v"),
                  sb_w_q=g("sb_w_q")[0], sb_w_o=g("sb_w_o")[0],
                  mlp_w1_0=g("mlp_w1")[0], mlp_w1_1=g("mlp_w1")[1], mlp_w2_0=g("mlp_w2")[0], mlp_w2_1=g("mlp_w2")[1],
                  ple_w_proj_0=g("ple_w_proj")[0], ple_w_proj_1=g("ple_w_proj")[1],
                  ple_w_gate_0=g("ple_w_gate")[0], ple_w_gate_1=g("ple_w_gate")[1],
                  )
    if STAGE == 2:
        shared["cache_k"] = g("cache_k").reshape(2560 * 128, 512)
        shared["cache_v"] = g("cache_v").reshape(2560 * 128, 512)
    shared = {k: np.ascontiguousarray(v, dtype=f32) for k, v in shared.items()}

    shared["xTp"] = np.concatenate([_fm(x_prompt[i * TP:(i + 1) * TP], 16) for i in range(8)], axis=0)
    for l in range(2):
        shared[f"pTp{l}"] = np.concatenate([_fm(p_prompt[l, i * TP:(i + 1) * TP], 2) for i in range(8)], axis=0)
    shared["c_iota"] = np.arange(128, dtype=f32).reshape(128, 1)
    in_maps = []
    for c in range(NC):
        m = dict(shared)
        m["xTs"] = _fm(x_sample[c * TS:(c + 1) * TS], 16)
        for l in range(2):
            m[f"pTs{l}"] = _fm(p_sample[l, c * TS:(c + 1) * TS], 2)
        for nm, s in (("h0re", sre), ("h0im", sim)):
            a = s[c * 16:(c + 1) * 16].reshape(16, 64, 2, 64).transpose(2, 3, 1, 0)
            m[nm] = np.ascontiguousarray(a.reshape(128, 1024)).astype(f32)
        m["ptab"] = np.ascontiguousarray(np.broadcast_to(ptab_full[c * 16:(c + 1) * 16].reshape(1, 256), (128, 256))).astype(np.int32)
        r = np.array([((c - j) % NC) for j in range(8)], np.int32)
        m["ridx"] = (r[None, :] * 128 + np.arange(128, dtype=np.int32)[:, None]).astype(np.int32)
        cm = np.array([0.0 if (c - j) >= 0 else NEG for j in range(8)], f32)
        m["cmask"] = np.broadcast_to(cm[None, :], (128, 8)).copy()
        in_maps.append(m)

    nc = _get_prog()
    res = run_bass_kernel_spmd(nc, in_maps, core_ids=list(range(NC)))
    R = res.results

    y_all = [_unfm(np.asarray(R[c]["o_y"]), 16, T) for c in range(NC)]
    y_prompt = np.concatenate([y[:TP] for y in y_all], axis=0)[None]
    y_sample = np.concatenate([y[TP:] for y in y_all], axis=0).reshape(128, 8, 2048)

    def st_prompt(a):
        return np.ascontiguousarray(np.asarray(a).reshape(2, 64, 64).transpose(2, 0, 1).reshape(1, 1, 128, 64))
    sre_p = st_prompt(R[0]["o_sre"]); sim_p = st_prompt(R[0]["o_sim"])

    def st_sample(nm):
        outs = []
        for c in range(NC):
            a = np.asarray(R[c][nm]).reshape(2, 64, 64, 16).transpose(3, 2, 0, 1)
            outs.append(a.reshape(16, 128, 64))
        return np.ascontiguousarray(np.concatenate(outs, axis=0)[None])
    sre_s = st_sample("o_ssre"); sim_s = st_sample("o_ssim")

    kp = np.asarray(R[0]["o_kTp"]).reshape(8, 128, 4, 1024).transpose(0, 3, 2, 1)
    k_p = np.ascontiguousarray(kp.reshape(1, 8192, 4, 128))
    vp = np.asarray(R[0]["o_vp"]).reshape(8, 128, 8, 512).transpose(0, 2, 1, 3)
    v_p = np.ascontiguousarray(vp.reshape(1, 8192, 4, 128))
    k_s = np.concatenate([np.asarray(R[c]["o_kTs"]).reshape(128, 4, 16, 8).transpose(2, 3, 1, 0) for c in range(NC)], axis=0)
    v_s = np.concatenate([np.asarray(R[c]["o_vs"]).reshape(16, 8, 4, 128) for c in range(NC)], axis=0)
    outs = (y_prompt, y_sample, sre_p, sim_p, k_p, v_p, sre_s, sim_s, np.ascontiguousarray(k_s), np.ascontiguousarray(v_s))
    return tuple(np.ascontiguousarray(o, dtype=f32) for o in outs)
```

```python
import math
from contextlib import ExitStack
import numpy as np
import ml_dtypes
import concourse.bass as bass
import concourse.mybir as mybir
from concourse.bass_utils import run_bass_kernel_spmd

F32 = mybir.dt.float32
BF16 = mybir.dt.bfloat16
I32 = mybir.dt.int32
AF = mybir.ActivationFunctionType
ALU = mybir.AluOpType

NC = 8
TP = 1024
TS = 128
T = TP + TS
TT = 384
NTT = 3
D = 2048
KC = 16
DFF = 8192
ALPHA = 4.0 ** 0.25
EPS = 1e-5
NEG = -30000.0
STAGE = 2
SAME_ENG_INORDER = ("pe", "dve", "act")


class _Op:
    __slots__ = ("eng", "fn", "deps", "dma", "inc", "val", "waits", "ep", "strict")


class Prog:
    ENGS = ("pe", "act", "dve", "pool", "sp")

    def __init__(self, nc):
        self.nc = nc
        self.ops = []
        self.last_w = {}
        self.readers = {}
        self.epoch = 0
        self.strict = True

    def add(self, eng, fn, reads=(), writes=(), dma=None):
        i = len(self.ops)
        deps = set()
        for t in reads:
            w = self.last_w.get(t)
            if w is not None:
                deps.add(w)
        for t in writes:
            w = self.last_w.get(t)
            if w is not None:
                deps.add(w)
            r = self.readers.get(t)
            if r:
                deps.update(r)
        op = _Op()
        op.eng, op.fn, op.deps, op.dma = eng, fn, deps, dma
        op.inc, op.val, op.waits = False, 0, None
        op.ep = self.epoch
        op.strict = self.strict
        for t in writes:
            self.last_w[t] = i
            self.readers[t] = []
        for t in reads:
            self.readers.setdefault(t, []).append(i)
        self.ops.append(op)
        return i

    def barrier(self, new_epoch=False):
        toks = list(self.last_w.keys() | self.readers.keys())
        idxs = []
        for e in self.ENGS:
            idxs.append(self.add(e, None, reads=(), writes=toks))
        if new_epoch:
            self.epoch += 1
        return idxs

    def finalize(self):
        ops = self.ops
        for op in ops:
            for d in op.deps:
                dep = ops[d]
                if dep.dma is None:
                    if dep.eng == op.eng and op.dma is None and (dep.eng == "pe" or (dep.eng in SAME_ENG_INORDER and not (op.strict or dep.strict))):
                        continue
                    dep.inc = True
        cnt = {}
        scnt = {}
        for op in ops:
            if op.dma is None:
                if op.inc:
                    k_ = (op.eng, op.ep)
                    cnt[k_] = cnt.get(k_, 0) + 1
                    op.val = cnt[k_]
            else:
                scnt[op.dma] = scnt.get(op.dma, 0) + 16
                op.val = scnt[op.dma]
        seen = {e: {} for e in self.ENGS}
        for op in ops:
            need = {}
            for d in op.deps:
                dep = ops[d]
                if dep.dma is None:
                    if dep.eng == op.eng and op.dma is None and (dep.eng == "pe" or (dep.eng in SAME_ENG_INORDER and not (op.strict or dep.strict))):
                        continue
                    key = ("e", dep.eng, dep.ep)
                else:
                    key = ("d", dep.dma)
                if dep.val > need.get(key, 0):
                    need[key] = dep.val
            s = seen[op.eng]
            w = []
            for key, v in need.items():
                if s.get(key, 0) < v:
                    s[key] = v
                    w.append((key, v))
            op.waits = w
        self.streams = sorted(scnt.keys(), key=str)
        self.nep = self.epoch + 1
        print("ops", len(ops), "epochs", self.nep, "streams", len(self.streams), "max eng cnt", max(cnt.values()), "max dma cnt", max(scnt.values()))

    def emit(self):
        nc = self.nc
        self.finalize()
        NSET = self.nep
        pool_ = [{e: nc.alloc_semaphore(name=f"se_{e}_{k}") for e in self.ENGS} for k in range(NSET)]
        sems = {}
        for ep in range(self.nep):
            for e in self.ENGS:
                sems[("e", e, ep)] = pool_[ep % NSET][e]
        for k, s in enumerate(self.streams):
            sems[("d", s)] = nc.alloc_semaphore(name=f"sd_{k}")
        ops = self.ops

        def run(engname, e):
            for op in ops:
                if op.eng != engname:
                    continue
                for key, v in op.waits:
                    e.wait_ge(sems[key], v)
                if op.fn is None:
                    if op.inc:
                        e.nop().then_inc(sems[("e", engname, op.ep)], 1)
                    continue
                ins = op.fn(e)
                if op.dma is not None:
                    ins.then_inc(sems[("d", op.dma)], 16)
                elif op.inc:
                    ins.then_inc(sems[("e", engname, op.ep)], 1)

        with nc.Block() as block:
            @block.tensor
            def _(e):
                run("pe", e)

            @block.scalar
            def _(e):
                run("act", e)

            @block.vector
            def _(e):
                run("dve", e)

            @block.gpsimd
            def _(e):
                run("pool", e)

            @block.sync
            def _(e):
                run("sp", e)


class Builder:
    def __init__(self, nc):
        self.nc = nc
        self.P = Prog(nc)
        self.dram = {}

    def mm(self, out, lhsT, rhs, start, stop, r, w):
        self.P.add("pe", lambda e: e.matmul(out, lhsT, rhs, start=start, stop=stop), r, w)

    def tr(self, out, in_, ident, r, w):
        self.P.add("pe", lambda e: e.transpose(out, in_, ident), r, w)

    def act(self, out, in_, func, r, w, bias=None, scale=None):
        kw = {}
        if bias is not None:
            kw["bias"] = bias
        if scale is not None:
            kw["scale"] = scale
        self.P.add("act", lambda e: e.activation(out, in_, func, **kw), r, w)

    def tt(self, eng, out, in0, in1, op, r, w):
        self.P.add(eng, lambda e: e.tensor_tensor(out, in0, in1, op), r, w)

    def ts(self, eng, out, in0, s1, s2, op0, op1, r, w):
        if s2 is None:
            self.P.add(eng, lambda e: e.tensor_scalar(out, in0, s1, None, op0), r, w)
        else:
            self.P.add(eng, lambda e: e.tensor_scalar(out, in0, s1, s2, op0, op1), r, w)

    def stt(self, eng, out, in0, scalar, in1, op0, op1, r, w):
        self.P.add(eng, lambda e: e.scalar_tensor_tensor(out, in0, scalar, in1, op0, op1), r, w)

    def cp(self, eng, out, in_, r, w):
        if eng == "act":
            self.P.add("act", lambda e: e.activation(out, in_, AF.Copy), r, w)
        else:
            self.P.add(eng, lambda e: e.tensor_copy(out, in_), r, w)

    def memset(self, eng, ap, val, w):
        self.P.add(eng, lambda e: e.memset(ap, val), (), w)

    def dma(self, q, out, in_, r, w, stream):
        w = list(w)
        if stream == "prm":
            w.append("_prmchain")
        self.P.add(q, lambda e: e.dma_start(out=out, in_=in_), r, w, dma=stream)


def _sig(eng_b, out, in_, r, w, scale=1.0):
    eng_b.act(out, in_, AF.Sigmoid, r, w, scale=scale)


def build_program():
    nc = bass.Bass("TRN2", target_bir_lowering=False)
    B = Builder(nc)
    P = B.P

    def din(name, shape, dt=F32):
        return nc.dram_tensor(name, list(shape), dt, kind="ExternalInput").ap()

    def dout(name, shape, dt=F32):
        return nc.dram_tensor(name, list(shape), dt, kind="ExternalOutput").ap()

    def dint(name, shape, dt=F32):
        return nc.dram_tensor(name, list(shape), dt, kind="Internal").ap()

    xTp = din("xTp", [8 * 128, KC * 1024])
    xTs = din("xTs", [128, KC * 128])
    pTp = [din(f"pTp{l}", [8 * 128, 2 * 1024]) for l in range(2)]
    pTs = [din(f"pTs{l}", [128, 2 * 128]) for l in range(2)]
    h0re = din("h0re", [128, 64 * 16])
    h0im = din("h0im", [128, 64 * 16])
    sLR = din("sLR", [128, 2048])
    sLI = din("sLI", [128, 2048])
    sDT = din("sDT", [128, 2048])
    sBre = din("sBre", [128, 2048])
    sBim = din("sBim", [128, 2048])
    lr2 = din("lr2", [128, 64])
    li2 = din("li2", [128, 64])
    dt2 = din("dt2", [128, 64])
    cpad = din("cpad", [16, 128, 1024])
    vecs = din("vecs", [128, 13 * 16])
    sbb = din("sbb", [128, 16])
    brow = din("brow", [2, 128])
    m01 = din("m01", [2, 2])
    ptab = din("ptab", [128, 256], I32)
    ridx = din("ridx", [128, 8], I32)
    cmask = din("cmask", [128, 8])
    c_ident = din("c_ident", [128, 128])
    c_trineg = din("c_trineg", [128, 128])
    c_onesneg = din("c_onesneg", [128, 128])
    c_onesmean = din("c_onesmean", [128, 128])
    c_trimask = din("c_trimask", [128, 128])
    c_smask = din("c_smask", [8, 128])
    c_dsamp = din("c_dsamp", [128, 128])
    c_m96 = din("c_m96", [128, 1])
    c_iota = din("c_iota", [128, 1])
    w_gate = din("s5_w_gate", [D, D])
    w_out = din("s5_w_out", [D, D])
    w_k = din("w_k", [D, 512])
    w_v = din("w_v", [D, 512])
    w_q = din("sb_w_q", [D, D])
    w_o = din("sb_w_o", [D, D])
    w1 = [din(f"mlp_w1_{l}", [D, DFF]) for l in range(2)]
    w2 = [din(f"mlp_w2_{l}", [DFF, D]) for l in range(2)]
    wpp = [din(f"ple_w_proj_{l}", [256, D]) for l in range(2)]
    wpg = [din(f"ple_w_gate_{l}", [D, D]) for l in range(2)]
    if STAGE == 2:
        cache_k = din("cache_k", [2560 * 128, 512])
        cache_v = din("cache_v", [2560 * 128, 512])

    o_y = dout("o_y", [128, KC * T])
    o_sre = dout("o_sre", [128, 64])
    o_sim = dout("o_sim", [128, 64])
    o_ssre = dout("o_ssre", [128, 64 * 16])
    o_ssim = dout("o_ssim", [128, 64 * 16])
    o_kTp = dout("o_kTp", [8 * 128, 4096])
    o_vp = dout("o_vp", [8 * 128, 4096])
    o_kTs = dout("o_kTs", [128, 512])
    o_vs = dout("o_vs", [128, 512])

    KTa = [dint(f"KTa{k}", [8 * 128, 1024], BF16) for k in range(4)]
    Va = [dint(f"Va{k}", [8 * 128, 1024], BF16) for k in range(4)]
    KTs_d = dint("KTs_d", [128, 512], BF16)
    Vs_d = dint("Vs_d", [128, 512], BF16)
    X1 = [dint(f"X1_{k}", [8 * 128, 8 * T]) for k in range(2)]
    X1s = dint("X1s", [128, KC * 128])
    TAB = [dint(f"TAB{k}", [128, 3 * 512]) for k in range(64)]

    es = ExitStack()
    _u = {"n": 0}

    def sbt(name, shape, dt=F32):
        _u["n"] += 1
        return nc.sbuf_tensor(f"{name}_u{_u['n']}", list(shape), dt)

    def sb(name, shape, dt=F32):
        return es.enter_context(nc.sbuf_tensor(name, list(shape), dt))

    def ps(name, shape, dt=F32):
        return es.enter_context(nc.psum_tensor(name, list(shape), dt))

    with es:
        xres = sb("xres", [128, KC, T])
        xb = sb("xb", [128, KC, T], BF16)
        hA = sb("hA", [128, KC, T], BF16)
        vec = sb("vec", [128, 13 * 16])
        ident = sb("ident", [128, 128], BF16)
        trineg = sb("trineg", [128, 128], BF16)
        onesneg = sb("onesneg", [128, 128], BF16)
        onesmean = sb("onesmean", [128, 128], BF16)
        trimask = sb("trimask", [128, 128], BF16)
        smask = sb("smask", [8, 128], BF16)
        PS = [ps(f"ps{i}", [128, 512]) for i in range(8)]

        def pstok(i):
            return f"ps{i}"

        cst_f = xres[:, 0, 0:128]
        for nm, src, dst, npart in (("ident", c_ident, ident, 128), ("trineg", c_trineg, trineg, 128),
                                    ("onesneg", c_onesneg, onesneg, 128), ("onesmean", c_onesmean, onesmean, 128),
                                    ("trimask", c_trimask, trimask, 128), ("smask", c_smask, smask, 8)):
            B.dma("sp", cst_f[0:npart, :], src, (), ["xres0"], "cst")
            B.cp("dve", dst[0:npart, :], cst_f[0:npart, :], ["xres0"], [nm])
        B.dma("sp", vec[:], vecs, (), ["vec"], "vec")

        def vcol(i, kc):
            return vec[:, i * 16 + kc:i * 16 + kc + 1]


        cfg = {"tts": None, "ncols": 0}

        def load_x(ti):
            nco = cfg["ncols"]
            src = xTp[ti * 128:(ti + 1) * 128, :].rearrange("p (k t) -> p k t", k=KC)
            for g4 in range(4):
                ks = slice(4 * g4, 4 * g4 + 4)
                B.dma("sp", xres[:, ks, 0:1024], src[:, ks, :], (), [f"xres{k}" for k in range(4 * g4, 4 * g4 + 4)], f"xl{g4}")
            if nco > 1024:
                B.dma("sp", xres[:, :, 1024:1152], xTs.rearrange("p (k t) -> p k t", k=KC), (), [f"xres{k}" for k in range(KC)], "xls")
            P.strict = False
            for kc in range(KC):
                B.cp("act", xb[:, kc, 0:nco], xres[:, kc, 0:nco], [f"xres{kc}"], [f"xb{kc}"])
            P.strict = True

        es_ssm = ExitStack()
        es.enter_context(es_ssm)

        def sbs(name, shape, dt=F32):
            return es_ssm.enter_context(sbt(name, list(shape), dt))
        BBT = sbs("BBT", [128, 16, 2, 128], BF16)
        m96t = sbs("m96t", [128, 1])
        sp2 = sbs("sp2", [128, 8, 64])
        LRc, LIc, DTc, Rr, Cc, Sc, Ta, Tb = [sp2[:, i, :] for i in range(8)]
        Hst = sbs("Hst", [128, 2, 64])
        B.dma("sp", m96t[:], c_m96, (), ["m96t"], "prm")
        B.dma("sp", LRc, lr2, (), ["sp2"], "prm")
        B.dma("sp", LIc, li2, (), ["sp2"], "prm")
        B.dma("sp", DTc, dt2, (), ["sp2"], "prm")
        B.memset("dve", Hst[:].rearrange("p a b -> p (a b)"), 0.0, ["Hst"])

        def lam_bar(lr, li, dtt, mag, c, s, t1, t2, tok):
            B.act(dtt, dtt, AF.Exp, tok, tok)
            B.tt("dve", t1, lr, dtt, ALU.mult, tok, tok)
            B.act(mag, t1, AF.Exp, tok, tok)
            B.tt("dve", t1, li, dtt, ALU.mult, tok, tok)
            B.act(s, t1, AF.Sin, tok, tok, scale=1.0 / 16)
            B.ts("dve", t2, t1, 1.0 / 16, math.pi / 2, ALU.mult, ALU.add, tok, tok)
            B.act(c, t2, AF.Sin, tok, tok)
            for _ in range(4):
                B.tt("dve", t1, c, c, ALU.mult, tok, tok)
                B.tt("dve", t2, s, s, ALU.mult, tok, tok)
                B.stt("dve", s, c, 2.0, s, ALU.mult, ALU.mult, tok, tok)
                B.tt("dve", c, t1, t2, ALU.subtract, tok, tok)

        lam_bar(LRc, LIc, DTc, Rr, Cc, Sc, Ta, Tb, ["sp2"])

        with ExitStack() as es3:
            sl = es3.enter_context(sbt("sl", [128, 12, 512], F32))
            for q in range(4):
                cs_ = slice(q * 512, (q + 1) * 512)
                tk = ["sl"]
                for i, src in enumerate((sLR, sLI, sDT, sBre, sBim)):
                    B.dma("sp", sl[:, i, :], src[:, cs_], (), tk, "prm")
                lr, li, dtt, bre_, bim_, mag, c, s, t1, t2, fr, fi = [sl[:, i, :] for i in range(12)]
                lam_bar(lr, li, dtt, mag, c, s, t1, t2, tk)
                B.tt("dve", c, mag, c, ALU.mult, tk, tk)
                B.tt("dve", s, mag, s, ALU.mult, tk, tk)
                B.ts("dve", c, c, -1.0, None, ALU.add, None, tk, tk)
                B.tt("dve", t1, lr, lr, ALU.mult, tk, tk)
                B.tt("dve", t2, li, li, ALU.mult, tk, tk)
                B.tt("dve", mag, t1, t2, ALU.add, tk, tk)
                P.add("dve", lambda e, o=mag: e.reciprocal(o, o), tk, tk)
                B.tt("dve", t1, c, lr, ALU.mult, tk, tk)
                B.tt("dve", t2, s, li, ALU.mult, tk, tk)
                B.tt("dve", t1, t1, t2, ALU.add, tk, tk)
                B.tt("dve", fr, t1, mag, ALU.mult, tk, tk)
                B.tt("dve", t1, s, lr, ALU.mult, tk, tk)
                B.tt("dve", t2, c, li, ALU.mult, tk, tk)
                B.tt("dve", t1, t1, t2, ALU.subtract, tk, tk)
                B.tt("dve", fi, t1, mag, ALU.mult, tk, tk)
                B.tt("dve", t1, fr, bre_, ALU.mult, tk, tk)
                B.tt("dve", t2, fi, bim_, ALU.mult, tk, tk)
                B.tt("dve", BBT[:, 4 * q:4 * q + 4, 0, :], t1.rearrange("p (a b) -> p a b", a=4),
                     t2.rearrange("p (a b) -> p a b", a=4), ALU.subtract, tk, ["BBT"])
                B.tt("dve", t1, fr, bim_, ALU.mult, tk, tk)
                B.tt("dve", t2, fi, bre_, ALU.mult, tk, tk)
                B.tt("dve", BBT[:, 4 * q:4 * q + 4, 1, :], t1.rearrange("p (a b) -> p a b", a=4),
                     t2.rearrange("p (a b) -> p a b", a=4), ALU.add, tk, ["BBT"])
        P.barrier()

        def ssm_phase(ti):
            with ExitStack() as es2:
                def sb2(name, shape, dt=F32):
                    return es2.enter_context(sbt(name, list(shape), dt))
                tabt = [sb2(f"tabt{i}", [128, 3, 512]) for i in range(2)]
                tsel = {"i": 0}
                decs = [sb2(f"dec{i}", [128, 512]) for i in range(2)]
                if ti == 0:
                    dsr = sb2("dsr", [128, 128])
                    dsamp = sb2("dsamp", [128, 128])
                nbb = 1 if ti == 0 else 2
                bres = [sb2(f"bre{i}", [128, 512]) for i in range(nbb)]; bims = [sb2(f"bim{i}", [128, 512]) for i in range(nbb)]
                t1 = sb2("t1", [128, 512]); t2 = sb2("t2", [128, 512])
                rre = sb2("rre", [128, 512]); rim = sb2("rim", [128, 512])
                gres = [sb2(f"gre{i}", [128, 512]) for i in range(2)]; gims = [sb2(f"gim{i}", [128, 512]) for i in range(2)]
                p1 = sb2("p1", [128, 512]); p2 = sb2("p2", [128, 512])
                gsel = {"n": 0}
                hreb = sb2("hreb", [128, 512], BF16); nhimb = sb2("nhimb", [128, 512], BF16)
                cpb = [sb2(f"cpb{b}", [128, 4, 2, 128], BF16) for b in range(2)]
                hl = sb2("hl", [128, 8])
                if ti == 0:
                    h0t = sb2("h0t", [128, 2, 16])
                    hst = sb2("hst", [128, 2, 16])
                nsm = sb2("nsm", [128, 2])
                bbz = sb2("bbz", [128, 2, 128], BF16)
                if ti == 0:
                    B.dma("sp", dsamp[:], c_dsamp, (), ["dsamp"], "prm")
                tk = ["tab0"]

                def cur_tabs():
                    tt_ = tabt[tsel["i"]]
                    return tt_[:, 0, :], tt_[:, 1, :], tt_[:, 2, :]

                def gen_tables(pair):
                    tsel["i"] = pair % 2
                    tk[0] = f"tab{pair % 2}"
                    ct, st, nst = cur_tabs()
                    dec = decs[pair % 2]
                    dtk = f"dec{pair % 2}"
                    if ti > 0:
                        B.dma("sp", tabt[pair % 2][:].rearrange("p a b -> p (a b)"), TAB[pair], ["TAB"], [tk[0]], f"tabl{pair % 2}")
                        B.ts("pool", dec[:], ct, 0.0, Rr[:, pair:pair + 1], ALU.mult, ALU.add, tk + ["sp2"], [dtk])
                        return
                    geng = "pool" if pair % 2 == 0 else "dve"
                    B.cp(geng, ct[:, 0:1], Cc[:, pair:pair + 1], ["sp2"] + tk, tk)
                    B.cp(geng, st[:, 0:1], Sc[:, pair:pair + 1], ["sp2"] + tk, tk)
                    m = 1
                    dk = tk + [dtk]
                    while m < 512:
                        cm = ct[:, m - 1:m]; sm = st[:, m - 1:m]
                        B.ts(geng, nsm[:, 0:1], sm, -1.0, None, ALU.mult, None, tk + ["nsm"], ["nsm"])
                        B.ts(geng, nst[:, 0:m], ct[:, 0:m], cm, None, ALU.mult, None, dk, dk)
                        B.ts(geng, dec[:, 0:m], st[:, 0:m], nsm[:, 0:1], None, ALU.mult, None, dk + ["nsm"], dk)
                        B.tt(geng, ct[:, m:2 * m], nst[:, 0:m], dec[:, 0:m], ALU.add, dk, dk)
                        B.ts(geng, nst[:, 0:m], ct[:, 0:m], sm, None, ALU.mult, None, dk, dk)
                        B.ts(geng, dec[:, 0:m], st[:, 0:m], cm, None, ALU.mult, None, dk, dk)
                        B.tt(geng, st[:, m:2 * m], nst[:, 0:m], dec[:, 0:m], ALU.add, dk, dk)
                        m *= 2
                    B.ts(geng, nst, st, -1.0, None, ALU.mult, None, tk, tk)
                    B.dma("sp", TAB[pair], tabt[pair % 2][:].rearrange("p a b -> p (a b)"), list(tk), ["TAB"], f"tabs{pair % 2}")
                    B.ts(geng, dec[:], ct, 0.0, Rr[:, pair:pair + 1], ALU.mult, ALU.add, tk + ["sp2"], [dtk])
                    if ti == 0:
                        B.ts("pool", dsr[:], dsamp[:], Rr[:, pair:pair + 1], None, ALU.mult, None, ["dsamp", "sp2"], ["dsr"])

                def ssm_tile(pair, kind, init_re, init_im, init_tok):
                    ct, st, nst = cur_tabs()
                    tk = [f"tab{pair % 2}"]
                    gi = gsel["n"] % 2
                    gsel["n"] += 1
                    gre, gim = gres[gi], gims[gi]
                    gtr, gti = f"gre{gi}", f"gim{gi}"
                    reng = "dve" if ti == 0 else "pool"
                    q1, q2 = (t1, t2) if ti == 0 else (p1, p2)
                    q1t, q2t = ("t1", "t2") if ti == 0 else ("p1", "p2")
                    chunk, j = pair // 4, pair % 4
                    rows = slice(32 * j, 32 * j + 32) if j < 3 else slice(64, 128)
                    if kind < 2:
                        N = 512; c0 = 512 * kind
                    else:
                        N = 128; c0 = 1024
                    P.strict = (kind == 2)
                    bi_ = gi % nbb
                    bre, bim = bres[bi_], bims[bi_]
                    bret, bimt = f"bre{bi_}", f"bim{bi_}"
                    pri, pii = (0, 1) if bi_ == 0 else (5, 6)
                    pr_, pi_ = PS[pri], PS[pii]
                    B.mm(pr_[:, 0:N], (BBT[rows, chunk, 0, :] if j < 3 else bbz[rows, 0, :]), xb[rows, chunk, c0:c0 + N], True, True, ["BBT", "bbz", f"xb{chunk}"], [f"ps{pri}"])
                    B.mm(pi_[:, 0:N], (BBT[rows, chunk, 1, :] if j < 3 else bbz[rows, 1, :]), xb[rows, chunk, c0:c0 + N], True, True, ["BBT", "bbz", f"xb{chunk}"], [f"ps{pii}"])
                    B.cp("act", bre[:, 0:N], pr_[:, 0:N], [f"ps{pri}"], [bret])
                    B.cp("act", bim[:, 0:N], pi_[:, 0:N], [f"ps{pii}"], [bimt])

                    def v(ap):
                        return ap[:, 0:N].rearrange("p (a b) -> p a b", b=8) if kind == 2 else ap[:, 0:N]

                    def tb(ap):
                        return ap[:, 0:8].unsqueeze(1).broadcast_to([128, 16, 8]) if kind == 2 else ap[:, 0:N]
                    B.tt("dve", v(t1), v(bre), tb(ct), ALU.mult, [bret] + tk, ["t1"])
                    B.tt("dve", v(t2), v(bim), tb(st), ALU.mult, [bimt] + tk, ["t2"])
                    B.tt("dve", rre[:, 0:N], t1[:, 0:N], t2[:, 0:N], ALU.add, ["t1", "t2"], ["rre"])
                    B.tt("dve", v(t1), v(bim), tb(ct), ALU.mult, [bimt] + tk, ["t1"])
                    B.tt("dve", v(t2), v(bre), tb(st), ALU.mult, [bret] + tk, ["t2"])
                    B.tt("dve", rim[:, 0:N], t1[:, 0:N], t2[:, 0:N], ALU.subtract, ["t1", "t2"], ["rim"])
                    if kind == 2:
                        B.dma("sp", h0t[:, 0, :], h0re[:, pair * 16:(pair + 1) * 16], (), ["h0t"], "h0")
                        B.dma("sp", h0t[:, 1, :], h0im[:, pair * 16:(pair + 1) * 16], (), ["h0t"], "h0")
                        B.stt("dve", v(rre)[:, :, 0], h0t[:, 0, :], Rr[:, pair:pair + 1], v(rre)[:, :, 0], ALU.mult, ALU.add, ["h0t", "sp2", "rre"], ["rre"])
                        B.stt("dve", v(rim)[:, :, 0], h0t[:, 1, :], Rr[:, pair:pair + 1], v(rim)[:, :, 0], ALU.mult, ALU.add, ["h0t", "sp2", "rim"], ["rim"])
                        dec_ = dsr; dtok = "dsr"
                    else:
                        dec_ = decs[pair % 2]; dtok = f"dec{pair % 2}"
                    P.add("dve", lambda e: e.tensor_tensor_scan(gre[:, 0:N], dec_[:, 0:N], rre[:, 0:N], init_re, ALU.mult, ALU.add),
                          ["rre", dtok] + init_tok, [gtr])
                    P.add("dve", lambda e: e.tensor_tensor_scan(gim[:, 0:N], dec_[:, 0:N], rim[:, 0:N], init_im, ALU.mult, ALU.add),
                          ["rim", dtok] + init_tok, [gti])
                    P.strict = True
                    if kind < 2:
                        L = N - 1
                        B.tt("dve", hl[:, 2:3], gre[:, L:N], ct[:, L:N], ALU.mult, [gtr] + tk, ["hl2"])
                        B.stt("dve", hl[:, 0:1], gim[:, L:N], nst[:, L:N], hl[:, 2:3], ALU.mult, ALU.add, [gti, "hl2"] + tk, ["hl"])
                        B.tt("dve", hl[:, 3:4], gre[:, L:N], st[:, L:N], ALU.mult, [gtr] + tk, ["hl3"])
                        B.stt("dve", hl[:, 1:2], gim[:, L:N], ct[:, L:N], hl[:, 3:4], ALU.mult, ALU.add, [gti, "hl3"] + tk, ["hl"])
                    else:
                        B.ts("dve", hst[:, 0, :], v(gre)[:, :, 7], ct[:, 7:8], None, ALU.mult, None, [gtr] + tk, ["hst0"])
                        B.stt("dve", hst[:, 0, :], v(gim)[:, :, 7], nst[:, 7:8], hst[:, 0, :], ALU.mult, ALU.add, [gti, "hst0"] + tk, ["hst0"])
                        B.ts("dve", hst[:, 1, :], v(gre)[:, :, 7], st[:, 7:8], None, ALU.mult, None, [gtr] + tk, ["hst1"])
                        B.stt("dve", hst[:, 1, :], v(gim)[:, :, 7], ct[:, 7:8], hst[:, 1, :], ALU.mult, ALU.add, [gti, "hst1"] + tk, ["hst1"])
                        B.dma("sp", o_ssre[:, pair * 16:(pair + 1) * 16], hst[:, 0, :], ["hst0"], ["o_ssre"], "hso0")
                        B.dma("sp", o_ssim[:, pair * 16:(pair + 1) * 16], hst[:, 1, :], ["hst1"], ["o_ssim"], "hso1")
                    P.strict = (kind == 2)
                    B.tt(reng, v(q1), v(gre), tb(ct), ALU.mult, [gtr] + tk, [q1t])
                    B.tt(reng, v(q2), v(gim), tb(st), ALU.mult, [gti] + tk, [q2t])
                    B.tt(reng, hreb[:, 0:N], q1[:, 0:N], q2[:, 0:N], ALU.subtract, [q1t, q2t], ["hreb"])
                    B.tt(reng, v(q1), v(gre), tb(nst), ALU.mult, [gtr] + tk, [q1t])
                    B.tt(reng, v(q2), v(gim), tb(ct), ALU.mult, [gti] + tk, [q2t])
                    B.tt(reng, nhimb[:, 0:N], q1[:, 0:N], q2[:, 0:N], ALU.subtract, [q1t, q2t], ["nhimb"])
                    py = PS[2 + kind]
                    cb = cpb[chunk % 2]
                    B.mm(py[:, 0:N], cb[:, j, 0, :], hreb[:, 0:N], j == 0, False, [f"cpb{chunk % 2}", "hreb"], [f"ps{2 + kind}"])
                    B.mm(py[:, 0:N], cb[:, j, 1, :], nhimb[:, 0:N], False, j == 3, [f"cpb{chunk % 2}", "nhimb"], [f"ps{2 + kind}"])
                    P.strict = True

                kinds = 3 if ti == 0 else 2
                for chunk in range(16):
                    cb = cpb[chunk % 2]
                    B.dma("pool", cb[:].rearrange("p a b c -> p (a b c)"), cpad[chunk], (), [f"cpb{chunk % 2}"], f"cpb{chunk % 2}")
                    B.ts("dve", bbz[:].rearrange("p a b -> p (a b)"), BBT[:, chunk, :, :].rearrange("p a b -> p (a b)"), m96t[:, 0:1], None,
                         ALU.mult, None, ["BBT", "m96t"], ["bbz"])
                    for j in range(4):
                        pair = chunk * 4 + j
                        gen_tables(pair)
                        ssm_tile(pair, 0, Hst[:, 0, pair:pair + 1], Hst[:, 1, pair:pair + 1], ["Hst"])
                        ssm_tile(pair, 1, hl[:, 0:1], hl[:, 1:2], ["hl"])
                        B.cp("dve", Hst[:, 0, pair:pair + 1], hl[:, 0:1], ["hl"], ["Hst"])
                        B.cp("dve", Hst[:, 1, pair:pair + 1], hl[:, 1:2], ["hl"], ["Hst"])
                        if ti == 0:
                            ssm_tile(pair, 2, 0.0, 0.0, [])
                    for kind in range(kinds):
                        N = 512 if kind < 2 else 128
                        c0 = 512 * kind
                        py = PS[2 + kind]
                        xr = xres[:, chunk, c0:c0 + N]
                        P.strict = (kind == 2)
                        B.stt("dve", rre[:, 0:N], xr, vcol(0, chunk), py[:, 0:N], ALU.mult, ALU.add, [f"xres{chunk}", "vec", f"ps{2 + kind}"], ["rre"])
                        B.act(t1[:, 0:N], rre[:, 0:N], AF.Square, ["rre"], ["t1"])
                        B.ts("dve", t1[:, 0:N], t1[:, 0:N], 0.044715, 1.0, ALU.mult, ALU.add, ["t1"], ["t1"])
                        B.tt("dve", t1[:, 0:N], t1[:, 0:N], rre[:, 0:N], ALU.mult, ["t1", "rre"], ["t1"])
                        B.act(t2[:, 0:N], t1[:, 0:N], AF.Sigmoid, ["t1"], ["t2"], scale=1.5957691216057308)
                        B.tt("dve", hA[:, chunk, c0:c0 + N], rre[:, 0:N], t2[:, 0:N], ALU.mult, ["rre", "t2"], [f"hA{chunk}"])
                        P.strict = True
                if ti == 7:
                    B.dma("sp", o_sre, Hst[:, 0, :], ["Hst"], ["o_sre"], "osp")
                    B.dma("sp", o_sim, Hst[:, 1, :], ["Hst"], ["o_sim"], "osp")
            P.barrier()

        dn = {}
        st_ = {"w": 0, "ps": 0, "e": 0}

        def dense_scope(stack):
            def sb5(name, shape, dt=F32):
                return stack.enter_context(sbt(name, list(shape), dt))
            dn["wbuf"] = [sb5(f"wbuf{i}", [128, 16, 256], BF16) for i in range(3)]
            dn["wps"] = [sb5(f"wps{i}", [128, 2, 256], BF16) for i in range(3)]
            dn["pbt"] = sb5("pbt", [128, 2, T], BF16)
            dn["mean_sb"] = sb5("mean_sb", [128, T])
            dn["rstd_sb"] = sb5("rstd_sb", [128, T])
            dn["etmp"] = [sb5(f"etmp{i}", [128, 512]) for i in range(2)]
            dn["epsb"] = sb5("epsb", [128, 1])
            B.memset("dve", dn["epsb"][:], EPS, ["epsb"])

        def next_ps():
            i = st_["ps"] % 6
            st_["ps"] += 1
            return i

        def next_et():
            i = st_["e"] % 2
            st_["e"] += 1
            return i

        def wview(W, r0, kcn, m0, n):
            return W[r0:r0 + kcn * 128, :].rearrange("(kc p) n -> p kc n", p=128)[:, :, m0:m0 + n]

        WB = {}

        def linear(W, r0, kcn, n_out, in_tile, in_name, epi, extra_load=None, wkey=None):
            wbuf = dn["wbuf"]
            tiles = [(mp * 256, min(256, n_out * 128 - mp * 256)) for mp in range((n_out * 128 + 255) // 256)]
            slots = []
            wmode = cfg.get("wmode") if wkey is not None else None
            if wmode is not None and wkey not in WB:
                WB[wkey] = [dint(f"WB_{wkey}_{i}", [128, kcn * 256], BF16) for i in range(len(tiles))]

            def load(i):
                s = st_["w"] % 3
                st_["w"] += 1
                m0, n = tiles[i]
                if wmode == "use":
                    B.dma("sp", wbuf[s][:, 0:kcn, :].rearrange("p a b -> p (a b)"), WB[wkey][i], [f"WB_{wkey}_{i}"], [f"wbuf{s}"], f"wbuf{s}")
                else:
                    B.dma("pool", wbuf[s][:, 0:kcn, 0:n], wview(W, r0, kcn, m0, n), (), [f"wbuf{s}"], f"wbuf{s}")
                    if wmode == "fill":
                        B.dma("sp", WB[wkey][i], wbuf[s][:, 0:kcn, :].rearrange("p a b -> p (a b)"), [f"wbuf{s}"], [f"WB_{wkey}_{i}"], f"wbs{s}")
                if extra_load is not None:
                    extra_load(s, m0, n)
                slots.append(s)
            for i in range(min(2, len(tiles))):
                load(i)
            for i, (m0, n) in enumerate(tiles):
                s = slots[i]
                for mi in range(n // 128):
                    m = m0 // 128 + mi
                    for (c0, nt) in cfg["tts"]:
                        pi_ = next_ps()
                        P.strict = (nt < 512)
                        for kc in range(kcn):
                            B.mm(PS[pi_][:, 0:nt], wbuf[s][:, kc, mi * 128:(mi + 1) * 128], in_tile[:, kc, c0:c0 + nt],
                                 kc == 0, kc == kcn - 1, [f"wbuf{s}", f"{in_name}{kc}"], [f"ps{pi_}"])
                        epi(m, (c0, nt), PS[pi_][:, 0:nt], f"ps{pi_}", s, mi)
                        P.strict = True
                if i + 2 < len(tiles):
                    load(i + 2)

        def epi_resid(first):
            def f(m, tt, pp, ptok, s, mi):
                c0, nt = tt
                xr = xres[:, m, c0:c0 + nt]
                if first:
                    B.stt("dve", xr, xr, ALPHA, pp, ALU.mult, ALU.add, [f"xres{m}", ptok], [f"xres{m}"])
                else:
                    B.tt("dve", xr, xr, pp, ALU.add, [f"xres{m}", ptok], [f"xres{m}"])
            return f

        def layer_norm(gi, bi):
            nco = cfg["ncols"]
            mean_sb, rstd_sb, epsb = dn["mean_sb"], dn["rstd_sb"], dn["epsb"]
            P.strict = False
            for kc in range(KC):
                B.cp("act", xb[:, kc, 0:nco], xres[:, kc, 0:nco], [f"xres{kc}"], [f"xb{kc}"])
                B.act(hA[:, kc, 0:nco], xres[:, kc, 0:nco], AF.Square, [f"xres{kc}"], [f"hA{kc}"])
            for (c0, nt) in cfg["tts"]:
                P.strict = (nt < 512)
                p1, p2 = next_ps(), next_ps()
                for kc in range(KC):
                    B.mm(PS[p1][:, 0:nt], onesmean[:], xb[:, kc, c0:c0 + nt], kc == 0, kc == KC - 1, ["onesmean", f"xb{kc}"], [f"ps{p1}"])
                for kc in range(KC):
                    B.mm(PS[p2][:, 0:nt], onesmean[:], hA[:, kc, c0:c0 + nt], kc == 0, kc == KC - 1, ["onesmean", f"hA{kc}"], [f"ps{p2}"])
                ms = mean_sb[:, c0:c0 + nt]
                rs = rstd_sb[:, c0:c0 + nt]
                B.cp("dve", ms, PS[p1][:, 0:nt], [f"ps{p1}"], ["mean_sb"])
                B.tt("dve", rs, ms, ms, ALU.mult, ["mean_sb"], ["rstd_sb"])
                B.tt("dve", rs, PS[p2][:, 0:nt], rs, ALU.subtract, [f"ps{p2}", "rstd_sb"], ["rstd_sb"])
                B.act(rs, rs, AF.Sqrt, ["rstd_sb", "epsb"], ["rstd_sb"], bias=epsb[:, 0:1])
                P.add("dve", lambda e, o=rs: e.reciprocal(o, o), ["rstd_sb"], ["rstd_sb"])
            P.strict = False
            for kc in range(KC):
                xr = xres[:, kc, 0:nco]
                B.tt("dve", xr, xr, mean_sb[:, 0:nco], ALU.subtract, [f"xres{kc}", "mean_sb"], [f"xres{kc}"])
                B.tt("dve", xr, xr, rstd_sb[:, 0:nco], ALU.mult, [f"xres{kc}", "rstd_sb"], [f"xres{kc}"])
                B.act(xr, xr, AF.Identity, [f"xres{kc}", "vec"], [f"xres{kc}"], bias=vcol(bi, kc), scale=vcol(gi, kc))
                B.cp("act", xb[:, kc, 0:nco], xr, [f"xres{kc}"], [f"xb{kc}"])
            P.strict = True

        def mlp(l):
            etmp = dn["etmp"]
            for q in range(4):
                def epi1(m, tt, pp, ptok, s, mi):
                    c0, nt = tt
                    k = next_et()
                    B.act(etmp[k][:, 0:nt], pp, AF.Relu, [ptok], [f"etmp{k}"])
                    B.tt("dve", hA[:, m, c0:c0 + nt], etmp[k][:, 0:nt], etmp[k][:, 0:nt], ALU.mult, [f"etmp{k}"], [f"hA{m}"])
                linear(w1[l][:, q * 2048:(q + 1) * 2048], 0, 16, 16, xb, "xb", epi1, wkey=f"w1_{l}_{q}")
                linear(w2[l], q * 2048, 16, 16, hA, "hA", epi_resid(q == 0), wkey=f"w2_{l}_{q}")

        def ple(l, psrc_list):
            etmp, pbt, wps = dn["etmp"], dn["pbt"], dn["wps"]
            for (dst_cols, src_ap) in psrc_list:
                B.dma("pool", pbt[:, :, dst_cols], src_ap, (), ["pbt"], "pbt")

            def xload(s, m0, n):
                B.dma("pool", wps[s][:, :, 0:n], wpp[l].rearrange("(kc p) n -> p kc n", p=128)[:, :, m0:m0 + n], (), [f"wps{s}"], f"wps{s}")

            def epi(m, tt, pp, ptok, s, mi):
                c0, nt = tt
                k = next_et()
                B.act(etmp[k][:, 0:nt], pp, AF.Sigmoid, [ptok], [f"etmp{k}"])
                p2 = next_ps()
                for kc in range(2):
                    B.mm(PS[p2][:, 0:nt], wps[s][:, kc, mi * 128:(mi + 1) * 128], pbt[:, kc, c0:c0 + nt], kc == 0, kc == 1, [f"wps{s}", "pbt"], [f"ps{p2}"])
                B.tt("dve", etmp[k][:, 0:nt], PS[p2][:, 0:nt], etmp[k][:, 0:nt], ALU.mult, [f"ps{p2}", f"etmp{k}"], [f"etmp{k}"])
                xr = xres[:, m, c0:c0 + nt]
                B.stt("dve", xr, xr, ALPHA, etmp[k][:, 0:nt], ALU.mult, ALU.add, [f"xres{m}", f"etmp{k}"], [f"xres{m}"])
            linear(wpg[l], 0, 16, 16, xb, "xb", epi, extra_load=xload, wkey=f"pg{l}")

        def epi_gate(m, tt, pp, ptok, s, mi):
            c0, nt = tt
            etmp = dn["etmp"]
            k = next_et()
            B.act(etmp[k][:, 0:nt], pp, AF.Sigmoid, [ptok], [f"etmp{k}"])
            B.tt("dve", xb[:, m, c0:c0 + nt], hA[:, m, c0:c0 + nt], etmp[k][:, 0:nt], ALU.mult, [f"hA{m}", f"etmp{k}"], [f"xb{m}"])

        def psrc(l, ti, with_s):
            lst = [(slice(0, 1024), pTp[l][ti * 128:(ti + 1) * 128, :].rearrange("p (k t) -> p k t", k=2))]
            if with_s:
                lst.append((slice(1024, 1152), pTs[l].rearrange("p (k t) -> p k t", k=2)))
            return lst

        def kv_proj(ti):
            etmp, wbuf = dn["etmp"], dn["wbuf"]

            def epi_k(m, tt, pp, ptok, s, mi):
                c0, nt = tt
                k = next_et()
                B.cp("dve", etmp[k][:, 0:nt], pp, [ptok], [f"etmp{k}"])
                if c0 < 1024:
                    B.dma("sp", o_kTp[ti * 128:(ti + 1) * 128, m * 1024 + c0:m * 1024 + c0 + nt], etmp[k][:, 0:nt], [f"etmp{k}"], ["o_kTp"], f"okt{k}")
                    B.dma("pool", KTa[m][ti * 128:(ti + 1) * 128, c0:c0 + nt], etmp[k][:, 0:nt], [f"etmp{k}"], ["KTa"], f"okb{k}")
                else:
                    B.dma("sp", o_kTs[:, m * 128:(m + 1) * 128], etmp[k][:, 0:nt], [f"etmp{k}"], ["o_kTs"], f"okt{k}")
                    B.dma("pool", KTs_d[:, m * 128:(m + 1) * 128], etmp[k][:, 0:nt], [f"etmp{k}"], ["KTs_d"], f"okb{k}")
            linear(w_k, 0, 16, 4, xb, "xb", epi_k, wkey="wk")
            for hf in range(2):
                B.dma("pool", wbuf[hf][:], wview(w_v, 0, 16, hf * 256, 256), (), [f"wbuf{hf}"], f"wbuf{hf}")
            nblk = cfg["ncols"] // 128
            P.strict = False
            for tb_ in range(nblk):
                pi_ = next_ps()
                for hf in range(2):
                    for kc in range(KC):
                        B.mm(PS[pi_][:, hf * 256:(hf + 1) * 256], xb[:, kc, tb_ * 128:(tb_ + 1) * 128], wbuf[hf][:, kc, :], kc == 0, kc == KC - 1,
                             [f"xb{kc}", f"wbuf{hf}"], [f"ps{pi_}"])
                k = next_et()
                B.cp("dve", etmp[k][:], PS[pi_][:, :], [f"ps{pi_}"], [f"etmp{k}"])
                if tb_ < 8:
                    B.dma("sp", o_vp[ti * 128:(ti + 1) * 128, tb_ * 512:(tb_ + 1) * 512], etmp[k][:], [f"etmp{k}"], ["o_vp"], f"okt{k}")
                    for kvh in range(4):
                        B.dma("pool", Va[kvh][ti * 128:(ti + 1) * 128, tb_ * 128:(tb_ + 1) * 128], etmp[k][:, kvh * 128:(kvh + 1) * 128],
                              [f"etmp{k}"], ["Va"], f"okb{k}")
                else:
                    B.dma("sp", o_vs, etmp[k][:], [f"etmp{k}"], ["o_vs"], f"okt{k}")
                    B.dma("pool", Vs_d, etmp[k][:], [f"etmp{k}"], ["Vs_d"], f"okb{k}")
            P.strict = True

        NT0 = 8
        for ti in range(NT0):
            if ti == 0:
                cfg["ncols"] = T
                cfg["tts"] = [(0, 512), (512, 512), (1024, 128)]
            else:
                cfg["ncols"] = 1024
                cfg["tts"] = [(0, 512), (512, 512)]
            cfg["wmode"] = "fill" if ti == 0 else "use"
            load_x(ti)
            ssm_phase(ti)
            with ExitStack() as esd:
                dense_scope(esd)
                linear(w_gate, 0, 16, 16, hA, "hA", epi_gate, wkey="wg")
                linear(w_out, 0, 16, 16, xb, "xb", epi_resid(True), wkey="wo")
                layer_norm(1, 2)
                mlp(0)
                layer_norm(3, 4)
                ple(0, psrc(0, ti, ti == 0))
                layer_norm(5, 6)
                kv_proj(ti)
                for hf in range(2):
                    B.dma("sp", X1[hf][ti * 128:(ti + 1) * 128, :], xres[:, 8 * hf:8 * hf + 8, :].rearrange("p a b -> p (a b)"),
                          [f"xres{k}" for k in range(8 * hf, 8 * hf + 8)], ["X1"], f"x1s{hf}")
                if ti == 0:
                    B.dma("sp", X1s.rearrange("p (k t) -> p k t", k=KC), xres[:, :, 1024:1152], [f"xres{k}" for k in range(KC)], ["X1s"], "x1ss")
                P.barrier(new_epoch=True)
        cfg["wmode"] = None
        es_ssm.close()

        cfg["ncols"] = T
        cfg["tts"] = [(0, 512), (512, 512), (1024, 128)]
        ridx_sb = sb("ridx_sb", [128, 8], I32)
        B.dma("sp", ridx_sb[:], ridx, (), ["ridx_sb"], "prm")
        for hf in range(2):
            P.add("pool", lambda e, hf=hf: e.indirect_dma_start(
                out=xres[:, 8 * hf:8 * hf + 8, :].rearrange("p a b -> p (a b)"), out_offset=None, in_=X1[hf],
                in_offset=bass.IndirectOffsetOnAxis(ap=ridx_sb[:, 0:1], axis=0)),
                ["ridx_sb", "X1"], [f"xres{k}" for k in range(8 * hf, 8 * hf + 8)], dma=f"x1g{hf}")
        B.dma("sp", xres[:, :, 1024:1152], X1s.rearrange("p (k t) -> p k t", k=KC), ["X1s"], [f"xres{k}" for k in range(KC)], "x1sl")
        for kc in range(KC):
            B.cp("act", xb[:, kc, :], xres[:, kc, :], [f"xres{kc}"], [f"xb{kc}"])

        with ExitStack() as esd:
            dense_scope(esd)

            def epi_q(m, tt, pp, ptok, s, mi):
                c0, nt = tt
                B.act(hA[:, m, c0:c0 + nt], pp, AF.Copy, [ptok], [f"hA{m}"], scale=128.0 ** -0.5)
            linear(w_q, 0, 16, 16, xb, "xb", epi_q)
            P.barrier()

        if STAGE == 2:
            with ExitStack() as esa:
                def sba(name, shape, dt=F32):
                    return esa.enter_context(sbt(name, list(shape), dt))
                biasT = sba("biasT", [128, 16, 8])
                sbb_sb = sba("sbb_sb", [128, 16])
                cm_sb = sba("cm_sb", [128, 8])
                zerosT = sba("zerosT", [128, 128], BF16)
                def mk_state(nm):
                    return {"nm": nm, "Rf": sba(f"Rf{nm}", [128, 512]), "Rb": [sba(f"Rb{nm}{i}", [128, 512], BF16) for i in range(2)], "r": 0}
                SW = [mk_state("a"), mk_state("b")]
                Ef = [sba(f"Ef{i}", [128, 512]) for i in range(3)]
                Gf = [sba(f"Gf{i}", [128, 512]) for i in range(3)]
                Lpb = [sba(f"Lpb{i}", [128, 512], BF16) for i in range(2)]
                wb = [sba(f"wb{i}", [128, 512], BF16) for i in range(3)]
                B.dma("sp", sbb_sb[:], sbb, (), ["sbb_sb"], "prm")
                B.dma("sp", cm_sb[:], cmask, (), ["cm_sb"], "prm")
                B.memset("dve", zerosT[:], 0.0, ["zerosT"])
                for j in range(8):
                    B.ts("dve", biasT[:, :, j], sbb_sb[:], cm_sb[:, j:j + 1], None, ALU.add, None, ["sbb_sb", "cm_sb"], ["biasT"])
                bc = {"n": 0}

                pend = {"p": None}

                fcnt = {"n": 0}

                def pe_fence():
                    fcnt["n"] += 1
                    tkn = f"fence{fcnt['n']}"
                    B.mm(PS[7][0:8, 0:8], ident[0:8, 0:8], ident[0:8, 0:8], True, True, ["ident"], [tkn])
                    return tkn

                def sb_block(S, nk, N, qk_list, bias_ap, mask_fn, first, roff, wv_list):
                    i = bc["n"] % 2
                    wi = bc["n"] % 3
                    bc["n"] += 1
                    pa = i
                    A = PS[pa]
                    P.strict = (N < 512)
                    nq = len(qk_list)
                    started = False
                    if mask_fn is not None and mask_fn[0] == "pre":
                        for (o_, l_, r_, rt) in mask_fn[1]:
                            B.mm(o_(A), l_, r_, not started, False, rt, [f"ps{pa}"])
                            started = True
                    for qi, (oc, l_, r_, rt) in enumerate(qk_list):
                        last = (qi == nq - 1) and not (mask_fn is not None and mask_fn[0] == "post")
                        B.mm(A[0:nk, oc], l_, r_, not started, last, rt, [f"ps{pa}"])
                        if len(qk_list) == 1:
                            started = True
                    if mask_fn is not None and mask_fn[0] == "post":
                        for (o_, l_, r_, rt) in mask_fn[1]:
                            B.mm(o_(A), l_, r_, False, True, rt, [f"ps{pa}"])
                    E = Ef[wi][0:nk, 0:N]; L = Lpb[i][0:nk, 0:N]
                    fz = []
                    if S.get("fence"):
                        fz = [pe_fence()]
                    if bias_ap is not None:
                        B.act(E, A[0:nk, 0:N], AF.Exp, [f"ps{pa}", "biasT"] + fz, [f"Ef{wi}"], bias=bias_ap)
                    else:
                        B.act(E, A[0:nk, 0:N], AF.Exp, [f"ps{pa}"] + fz, [f"Ef{wi}"])
                    B.act(L, E, AF.Ln, [f"Ef{wi}"], [f"Lpb{i}"], bias=1.0)
                    P.strict = True
                    cur = (S, i, nk, N, first, roff, wv_list, wi)
                    if not S.get("pipe", True):
                        stage2(*cur)
                        stage3(*cur)
                        return
                    q = pend.setdefault("q", [])
                    q.append(cur)
                    if len(q) >= 2:
                        stage2(*q[-2])
                    if len(q) >= 4:
                        stage3(*q[0])
                        q.pop(0)

                def flush_blocks():
                    q = pend.setdefault("q", [])
                    if q:
                        stage2(*q[-1])
                    for c_ in q:
                        stage3(*c_)
                    q.clear()

                def stage2(S, i, nk, N, first, roff, wv_list, wi):
                    pb_ = 2 + i
                    Bk = PS[pb_]
                    P.strict = (N < 512)
                    E = Ef[wi][0:nk, 0:N]; L = Lpb[i][0:nk, 0:N]; G = Gf[wi][0:nk, 0:N]; W = wb[wi][0:nk, 0:N]
                    Rf, Rb, nm = S["Rf"], S["Rb"], S["nm"]
                    rcur = S["r"]
                    B.mm(Bk[0:nk, 0:N], trineg[0:nk, 0:nk], L, True, first, ["trineg", f"Lpb{i}"], [f"ps{pb_}"])
                    if not first:
                        B.mm(Bk[0:nk, 0:N], onesneg[:, 0:nk], Rb[rcur][:, roff:roff + N], False, True, ["onesneg", f"Rb{nm}{rcur}"], [f"ps{pb_}"])
                    rn = 1 - rcur
                    reng_ = S.get("r_eng", "dve")
                    B.tt(reng_, Rf[0:nk, roff:roff + N], Rf[0:nk, roff:roff + N], L, ALU.add, [f"Rf{nm}", f"Lpb{i}"], [f"Rf{nm}"])
                    B.cp(reng_, Rb[rn][:, :], Rf[:, :], [f"Rf{nm}"], [f"Rb{nm}{rn}"])
                    S["r"] = rn
                    fz = []
                    if S.get("fence"):
                        fz = [pe_fence()]
                    B.act(G, Bk[0:nk, 0:N], AF.Exp, [f"ps{pb_}"] + fz, [f"Gf{wi}"])
                    B.tt(S.get("w_eng", "pool"), W, E, G, ALU.mult, [f"Ef{wi}", f"Gf{wi}"], [f"wb{wi}"])
                    P.strict = True

                def stage3(S, i, nk, N, first, roff, wv_list, wi):
                    P.strict = (N < 512)
                    for (po, ptok, l_, wc, st0, sp0, rt) in wv_list:
                        B.mm(po, l_, wb[wi][0:nk, wc], st0, sp0, rt + [f"wb{wi}"], [ptok])
                    P.strict = True

                with ExitStack() as esp:
                    def sbp(name, shape, dt=F32):
                        return esp.enter_context(sbt(name, list(shape), dt))
                    kbuf = [sbp(f"kbuf{i}", [128, 1024], BF16) for i in range(3)]
                    vbuf = [sbp(f"vbuf{i}", [128, 1024], BF16) for i in range(3)]

                    def gather(hj):
                        h_, j_ = hj // 8, hj % 8
                        g_ = hj % 3
                        kvh_ = h_ // 4
                        P.add("pool", lambda e: e.indirect_dma_start(
                            out=kbuf[g_][:], out_offset=None, in_=KTa[kvh_],
                            in_offset=bass.IndirectOffsetOnAxis(ap=ridx_sb[:, j_:j_ + 1], axis=0)),
                            ["ridx_sb", "KTa"], [f"kbuf{g_}"], dma=f"kg{g_}")
                        P.add("pool", lambda e: e.indirect_dma_start(
                            out=vbuf[g_][:], out_offset=None, in_=Va[kvh_],
                            in_offset=bass.IndirectOffsetOnAxis(ap=ridx_sb[:, j_:j_ + 1], axis=0)),
                            ["ridx_sb", "Va"], [f"vbuf{g_}"], dma=f"vg{g_}")
                    gather(0)
                    gcount = {"n": 0}
                    for h in range(16):
                        kvh = h // 4
                        POs = [PS[4], PS[5]]
                        for qt in range(2):
                            B.memset("pool", SW[qt]["Rf"][:], 0.0, [f"Rf{SW[qt]['nm']}"])
                            B.mm(POs[qt][:, 0:512], zerosT[:], hA[:, h, qt * 512:(qt + 1) * 512], True, False, ["zerosT", f"hA{h}"], [f"ps{4 + qt}"])
                        firsts = [True, True]
                        for j in range(8):
                            hj = h * 8 + j
                            g = hj % 3
                            if hj + 1 < 128:
                                gather(hj + 1)
                            for kb in range(7, -1, -1):
                                for qt in ((1, 0) if j == 0 else (0, 1)):
                                    if j == 0 and kb > 4 * qt + 3:
                                        continue
                                    diag = (j == 0 and kb >= 4 * qt)
                                    off = (kb - 4 * qt) * 128 if diag else 0
                                    N = 512 - off
                                    q0 = qt * 512 + off
                                    qk = [(slice(0, N), kbuf[g][:, kb * 128:(kb + 1) * 128], hA[:, h, q0:q0 + N], [f"kbuf{g}", f"hA{h}"])]
                                    mk = None
                                    if diag:
                                        mk = ("post", [(lambda A_: A_[:, 0:128], ident[:], trimask[:], ["ident", "trimask"])])
                                    last_blk = (j == 7 and kb == 0)
                                    wv = [(POs[qt][:, off:512], f"ps{4 + qt}", vbuf[g][:, kb * 128:(kb + 1) * 128], slice(0, N), False, last_blk, [f"vbuf{g}"])]
                                    sb_block(SW[qt], 128, N, qk, biasT[:, h, j:j + 1], mk, firsts[qt], off, wv)
                                    firsts[qt] = False
                        flush_blocks()
                        for qt in range(2):
                            B.cp("act", hA[:, h, qt * 512:(qt + 1) * 512], POs[qt][:, 0:512], [f"ps{4 + qt}"], [f"hA{h}"])
                P.barrier()

                with ExitStack() as ess:
                    def sbq(name, shape, dt=F32):
                        return ess.enter_context(sbt(name, list(shape), dt))
                    xbf = xb[:].rearrange("p a b -> p (a b)")
                    Kraw = xbf[:, 0:8192].rearrange("p (a b) -> p a b", b=512)
                    Vraw = sbq("Vraw", [128, 16, 512], BF16)
                    KTt = xbf[:, 8192:16384].rearrange("p (a b) -> p a b", b=2048)
                    qS = sbq("qS", [128, 16, 128], BF16)
                    kTs = sbq("kTs", [128, 512], BF16)
                    vnew = sbq("vnew", [8, 512], BF16)
                    pt_sb = sbq("pt_sb", [128, 256], I32)
                    idx_all = sbq("idx_all", [128, 256], I32)
                    iota_c = sbq("iota_c", [128, 1])
                    br_f = sbq("br_f", [2, 128]); br_h = sbq("br_h", [2, 128], BF16); br_l = sbq("br_l", [2, 128])
                    br2 = sbq("br2", [2, 128], BF16); m01_sb = sbq("m01_sb", [2, 2]); ones2 = sbq("ones2", [2, 128], BF16)
                    B.dma("sp", pt_sb[:], ptab, (), ["pt_sb"], "prm")
                    B.dma("sp", iota_c[:], c_iota, (), ["iota_c"], "prm")
                    B.dma("sp", br_f[:], brow, (), ["br"], "prm")
                    B.dma("sp", m01_sb[:], m01, (), ["br"], "prm")
                    B.dma("sp", kTs[:], KTs_d, ["KTs_d"], ["kTs"], "prm")
                    B.ts("dve", idx_all[:], pt_sb[:], 128.0, iota_c[:, 0:1], ALU.mult, ALU.add, ["pt_sb", "iota_c"], ["idx_all"])
                    B.cp("dve", br_h[:], br_f[:], ["br"], ["br"])
                    B.tt("dve", br_l[:], br_f[:], br_h[:], ALU.subtract, ["br"], ["br"])
                    B.cp("dve", br_f[:], br_h[:], ["br"], ["br"])
                    B.ts("dve", br_f[:], br_f[:], m01_sb[:, 0:1], None, ALU.mult, None, ["br"], ["br"])
                    B.stt("dve", br2[:], br_l[:], m01_sb[:, 1:2], br_f[:], ALU.mult, ALU.add, ["br"], ["br2"])
                    B.memset("dve", ones2[:], 1.0, ["ones2"])
                    B.cp("dve", qS[:].rearrange("p b (h t) -> p b h t", t=8),
                         hA[:, :, 1024:1152].rearrange("p h (b t) -> p b h t", t=8), [f"hA{h}" for h in range(16)], ["qS"])
                    PTb = PS[6][:].bitcast(BF16)
                    for b in range(16):
                        for pg in range(16):
                            col = b * 16 + pg
                            P.add("pool", lambda e, pg=pg, col=col: e.indirect_dma_start(
                                out=Kraw[:, pg, :], out_offset=None, in_=cache_k,
                                in_offset=bass.IndirectOffsetOnAxis(ap=idx_all[:, col:col + 1], axis=0)),
                                ["idx_all"], [f"Kraw{pg}"], dma=f"ckg{pg % 4}")
                            P.add("pool", lambda e, pg=pg, col=col: e.indirect_dma_start(
                                out=Vraw[:, pg, :], out_offset=None, in_=cache_v,
                                in_offset=bass.IndirectOffsetOnAxis(ap=idx_all[:, col:col + 1], axis=0)),
                                ["idx_all"], [f"Vraw{pg}"], dma=f"cvg{pg % 4}")
                        B.dma("sp", vnew[:], Vs_d[b * 8:(b + 1) * 8, :], ["Vs_d"], ["vnew"], "vnew")
                        for pg in range(16):
                            for kvh in range(4):
                                B.tr(PTb[:, kvh * 128:(kvh + 1) * 128], Kraw[:, pg, kvh * 128:(kvh + 1) * 128], ident[:], [f"Kraw{p_}" for p_ in range(16)] + ["ident"], ["ps6"])
                            ftk = pe_fence()
                            B.cp("dve", KTt[:, :, pg * 128:(pg + 1) * 128], PTb[:, 0:512].rearrange("p (k s) -> p k s", k=4), ["ps6", ftk], ["KTt"])
                        PO = PS[4 + b % 2]
                        potok = f"ps{4 + b % 2}"
                        SS = SW[b % 2]
                        SS["fence"] = False
                        B.memset("pool", SS["Rf"][:], 0.0, [f"Rf{SS['nm']}"])
                        first = True
                        for blk in range(16, -1, -1):
                            nk = 8 if blk == 16 else 128
                            pre = [(lambda A_, nk=nk: A_[0:nk, 0:128], ones2[0:2, 0:nk], br2[0:2, :], ["ones2", "br2"])]
                            if blk == 16:
                                pre.append((lambda A_: A_[0:8, 0:128], ident[0:8, 0:8], smask[0:8, :], ["ident", "smask"]))
                            qk = []
                            wv = []
                            for kvh in range(4):
                                if blk == 16:
                                    kt_ap = kTs[:, kvh * 128 + b * 8:kvh * 128 + b * 8 + 8]
                                    v_ap = vnew[0:8, kvh * 128:(kvh + 1) * 128]
                                    rt = ["kTs"]; vt = ["vnew"]
                                else:
                                    kt_ap = KTt[:, kvh, blk * 128:(blk + 1) * 128]
                                    v_ap = Vraw[:, blk, kvh * 128:(kvh + 1) * 128]
                                    rt = ["KTt"]; vt = [f"Vraw{p_}" for p_ in range(16)]
                                qk.append((slice(kvh * 32, kvh * 32 + 32), kt_ap, qS[:, b, kvh * 32:(kvh + 1) * 32], rt + ["qS"]))
                                wv.append((PO[:, kvh * 32:(kvh + 1) * 32], potok, v_ap, slice(kvh * 32, kvh * 32 + 32), blk == 16, blk == 0, vt))
                            sb_block(SS, nk, 128, qk, None, ("pre", pre), first, 0, wv)
                            first = False
                        flush_blocks()
                        ftk = pe_fence()
                        B.cp("act", hA[:, :, 1024 + b * 8:1024 + b * 8 + 8], PO[:, 0:128].rearrange("p (h t) -> p h t", t=8), [potok, ftk], [f"hA{h}" for h in range(16)])
                P.barrier()

        with ExitStack() as esd:
            dense_scope(esd)
            linear(w_o, 0, 16, 16, hA, "hA", epi_resid(True))
            layer_norm(7, 8)
            mlp(1)
            layer_norm(9, 10)
            pown = esd.enter_context(sbt("pown", [128, 2048], F32))
            P.add("pool", lambda e: e.indirect_dma_start(out=pown[:], out_offset=None, in_=pTp[1],
                                                         in_offset=bass.IndirectOffsetOnAxis(ap=ridx_sb[:, 0:1], axis=0)),
                  ["ridx_sb"], ["pown"], dma="pown")
            B.cp("dve", dn["pbt"][:, :, 0:1024], pown[:].rearrange("p (k t) -> p k t", k=2), ["pown"], ["pbt"])
            ple(1, [(slice(1024, 1152), pTs[1].rearrange("p (k t) -> p k t", k=2))])
            layer_norm(11, 12)
            for kc in range(KC):
                B.dma("sp", o_y[:, kc * T:(kc + 1) * T], xres[:, kc, :], [f"xres{kc}"], ["o_y"], "oy")
        P.add("sp", None, ["o_y", "o_sre", "o_sim", "o_ssre", "o_ssim", "o_kTp", "o_vp", "o_kTs", "o_vs"], ["done"])
        P.emit()
    return nc


_PROG = None


def _get_prog():
    global _PROG
    if _PROG is None:
        _PROG = build_program()
    return _PROG


def _fm(a, nchunk):
    t = a.shape[0]
    return np.ascontiguousarray(a.reshape(t, nchunk, 128).transpose(2, 1, 0).reshape(128, nchunk * t))


def _unfm(a, nchunk, t):
    return np.ascontiguousarray(a.reshape(128, nchunk, t).transpose(2, 1, 0).reshape(t, nchunk * 128))


def kernel(**inp):
    f32 = np.float32
    g = lambda k: np.asarray(inp[k])
    x_prompt = g("x_prompt").astype(f32, copy=False)[0]
    x_sample = g("x_sample").astype(f32, copy=False).reshape(1024, 2048)
    p_prompt = g("p_prompt").astype(f32, copy=False)[:, 0]
    p_sample = g("p_sample").astype(f32, copy=False).reshape(2, 1024, 256)
    sre = g("state_ssm_re")[0]
    sim = g("state_ssm_im")[0]
    ptab_full = g("page_table").astype(np.int32)

    def gp_to_pair(a):
        return np.ascontiguousarray(a.reshape(64, 2, 64).transpose(1, 2, 0).reshape(128, 64)).astype(f32)

    lam_re = g("s5_lam_re")[0]; lam_im = g("s5_lam_im")[0]
    log_dt = g("s5_log_dt")[0]
    ldt_gp = np.repeat(log_dt[:, None], 64, axis=1)
    lr2 = gp_to_pair(lam_re); li2 = gp_to_pair(lam_im); dt2 = gp_to_pair(ldt_gp)

    def feat_layout(a_gp):
        a = a_gp.reshape(16, 4, 2, 64)
        a = a.transpose(1, 2, 0, 3)
        a = np.broadcast_to(a[:, :, None, :, None, :], (4, 2, 16, 16, 2, 64))
        return np.ascontiguousarray(a.reshape(128, 2048)).astype(f32)

    sLR = feat_layout(lam_re); sLI = feat_layout(lam_im); sDT = feat_layout(ldt_gp)

    def bpad(b):
        a = b.reshape(16, 4, 2, 64, 16).transpose(1, 2, 4, 0, 3)
        out = np.zeros((4, 2, 16, 16, 2, 64), f32)
        for g2 in range(2):
            out[:, g2, :, :, g2, :] = a[:, g2]
        return out.reshape(128, 2048)

    sBre = bpad(g("s5_b_re")[0]); sBim = bpad(g("s5_b_im")[0])

    def cpad_fn(cre, cim):
        out = np.zeros((16, 2, 64, 4, 2, 4, 2, 16), f32)
        for ri, c in enumerate((cre, cim)):
            a = c.reshape(16, 4, 2, 16, 64)
            for j in range(4):
                for g2 in range(2):
                    out[:, g2, :, j, ri, j, g2, :] = a[:, j, g2].transpose(0, 2, 1)
        return out.reshape(16, 128, 1024)

    cpad = cpad_fn(g("s5_c_re")[0], g("s5_c_im")[0])

    def pk(v):
        return np.ascontiguousarray(v.reshape(16, 128).T)

    vlist = [g("s5_d")[0]]
    for nm in ("ln_mix", "ln_mlp", "ln_ple"):
        vlist += [g(nm + "_g")[0], g(nm + "_b")[0]]
    for nm in ("ln_mix", "ln_mlp", "ln_ple"):
        vlist += [g(nm + "_g")[1], g(nm + "_b")[1]]
    vecs = np.concatenate([pk(v) for v in vlist], axis=1).astype(f32)

    ii = np.arange(128)
    c_ident = np.eye(128, dtype=f32)
    c_trineg = -(ii[:, None] >= ii[None, :]).astype(f32)
    c_onesneg = -np.ones((128, 128), f32)
    c_onesmean = np.full((128, 128), 1.0 / 2048, f32)
    c_trimask = np.where(ii[:, None] >= ii[None, :], NEG, 0.0).astype(f32)
    c_smask = np.where(np.arange(8)[:, None] >= (np.arange(128) % 8)[None, :], NEG, 0.0).astype(f32)
    c_dsamp = np.broadcast_to((np.arange(128) % 8 != 0).astype(f32)[None, :], (128, 128)).copy()
    sbb = np.broadcast_to(g("sb_bias")[0][None, :], (128, 16)).astype(f32).copy()
    brow = np.broadcast_to(np.repeat(g("sb_bias")[0], 8)[None, :], (2, 128)).astype(f32).copy()
    m01 = np.eye(2, dtype=f32)

    shared = dict(sLR=sLR, sLI=sLI, sDT=sDT, sBre=sBre, sBim=sBim, lr2=lr2, li2=li2, dt2=dt2, cpad=cpad, vecs=vecs,
                  sbb=sbb, brow=brow, m01=m01, c_ident=c_ident, c_trineg=c_trineg, c_onesneg=c_onesneg,
                  c_onesmean=c_onesmean, c_trimask=c_trimask, c_smask=c_smask, c_dsamp=c_dsamp, c_m96=(np.arange(128) >= 96).astype(np.float32).reshape(128, 1),
                  s5_w_gate=g("s5_w_gate")[0], s5_w_out=g("s5_w_out")[0], w_k=g("w_k"), w_v=g("w_v"),
                  sb_w_q=g("sb_w_q")[0], sb_w_o=g("sb_w_o")[0],
                  mlp_w1_0=g("mlp_w1")[0], mlp_w1_1=g("mlp_w1")[1], mlp_w2_0=g("mlp_w2")[0], mlp_w2_1=g("mlp_w2")[1],
                  ple_w_proj_0=g("ple_w_proj")[0], ple_w_proj_1=g("ple_w_proj")[1],
                  ple_w_gate_0=g("ple_w_gate")[0], ple_w_gate_1=g("ple_w_gate")[1],
                  )
    if STAGE == 2:
        shared["cache_k"] = g("cache_k").reshape(2560 * 128, 512)
        shared["cache_v"] = g("cache_v").reshape(2560 * 128, 512)
    shared = {k: np.ascontiguousarray(v, dtype=f32) for k, v in shared.items()}

    shared["xTp"] = np.concatenate([_fm(x_prompt[i * TP:(i + 1) * TP], 16) for i in range(8)], axis=0)
    for l in range(2):
        shared[f"pTp{l}"] = np.concatenate([_fm(p_prompt[l, i * TP:(i + 1) * TP], 2) for i in range(8)], axis=0)
    shared["c_iota"] = np.arange(128, dtype=f32).reshape(128, 1)
    in_maps = []
    for c in range(NC):
        m = dict(shared)
        m["xTs"] = _fm(x_sample[c * TS:(c + 1) * TS], 16)
        for l in range(2):
            m[f"pTs{l}"] = _fm(p_sample[l, c * TS:(c + 1) * TS], 2)
        for nm, s in (("h0re", sre), ("h0im", sim)):
            a = s[c * 16:(c + 1) * 16].reshape(16, 64, 2, 64).transpose(2, 3, 1, 0)
            m[nm] = np.ascontiguousarray(a.reshape(128, 1024)).astype(f32)
        m["ptab"] = np.ascontiguousarray(np.broadcast_to(ptab_full[c * 16:(c + 1) * 16].reshape(1, 256), (128, 256))).astype(np.int32)
        r = np.array([((c - j) % NC) for j in range(8)], np.int32)
        m["ridx"] = (r[None, :] * 128 + np.arange(128, dtype=np.int32)[:, None]).astype(np.int32)
        cm = np.array([0.0 if (c - j) >= 0 else NEG for j in range(8)], f32)
        m["cmask"] = np.broadcast_to(cm[None, :], (128, 8)).copy()
        in_maps.append(m)

    nc = _get_prog()
    res = run_bass_kernel_spmd(nc, in_maps, core_ids=list(range(NC)))
    R = res.results

    y_all = [_unfm(np.asarray(R[c]["o_y"]), 16, T) for c in range(NC)]
    y_prompt = np.concatenate([y[:TP] for y in y_all], axis=0)[None]
    y_sample = np.concatenate([y[TP:] for y in y_all], axis=0).reshape(128, 8, 2048)

    def st_prompt(a):
        return np.ascontiguousarray(np.asarray(a).reshape(2, 64, 64).transpose(2, 0, 1).reshape(1, 1, 128, 64))
    sre_p = st_prompt(R[0]["o_sre"]); sim_p = st_prompt(R[0]["o_sim"])

    def st_sample(nm):
        outs = []
        for c in range(NC):
            a = np.asarray(R[c][nm]).reshape(2, 64, 64, 16).transpose(3, 2, 0, 1)
            outs.append(a.reshape(16, 128, 64))
        return np.ascontiguousarray(np.concatenate(outs, axis=0)[None])
    sre_s = st_sample("o_ssre"); sim_s = st_sample("o_ssim")

    kp = np.asarray(R[0]["o_kTp"]).reshape(8, 128, 4, 1024).transpose(0, 3, 2, 1)
    k_p = np.ascontiguousarray(kp.reshape(1, 8192, 4, 128))
    vp = np.asarray(R[0]["o_vp"]).reshape(8, 128, 8, 512).transpose(0, 2, 1, 3)
    v_p = np.ascontiguousarray(vp.reshape(1, 8192, 4, 128))
    k_s = np.concatenate([np.asarray(R[c]["o_kTs"]).reshape(128, 4, 16, 8).transpose(2, 3, 1, 0) for c in range(NC)], axis=0)
    v_s = np.concatenate([np.asarray(R[c]["o_vs"]).reshape(16, 8, 4, 128) for c in range(NC)], axis=0)
    outs = (y_prompt, y_sample, sre_p, sim_p, k_p, v_p, sre_s, sim_s, np.ascontiguousarray(k_s), np.ascontiguousarray(v_s))
    return tuple(np.ascontiguousarray(o, dtype=f32) for o in outs)
```

```python
import math
from contextlib import ExitStack
import numpy as np
import ml_dtypes
import concourse.bass as bass
import concourse.mybir as mybir
from concourse.bass_utils import run_bass_kernel_spmd

F32 = mybir.dt.float32
BF16 = mybir.dt.bfloat16
I32 = mybir.dt.int32
AF = mybir.ActivationFunctionType
ALU = mybir.AluOpType

NC = 8
TP = 1024
TS = 128
T = TP + TS
TT = 384
NTT = 3
D = 2048
KC = 16
DFF = 8192
ALPHA = 4.0 ** 0.25
EPS = 1e-5
NEG = -30000.0
STAGE = 2
SAME_ENG_INORDER = ("pe", "dve", "act")


class _Op:
    __slots__ = ("eng", "fn", "deps", "dma", "inc", "val", "waits", "ep", "strict")


class Prog:
    ENGS = ("pe", "act", "dve", "pool", "sp")

    def __init__(self, nc):
        self.nc = nc
        self.ops = []
        self.last_w = {}
        self.readers = {}
        self.epoch = 0
        self.strict = True

    def add(self, eng, fn, reads=(), writes=(), dma=None):
        i = len(self.ops)
        deps = set()
        for t in reads:
            w = self.last_w.get(t)
            if w is not None:
                deps.add(w)
        for t in writes:
            w = self.last_w.get(t)
            if w is not None:
                deps.add(w)
            r = self.readers.get(t)
            if r:
                deps.update(r)
        op = _Op()
        op.eng, op.fn, op.deps, op.dma = eng, fn, deps, dma
        op.inc, op.val, op.waits = False, 0, None
        op.ep = self.epoch
        op.strict = self.strict
        for t in writes:
            self.last_w[t] = i
            self.readers[t] = []
        for t in reads:
            self.readers.setdefault(t, []).append(i)
        self.ops.append(op)
        return i

    def barrier(self, new_epoch=False):
        toks = list(self.last_w.keys() | self.readers.keys())
        idxs = []
        for e in self.ENGS:
            idxs.append(self.add(e, None, reads=(), writes=toks))
        if new_epoch:
            self.epoch += 1
        return idxs

    def finalize(self):
        ops = self.ops
        for op in ops:
            for d in op.deps:
                dep = ops[d]
                if dep.dma is None:
                    if dep.eng == op.eng and op.dma is None and (dep.eng == "pe" or (dep.eng in SAME_ENG_INORDER and not (op.strict or dep.strict))):
                        continue
                    dep.inc = True
        cnt = {}
        scnt = {}
        for op in ops:
            if op.dma is None:
                if op.inc:
                    k_ = (op.eng, op.ep)
                    cnt[k_] = cnt.get(k_, 0) + 1
                    op.val = cnt[k_]
            else:
                scnt[op.dma] = scnt.get(op.dma, 0) + 16
                op.val = scnt[op.dma]
        seen = {e: {} for e in self.ENGS}
        for op in ops:
            need = {}
            for d in op.deps:
                dep = ops[d]
                if dep.dma is None:
                    if dep.eng == op.eng and op.dma is None and (dep.eng == "pe" or (dep.eng in SAME_ENG_INORDER and not (op.strict or dep.strict))):
                        continue
                    key = ("e", dep.eng, dep.ep)
                else:
                    key = ("d", dep.dma)
                if dep.val > need.get(key, 0):
                    need[key] = dep.val
            s = seen[op.eng]
            w = []
            for key, v in need.items():
                if s.get(key, 0) < v:
                    s[key] = v
                    w.append((key, v))
            op.waits = w
        self.streams = sorted(scnt.keys(), key=str)
        self.nep = self.epoch + 1
        print("ops", len(ops), "epochs", self.nep, "streams", len(self.streams), "max eng cnt", max(cnt.values()), "max dma cnt", max(scnt.values()))

    def emit(self):
        nc = self.nc
        self.finalize()
        NSET = self.nep
        pool_ = [{e: nc.alloc_semaphore(name=f"se_{e}_{k}") for e in self.ENGS} for k in range(NSET)]
        sems = {}
        for ep in range(self.nep):
            for e in self.ENGS:
                sems[("e", e, ep)] = pool_[ep % NSET][e]
        for k, s in enumerate(self.streams):
            sems[("d", s)] = nc.alloc_semaphore(name=f"sd_{k}")
        ops = self.ops

        def run(engname, e):
            for op in ops:
                if op.eng != engname:
                    continue
                for key, v in op.waits:
                    e.wait_ge(sems[key], v)
                if op.fn is None:
                    if op.inc:
                        e.nop().then_inc(sems[("e", engname, op.ep)], 1)
                    continue
                ins = op.fn(e)
                if op.dma is not None:
                    ins.then_inc(sems[("d", op.dma)], 16)
                elif op.inc:
                    ins.then_inc(sems[("e", engname, op.ep)], 1)

        with nc.Block() as block:
            @block.tensor
            def _(e):
                run("pe", e)

            @block.scalar
            def _(e):
                run("act", e)

            @block.vector
            def _(e):
                run("dve", e)

            @block.gpsimd
            def _(e):
                run("pool", e)

            @block.sync
            def _(e):
                run("sp", e)


class Builder:
    def __init__(self, nc):
        self.nc = nc
        self.P = Prog(nc)
        self.dram = {}

    def mm(self, out, lhsT, rhs, start, stop, r, w):
        self.P.add("pe", lambda e: e.matmul(out, lhsT, rhs, start=start, stop=stop), r, w)

    def tr(self, out, in_, ident, r, w):
        self.P.add("pe", lambda e: e.transpose(out, in_, ident), r, w)

    def act(self, out, in_, func, r, w, bias=None, scale=None):
        kw = {}
        if bias is not None:
            kw["bias"] = bias
        if scale is not None:
            kw["scale"] = scale
        self.P.add("act", lambda e: e.activation(out, in_, func, **kw), r, w)

    def tt(self, eng, out, in0, in1, op, r, w):
        self.P.add(eng, lambda e: e.tensor_tensor(out, in0, in1, op), r, w)

    def ts(self, eng, out, in0, s1, s2, op0, op1, r, w):
        if s2 is None:
            self.P.add(eng, lambda e: e.tensor_scalar(out, in0, s1, None, op0), r, w)
        else:
            self.P.add(eng, lambda e: e.tensor_scalar(out, in0, s1, s2, op0, op1), r, w)

    def stt(self, eng, out, in0, scalar, in1, op0, op1, r, w):
        self.P.add(eng, lambda e: e.scalar_tensor_tensor(out, in0, scalar, in1, op0, op1), r, w)

    def cp(self, eng, out, in_, r, w):
        if eng == "act":
            self.P.add("act", lambda e: e.activation(out, in_, AF.Copy), r, w)
        else:
            self.P.add(eng, lambda e: e.tensor_copy(out, in_), r, w)

    def memset(self, eng, ap, val, w):
        self.P.add(eng, lambda e: e.memset(ap, val), (), w)

    def dma(self, q, out, in_, r, w, stream):
        w = list(w)
        if stream == "prm":
            w.append("_prmchain")
        self.P.add(q, lambda e: e.dma_start(out=out, in_=in_), r, w, dma=stream)


def _sig(eng_b, out, in_, r, w, scale=1.0):
    eng_b.act(out, in_, AF.Sigmoid, r, w, scale=scale)


def build_program():
    nc = bass.Bass("TRN2", target_bir_lowering=False)
    B = Builder(nc)
    P = B.P

    def din(name, shape, dt=F32):
        return nc.dram_tensor(name, list(shape), dt, kind="ExternalInput").ap()

    def dout(name, shape, dt=F32):
        return nc.dram_tensor(name, list(shape), dt, kind="ExternalOutput").ap()

    def dint(name, shape, dt=F32):
        return nc.dram_tensor(name, list(shape), dt, kind="Internal").ap()

    xTp = din("xTp", [8 * 128, KC * 1024])
    xTs = din("xTs", [128, KC * 128])
    pTp = [din(f"pTp{l}", [8 * 128, 2 * 1024]) for l in range(2)]
    pTs = [din(f"pTs{l}", [128, 2 * 128]) for l in range(2)]
    h0re = din("h0re", [128, 64 * 16])
    h0im = din("h0im", [128, 64 * 16])
    sLR = din("sLR", [128, 2048])
    sLI = din("sLI", [128, 2048])
    sDT = din("sDT", [128, 2048])
    sBre = din("sBre", [128, 2048])
    sBim = din("sBim", [128, 2048])
    lr2 = din("lr2", [128, 64])
    li2 = din("li2", [128, 64])
    dt2 = din("dt2", [128, 64])
    cpad = din("cpad", [16, 128, 1024])
    vecs = din("vecs", [128, 13 * 16])
    sbb = din("sbb", [128, 16])
    brow = din("brow", [2, 128])
    m01 = din("m01", [2, 2])
    ptab = din("ptab", [128, 256], I32)
    ridx = din("ridx", [128, 8], I32)
    cmask = din("cmask", [128, 8])
    c_ident = din("c_ident", [128, 128])
    c_trineg = din("c_trineg", [128, 128])
    c_onesneg = din("c_onesneg", [128, 128])
    c_onesmean = din("c_onesmean", [128, 128])
    c_trimask = din("c_trimask", [128, 128])
    c_smask = din("c_smask", [8, 128])
    c_dsamp = din("c_dsamp", [128, 128])
    c_m96 = din("c_m96", [128, 1])
    c_iota = din("c_iota", [128, 1])
    w_gate = din("s5_w_gate", [D, D])
    w_out = din("s5_w_out", [D, D])
    w_k = din("w_k", [D, 512])
    w_v = din("w_v", [D, 512])
    w_q = din("sb_w_q", [D, D])
    w_o = din("sb_w_o", [D, D])
    w1 = [din(f"mlp_w1_{l}", [D, DFF]) for l in range(2)]
    w2 = [din(f"mlp_w2_{l}", [DFF, D]) for l in range(2)]
    wpp = [din(f"ple_w_proj_{l}", [256, D]) for l in range(2)]
    wpg = [din(f"ple_w_gate_{l}", [D, D]) for l in range(2)]
    if STAGE == 2:
        cache_k = din("cache_k", [2560 * 128, 512])
        cache_v = din("cache_v", [2560 * 128, 512])

    o_y = dout("o_y", [128, KC * T])
    o_sre = dout("o_sre", [128, 64])
    o_sim = dout("o_sim", [128, 64])
    o_ssre = dout("o_ssre", [128, 64 * 16])
    o_ssim = dout("o_ssim", [128, 64 * 16])
    o_kTp = dout("o_kTp", [8 * 128, 4096])
    o_vp = dout("o_vp", [8 * 128, 4096])
    o_kTs = dout("o_kTs", [128, 512])
    o_vs = dout("o_vs", [128, 512])

    KTa = [dint(f"KTa{k}", [8 * 128, 1024], BF16) for k in range(4)]
    Va = [dint(f"Va{k}", [8 * 128, 1024], BF16) for k in range(4)]
    KTs_d = dint("KTs_d", [128, 512], BF16)
    Vs_d = dint("Vs_d", [128, 512], BF16)
    X1 = [dint(f"X1_{k}", [8 * 128, 8 * T]) for k in range(2)]
    X1s = dint("X1s", [128, KC * 128])
    TAB = [dint(f"TAB{k}", [128, 3 * 512]) for k in range(64)]

    es = ExitStack()
    _u = {"n": 0}

    def sbt(name, shape, dt=F32):
        _u["n"] += 1
        return nc.sbuf_tensor(f"{name}_u{_u['n']}", list(shape), dt)

    def sb(name, shape, dt=F32):
        return es.enter_context(nc.sbuf_tensor(name, list(shape), dt))

    def ps(name, shape, dt=F32):
        return es.enter_context(nc.psum_tensor(name, list(shape), dt))

    with es:
        xres = sb("xres", [128, KC, T])
        xb = sb("xb", [128, KC, T], BF16)
        hA = sb("hA", [128, KC, T], BF16)
        vec = sb("vec", [128, 13 * 16])
        ident = sb("ident", [128, 128], BF16)
        trineg = sb("trineg", [128, 128], BF16)
        onesneg = sb("onesneg", [128, 128], BF16)
        onesmean = sb("onesmean", [128, 128], BF16)
        trimask = sb("trimask", [128, 128], BF16)
        smask = sb("smask", [8, 128], BF16)
        PS = [ps(f"ps{i}", [128, 512]) for i in range(8)]

        def pstok(i):
            return f"ps{i}"

        cst_f = xres[:, 0, 0:128]
        for nm, src, dst, npart in (("ident", c_ident, ident, 128), ("trineg", c_trineg, trineg, 128),
                                    ("onesneg", c_onesneg, onesneg, 128), ("onesmean", c_onesmean, onesmean, 128),
                                    ("trimask", c_trimask, trimask, 128), ("smask", c_smask, smask, 8)):
            B.dma("sp", cst_f[0:npart, :], src, (), ["xres0"], "cst")
            B.cp("dve", dst[0:npart, :], cst_f[0:npart, :], ["xres0"], [nm])
        B.dma("sp", vec[:], vecs, (), ["vec"], "vec")

        def vcol(i, kc):
            return vec[:, i * 16 + kc:i * 16 + kc + 1]


        cfg = {"tts": None, "ncols": 0}

        def load_x(ti):
            nco = cfg["ncols"]
            src = xTp[ti * 128:(ti + 1) * 128, :].rearrange("p (k t) -> p k t", k=KC)
            for g4 in range(4):
                ks = slice(4 * g4, 4 * g4 + 4)
                B.dma("sp", xres[:, ks, 0:1024], src[:, ks, :], (), [f"xres{k}" for k in range(4 * g4, 4 * g4 + 4)], f"xl{g4}")
            if nco > 1024:
                B.dma("sp", xres[:, :, 1024:1152], xTs.rearrange("p (k t) -> p k t", k=KC), (), [f"xres{k}" for k in range(KC)], "xls")
            P.strict = False
            for kc in range(KC):
                B.cp("act", xb[:, kc, 0:nco], xres[:, kc, 0:nco], [f"xres{kc}"], [f"xb{kc}"])
            P.strict = True

        es_ssm = ExitStack()
        es.enter_context(es_ssm)

        def sbs(name, shape, dt=F32):
            return es_ssm.enter_context(sbt(name, list(shape), dt))
        BBT = sbs("BBT", [128, 16, 2, 128], BF16)
        m96t = sbs("m96t", [128, 1])
        sp2 = sbs("sp2", [128, 8, 64])
        LRc, LIc, DTc, Rr, Cc, Sc, Ta, Tb = [sp2[:, i, :] for i in range(8)]
        Hst = sbs("Hst", [128, 2, 64])
        B.dma("sp", m96t[:], c_m96, (), ["m96t"], "prm")
        B.dma("sp", LRc, lr2, (), ["sp2"], "prm")
        B.dma("sp", LIc, li2, (), ["sp2"], "prm")
        B.dma("sp", DTc, dt2, (), ["sp2"], "prm")
        B.memset("dve", Hst[:].rearrange("p a b -> p (a b)"), 0.0, ["Hst"])

        def lam_bar(lr, li, dtt, mag, c, s, t1, t2, tok):
            B.act(dtt, dtt, AF.Exp, tok, tok)
            B.tt("dve", t1, lr, dtt, ALU.mult, tok, tok)
            B.act(mag, t1, AF.Exp, tok, tok)
            B.tt("dve", t1, li, dtt, ALU.mult, tok, tok)
            B.act(s, t1, AF.Sin, tok, tok, scale=1.0 / 16)
            B.ts("dve", t2, t1, 1.0 / 16, math.pi / 2, ALU.mult, ALU.add, tok, tok)
            B.act(c, t2, AF.Sin, tok, tok)
            for _ in range(4):
                B.tt("dve", t1, c, c, ALU.mult, tok, tok)
                B.tt("dve", t2, s, s, ALU.mult, tok, tok)
                B.stt("dve", s, c, 2.0, s, ALU.mult, ALU.mult, tok, tok)
                B.tt("dve", c, t1, t2, ALU.subtract, tok, tok)

        lam_bar(LRc, LIc, DTc, Rr, Cc, Sc, Ta, Tb, ["sp2"])

        with ExitStack() as es3:
            sl = es3.enter_context(sbt("sl", [128, 12, 512], F32))
            for q in range(4):
                cs_ = slice(q * 512, (q + 1) * 512)
                tk = ["sl"]
                for i, src in enumerate((sLR, sLI, sDT, sBre, sBim)):
                    B.dma("sp", sl[:, i, :], src[:, cs_], (), tk, "prm")
                lr, li, dtt, bre_, bim_, mag, c, s, t1, t2, fr, fi = [sl[:, i, :] for i in range(12)]
                lam_bar(lr, li, dtt, mag, c, s, t1, t2, tk)
                B.tt("dve", c, mag, c, ALU.mult, tk, tk)
                B.tt("dve", s, mag, s, ALU.mult, tk, tk)
                B.ts("dve", c, c, -1.0, None, ALU.add, None, tk, tk)
                B.tt("dve", t1, lr, lr, ALU.mult, tk, tk)
                B.tt("dve", t2, li, li, ALU.mult, tk, tk)
                B.tt("dve", mag, t1, t2, ALU.add, tk, tk)
                P.add("dve", lambda e, o=mag: e.reciprocal(o, o), tk, tk)
                B.tt("dve", t1, c, lr, ALU.mult, tk, tk)
                B.tt("dve", t2, s, li, ALU.mult, tk, tk)
                B.tt("dve", t1, t1, t2, ALU.add, tk, tk)
                B.tt("dve", fr, t1, mag, ALU.mult, tk, tk)
                B.tt("dve", t1, s, lr, ALU.mult, tk, tk)
                B.tt("dve", t2, c, li, ALU.mult, tk, tk)
                B.tt("dve", t1, t1, t2, ALU.subtract, tk, tk)
                B.tt("dve", fi, t1, mag, ALU.mult, tk, tk)
                B.tt("dve", t1, fr, bre_, ALU.mult, tk, tk)
                B.tt("dve", t2, fi, bim_, ALU.mult, tk, tk)
                B.tt("dve", BBT[:, 4 * q:4 * q + 4, 0, :], t1.rearrange("p (a b) -> p a b", a=4),
                     t2.rearrange("p (a b) -> p a b", a=4), ALU.subtract, tk, ["BBT"])
                B.tt("dve", t1, fr, bim_, ALU.mult, tk, tk)
                B.tt("dve", t2, fi, bre_, ALU.mult, tk, tk)
                B.tt("dve", BBT[:, 4 * q:4 * q + 4, 1, :], t1.rearrange("p (a b) -> p a b", a=4),
                     t2.rearrange("p (a b) -> p a b", a=4), ALU.add, tk, ["BBT"])
        P.barrier()

        def ssm_phase(ti):
            with ExitStack() as es2:
                def sb2(name, shape, dt=F32):
                    return es2.enter_context(sbt(name, list(shape), dt))
                tabt = [sb2(f"tabt{i}", [128, 3, 512]) for i in range(2)]
                tsel = {"i": 0}
                decs = [sb2(f"dec{i}", [128, 512]) for i in range(2)]
                if ti == 0:
                    dsr = sb2("dsr", [128, 128])
                    dsamp = sb2("dsamp", [128, 128])
                nbb = 1 if ti == 0 else 2
                bres = [sb2(f"bre{i}", [128, 512]) for i in range(nbb)]; bims = [sb2(f"bim{i}", [128, 512]) for i in range(nbb)]
                t1 = sb2("t1", [128, 512]); t2 = sb2("t2", [128, 512])
                rre = sb2("rre", [128, 512]); rim = sb2("rim", [128, 512])
                gres = [sb2(f"gre{i}", [128, 512]) for i in range(2)]; gims = [sb2(f"gim{i}", [128, 512]) for i in range(2)]
                p1 = sb2("p1", [128, 512]); p2 = sb2("p2", [128, 512])
                gsel = {"n": 0}
                hreb = sb2("hreb", [128, 512], BF16); nhimb = sb2("nhimb", [128, 512], BF16)
                cpb = [sb2(f"cpb{b}", [128, 4, 2, 128], BF16) for b in range(2)]
                hl = sb2("hl", [128, 8])
                if ti == 0:
                    h0t = sb2("h0t", [128, 2, 16])
                    hst = sb2("hst", [128, 2, 16])
                nsm = sb2("nsm", [128, 2])
                bbz = sb2("bbz", [128, 2, 128], BF16)
                if ti == 0:
                    B.dma("sp", dsamp[:], c_dsamp, (), ["dsamp"], "prm")
                tk = ["tab0"]

                def cur_tabs():
                    tt_ = tabt[tsel["i"]]
                    return tt_[:, 0, :], tt_[:, 1, :], tt_[:, 2, :]

                def gen_tables(pair):
                    tsel["i"] = pair % 2
                    tk[0] = f"tab{pair % 2}"
                    ct, st, nst = cur_tabs()
                    dec = decs[pair % 2]
                    dtk = f"dec{pair % 2}"
                    if ti > 0:
                        B.dma("sp", tabt[pair % 2][:].rearrange("p a b -> p (a b)"), TAB[pair], ["TAB"], [tk[0]], f"tabl{pair % 2}")
                        B.ts("pool", dec[:], ct, 0.0, Rr[:, pair:pair + 1], ALU.mult, ALU.add, tk + ["sp2"], [dtk])
                        return
                    geng = "pool" if pair % 2 == 0 else "dve"
                    B.cp(geng, ct[:, 0:1], Cc[:, pair:pair + 1], ["sp2"] + tk, tk)
                    B.cp(geng, st[:, 0:1], Sc[:, pair:pair + 1], ["sp2"] + tk, tk)
                    m = 1
                    dk = tk + [dtk]
                    while m < 512:
                        cm = ct[:, m - 1:m]; sm = st[:, m - 1:m]
                        B.ts(geng, nsm[:, 0:1], sm, -1.0, None, ALU.mult, None, tk + ["nsm"], ["nsm"])
                        B.ts(geng, nst[:, 0:m], ct[:, 0:m], cm, None, ALU.mult, None, dk, dk)
                        B.ts(geng, dec[:, 0:m], st[:, 0:m], nsm[:, 0:1], None, ALU.mult, None, dk + ["nsm"], dk)
                        B.tt(geng, ct[:, m:2 * m], nst[:, 0:m], dec[:, 0:m], ALU.add, dk, dk)
                        B.ts(geng, nst[:, 0:m], ct[:, 0:m], sm, None, ALU.mult, None, dk, dk)
                        B.ts(geng, dec[:, 0:m], st[:, 0:m], cm, None, ALU.mult, None, dk, dk)
                        B.tt(geng, st[:, m:2 * m], nst[:, 0:m], dec[:, 0:m], ALU.add, dk, dk)
                        m *= 2
                    B.ts(geng, nst, st, -1.0, None, ALU.mult, None, tk, tk)
                    B.dma("sp", TAB[pair], tabt[pair % 2][:].rearrange("p a b -> p (a b)"), list(tk), ["TAB"], f"tabs{pair % 2}")
                    B.ts(geng, dec[:], ct, 0.0, Rr[:, pair:pair + 1], ALU.mult, ALU.add, tk + ["sp2"], [dtk])
                    if ti == 0:
                        B.ts("pool", dsr[:], dsamp[:], Rr[:, pair:pair + 1], None, ALU.mult, None, ["dsamp", "sp2"], ["dsr"])

                def ssm_tile(pair, kind, init_re, init_im, init_tok):
                    ct, st, nst = cur_tabs()
                    tk = [f"tab{pair % 2}"]
                    gi = gsel["n"] % 2
                    gsel["n"] += 1
                    gre, gim = gres[gi], gims[gi]
                    gtr, gti = f"gre{gi}", f"gim{gi}"
                    reng = "dve" if ti == 0 else "pool"
                    q1, q2 = (t1, t2) if ti == 0 else (p1, p2)
                    q1t, q2t = ("t1", "t2") if ti == 0 else ("p1", "p2")
                    chunk, j = pair // 4, pair % 4
                    rows = slice(32 * j, 32 * j + 32) if j < 3 else slice(64, 128)
                    if kind < 2:
                        N = 512; c0 = 512 * kind
                    else:
                        N = 128; c0 = 1024
                    P.strict = (kind == 2)
                    bi_ = gi % nbb
                    bre, bim = bres[bi_], bims[bi_]
                    bret, bimt = f"bre{bi_}", f"bim{bi_}"
                    pri, pii = (0, 1) if bi_ == 0 else (5, 6)
                    pr_, pi_ = PS[pri], PS[pii]
                    B.mm(pr_[:, 0:N], (BBT[rows, chunk, 0, :] if j < 3 else bbz[rows, 0, :]), xb[rows, chunk, c0:c0 + N], True, True, ["BBT", "bbz", f"xb{chunk}"], [f"ps{pri}"])
                    B.mm(pi_[:, 0:N], (BBT[rows, chunk, 1, :] if j < 3 else bbz[rows, 1, :]), xb[rows, chunk, c0:c0 + N], True, True, ["BBT", "bbz", f"xb{chunk}"], [f"ps{pii}"])
                    B.cp("act", bre[:, 0:N], pr_[:, 0:N], [f"ps{pri}"], [bret])
                    B.cp("act", bim[:, 0:N], pi_[:, 0:N], [f"ps{pii}"], [bimt])

                    def v(ap):
                        return ap[:, 0:N].rearrange("p (a b) -> p a b", b=8) if kind == 2 else ap[:, 0:N]

                    def tb(ap):
                        return ap[:, 0:8].unsqueeze(1).broadcast_to([128, 16, 8]) if kind == 2 else ap[:, 0:N]
                    B.tt("dve", v(t1), v(bre), tb(ct), ALU.mult, [bret] + tk, ["t1"])
                    B.tt("dve", v(t2), v(bim), tb(st), ALU.mult, [bimt] + tk, ["t2"])
                    B.tt("dve", rre[:, 0:N], t1[:, 0:N], t2[:, 0:N], ALU.add, ["t1", "t2"], ["rre"])
                    B.tt("dve", v(t1), v(bim), tb(ct), ALU.mult, [bimt] + tk, ["t1"])
                    B.tt("dve", v(t2), v(bre), tb(st), ALU.mult, [bret] + tk, ["t2"])
                    B.tt("dve", rim[:, 0:N], t1[:, 0:N], t2[:, 0:N], ALU.subtract, ["t1", "t2"], ["rim"])
                    if kind == 2:
                        B.dma("sp", h0t[:, 0, :], h0re[:, pair * 16:(pair + 1) * 16], (), ["h0t"], "h0")
                        B.dma("sp", h0t[:, 1, :], h0im[:, pair * 16:(pair + 1) * 16], (), ["h0t"], "h0")
                        B.stt("dve", v(rre)[:, :, 0], h0t[:, 0, :], Rr[:, pair:pair + 1], v(rre)[:, :, 0], ALU.mult, ALU.add, ["h0t", "sp2", "rre"], ["rre"])
                        B.stt("dve", v(rim)[:, :, 0], h0t[:, 1, :], Rr[:, pair:pair + 1], v(rim)[:, :, 0], ALU.mult, ALU.add, ["h0t", "sp2", "rim"], ["rim"])
                        dec_ = dsr; dtok = "dsr"
                    else:
                        dec_ = decs[pair % 2]; dtok = f"dec{pair % 2}"
                    P.add("dve", lambda e: e.tensor_tensor_scan(gre[:, 0:N], dec_[:, 0:N], rre[:, 0:N], init_re, ALU.mult, ALU.add),
                          ["rre", dtok] + init_tok, [gtr])
                    P.add("dve", lambda e: e.tensor_tensor_scan(gim[:, 0:N], dec_[:, 0:N], rim[:, 0:N], init_im, ALU.mult, ALU.add),
                          ["rim", dtok] + init_tok, [gti])
                    P.strict = True
                    if kind < 2:
                        L = N - 1
                        B.tt("dve", hl[:, 2:3], gre[:, L:N], ct[:, L:N], ALU.mult, [gtr] + tk, ["hl2"])
                        B.stt("dve", hl[:, 0:1], gim[:, L:N], nst[:, L:N], hl[:, 2:3], ALU.mult, ALU.add, [gti, "hl2"] + tk, ["hl"])
                        B.tt("dve", hl[:, 3:4], gre[:, L:N], st[:, L:N], ALU.mult, [gtr] + tk, ["hl3"])
                        B.stt("dve", hl[:, 1:2], gim[:, L:N], ct[:, L:N], hl[:, 3:4], ALU.mult, ALU.add, [gti, "hl3"] + tk, ["hl"])
                    else:
                        B.ts("dve", hst[:, 0, :], v(gre)[:, :, 7], ct[:, 7:8], None, ALU.mult, None, [gtr] + tk, ["hst0"])
                        B.stt("dve", hst[:, 0, :], v(gim)[:, :, 7], nst[:, 7:8], hst[:, 0, :], ALU.mult, ALU.add, [gti, "hst0"] + tk, ["hst0"])
                        B.ts("dve", hst[:, 1, :], v(gre)[:, :, 7], st[:, 7:8], None, ALU.mult, None, [gtr] + tk, ["hst1"])
                        B.stt("dve", hst[:, 1, :], v(gim)[:, :, 7], ct[:, 7:8], hst[:, 1, :], ALU.mult, ALU.add, [gti, "hst1"] + tk, ["hst1"])
                        B.dma("sp", o_ssre[:, pair * 16:(pair + 1) * 16], hst[:, 0, :], ["hst0"], ["o_ssre"], "hso0")
                        B.dma("sp", o_ssim[:, pair * 16:(pair + 1) * 16], hst[:, 1, :], ["hst1"], ["o_ssim"], "hso1")
                    P.strict = (kind == 2)
                    B.tt(reng, v(q1), v(gre), tb(ct), ALU.mult, [gtr] + tk, [q1t])
                    B.tt(reng, v(q2), v(gim), tb(st), ALU.mult, [gti] + tk, [q2t])
                    B.tt(reng, hreb[:, 0:N], q1[:, 0:N], q2[:, 0:N], ALU.subtract, [q1t, q2t], ["hreb"])
                    B.tt(reng, v(q1), v(gre), tb(nst), ALU.mult, [gtr] + tk, [q1t])
                    B.tt(reng, v(q2), v(gim), tb(ct), ALU.mult, [gti] + tk, [q2t])
                    B.tt(reng, nhimb[:, 0:N], q1[:, 0:N], q2[:, 0:N], ALU.subtract, [q1t, q2t], ["nhimb"])
                    py = PS[2 + kind]
                    cb = cpb[chunk % 2]
                    B.mm(py[:, 0:N], cb[:, j, 0, :], hreb[:, 0:N], j == 0, False, [f"cpb{chunk % 2}", "hreb"], [f"ps{2 + kind}"])
                    B.mm(py[:, 0:N], cb[:, j, 1, :], nhimb[:, 0:N], False, j == 3, [f"cpb{chunk % 2}", "nhimb"], [f"ps{2 + kind}"])
                    P.strict = True

                kinds = 3 if ti == 0 else 2
                for chunk in range(16):
                    cb = cpb[chunk % 2]
                    B.dma("pool", cb[:].rearrange("p a b c -> p (a b c)"), cpad[chunk], (), [f"cpb{chunk % 2}"], f"cpb{chunk % 2}")
                    B.ts("dve", bbz[:].rearrange("p a b -> p (a b)"), BBT[:, chunk, :, :].rearrange("p a b -> p (a b)"), m96t[:, 0:1], None,
                         ALU.mult, None, ["BBT", "m96t"], ["bbz"])
                    for j in range(4):
                        pair = chunk * 4 + j
                        gen_tables(pair)
                        ssm_tile(pair, 0, Hst[:, 0, pair:pair + 1], Hst[:, 1, pair:pair + 1], ["Hst"])
                        ssm_tile(pair, 1, hl[:, 0:1], hl[:, 1:2], ["hl"])
                        B.cp("dve", Hst[:, 0, pair:pair + 1], hl[:, 0:1], ["hl"], ["Hst"])
                        B.cp("dve", Hst[:, 1, pair:pair + 1], hl[:, 1:2], ["hl"], ["Hst"])
                        if ti == 0:
                            ssm_tile(pair, 2, 0.0, 0.0, [])
                    for kind in range(kinds):
                        N = 512 if kind < 2 else 128
                        c0 = 512 * kind
                        py = PS[2 + kind]
                        xr = xres[:, chunk, c0:c0 + N]
                        P.strict = (kind == 2)
                        B.stt("dve", rre[:, 0:N], xr, vcol(0, chunk), py[:, 0:N], ALU.mult, ALU.add, [f"xres{chunk}", "vec", f"ps{2 + kind}"], ["rre"])
                        B.act(t1[:, 0:N], rre[:, 0:N], AF.Square, ["rre"], ["t1"])
                        B.ts("dve", t1[:, 0:N], t1[:, 0:N], 0.044715, 1.0, ALU.mult, ALU.add, ["t1"], ["t1"])
                        B.tt("dve", t1[:, 0:N], t1[:, 0:N], rre[:, 0:N], ALU.mult, ["t1", "rre"], ["t1"])
                        B.act(t2[:, 0:N], t1[:, 0:N], AF.Sigmoid, ["t1"], ["t2"], scale=1.5957691216057308)
                        B.tt("dve", hA[:, chunk, c0:c0 + N], rre[:, 0:N], t2[:, 0:N], ALU.mult, ["rre", "t2"], [f"hA{chunk}"])
                        P.strict = True
                if ti == 7:
                    B.dma("sp", o_sre, Hst[:, 0, :], ["Hst"], ["o_sre"], "osp")
                    B.dma("sp", o_sim, Hst[:, 1, :], ["Hst"], ["o_sim"], "osp")
            P.barrier()

        dn = {}
        st_ = {"w": 0, "ps": 0, "e": 0}

        def dense_scope(stack):
            def sb5(name, shape, dt=F32):
                return stack.enter_context(sbt(name, list(shape), dt))
            dn["wbuf"] = [sb5(f"wbuf{i}", [128, 16, 256], BF16) for i in range(3)]
            dn["wps"] = [sb5(f"wps{i}", [128, 2, 256], BF16) for i in range(3)]
            dn["pbt"] = sb5("pbt", [128, 2, T], BF16)
            dn["mean_sb"] = sb5("mean_sb", [128, T])
            dn["rstd_sb"] = sb5("rstd_sb", [128, T])
            dn["etmp"] = [sb5(f"etmp{i}", [128, 512]) for i in range(2)]
            dn["epsb"] = sb5("epsb", [128, 1])
            B.memset("dve", dn["epsb"][:], EPS, ["epsb"])

        def next_ps():
            i = st_["ps"] % 6
            st_["ps"] += 1
            return i

        def next_et():
            i = st_["e"] % 2
            st_["e"] += 1
            return i

        def wview(W, r0, kcn, m0, n):
            return W[r0:r0 + kcn * 128, :].rearrange("(kc p) n -> p kc n", p=128)[:, :, m0:m0 + n]

        WB = {}

        def linear(W, r0, kcn, n_out, in_tile, in_name, epi, extra_load=None, wkey=None):
            wbuf = dn["wbuf"]
            tiles = [(mp * 256, min(256, n_out * 128 - mp * 256)) for mp in range((n_out * 128 + 255) // 256)]
            slots = []
            wmode = cfg.get("wmode") if wkey is not None else None
            if wmode is not None and wkey not in WB:
                WB[wkey] = [dint(f"WB_{wkey}_{i}", [128, kcn * 256], BF16) for i in range(len(tiles))]

            def load(i):
                s = st_["w"] % 3
                st_["w"] += 1
                m0, n = tiles[i]
                if wmode == "use":
                    B.dma("sp", wbuf[s][:, 0:kcn, :].rearrange("p a b -> p (a b)"), WB[wkey][i], [f"WB_{wkey}_{i}"], [f"wbuf{s}"], f"wbuf{s}")
                else:
                    B.dma("pool", wbuf[s][:, 0:kcn, 0:n], wview(W, r0, kcn, m0, n), (), [f"wbuf{s}"], f"wbuf{s}")
                    if wmode == "fill":
                        B.dma("sp", WB[wkey][i], wbuf[s][:, 0:kcn, :].rearrange("p a b -> p (a b)"), [f"wbuf{s}"], [f"WB_{wkey}_{i}"], f"wbs{s}")
                if extra_load is not None:
                    extra_load(s, m0, n)
                slots.append(s)
            for i in range(min(2, len(tiles))):
                load(i)
            for i, (m0, n) in enumerate(tiles):
                s = slots[i]
                for mi in range(n // 128):
                    m = m0 // 128 + mi
                    for (c0, nt) in cfg["tts"]:
                        pi_ = next_ps()
                        P.strict = (nt < 512)
                        for kc in range(kcn):
                            B.mm(PS[pi_][:, 0:nt], wbuf[s][:, kc, mi * 128:(mi + 1) * 128], in_tile[:, kc, c0:c0 + nt],
                                 kc == 0, kc == kcn - 1, [f"wbuf{s}", f"{in_name}{kc}"], [f"ps{pi_}"])
                        epi(m, (c0, nt), PS[pi_][:, 0:nt], f"ps{pi_}", s, mi)
                        P.strict = True
                if i + 2 < len(tiles):
                    load(i + 2)

        def epi_resid(first):
            def f(m, tt, pp, ptok, s, mi):
                c0, nt = tt
                xr = xres[:, m, c0:c0 + nt]
                if first:
                    B.stt("dve", xr, xr, ALPHA, pp, ALU.mult, ALU.add, [f"xres{m}", ptok], [f"xres{m}"])
                else:
                    B.tt("dve", xr, xr, pp, ALU.add, [f"xres{m}", ptok], [f"xres{m}"])
            return f

        def layer_norm(gi, bi):
            nco = cfg["ncols"]
            mean_sb, rstd_sb, epsb = dn["mean_sb"], dn["rstd_sb"], dn["epsb"]
            P.strict = False
            for kc in range(KC):
                B.cp("act", xb[:, kc, 0:nco], xres[:, kc, 0:nco], [f"xres{kc}"], [f"xb{kc}"])
                B.act(hA[:, kc, 0:nco], xres[:, kc, 0:nco], AF.Square, [f"xres{kc}"], [f"hA{kc}"])
            for (c0, nt) in cfg["tts"]:
                P.strict = (nt < 512)
                p1, p2 = next_ps(), next_ps()
                for kc in range(KC):
                    B.mm(PS[p1][:, 0:nt], onesmean[:], xb[:, kc, c0:c0 + nt], kc == 0, kc == KC - 1, ["onesmean", f"xb{kc}"], [f"ps{p1}"])
                for kc in range(KC):
                    B.mm(PS[p2][:, 0:nt], onesmean[:], hA[:, kc, c0:c0 + nt], kc == 0, kc == KC - 1, ["onesmean", f"hA{kc}"], [f"ps{p2}"])
                ms = mean_sb[:, c0:c0 + nt]
                rs = rstd_sb[:, c0:c0 + nt]
                B.cp("dve", ms, PS[p1][:, 0:nt], [f"ps{p1}"], ["mean_sb"])
                B.tt("dve", rs, ms, ms, ALU.mult, ["mean_sb"], ["rstd_sb"])
                B.tt("dve", rs, PS[p2][:, 0:nt], rs, ALU.subtract, [f"ps{p2}", "rstd_sb"], ["rstd_sb"])
                B.act(rs, rs, AF.Sqrt, ["rstd_sb", "epsb"], ["rstd_sb"], bias=epsb[:, 0:1])
                P.add("dve", lambda e, o=rs: e.reciprocal(o, o), ["rstd_sb"], ["rstd_sb"])
            P.strict = False
            for kc in range(KC):
                xr = xres[:, kc, 0:nco]
                B.tt("dve", xr, xr, mean_sb[:, 0:nco], ALU.subtract, [f"xres{kc}", "mean_sb"], [f"xres{kc}"])
                B.tt("dve", xr, xr, rstd_sb[:, 0:nco], ALU.mult, [f"xres{kc}", "rstd_sb"], [f"xres{kc}"])
                B.act(xr, xr, AF.Identity, [f"xres{kc}", "vec"], [f"xres{kc}"], bias=vcol(bi, kc), scale=vcol(gi, kc))
                B.cp("act", xb[:, kc, 0:nco], xr, [f"xres{kc}"], [f"xb{kc}"])
            P.strict = True

        def mlp(l):
            etmp = dn["etmp"]
            for q in range(4):
                def epi1(m, tt, pp, ptok, s, mi):
                    c0, nt = tt
                    k = next_et()
                    B.act(etmp[k][:, 0:nt], pp, AF.Relu, [ptok], [f"etmp{k}"])
                    B.tt("dve", hA[:, m, c0:c0 + nt], etmp[k][:, 0:nt], etmp[k][:, 0:nt], ALU.mult, [f"etmp{k}"], [f"hA{m}"])
                linear(w1[l][:, q * 2048:(q + 1) * 2048], 0, 16, 16, xb, "xb", epi1, wkey=f"w1_{l}_{q}")
                linear(w2[l], q * 2048, 16, 16, hA, "hA", epi_resid(q == 0), wkey=f"w2_{l}_{q}")

        def ple(l, psrc_list):
            etmp, pbt, wps = dn["etmp"], dn["pbt"], dn["wps"]
            for (dst_cols, src_ap) in psrc_list:
                B.dma("pool", pbt[:, :, dst_cols], src_ap, (), ["pbt"], "pbt")

            def xload(s, m0, n):
                B.dma("pool", wps[s][:, :, 0:n], wpp[l].rearrange("(kc p) n -> p kc n", p=128)[:, :, m0:m0 + n], (), [f"wps{s}"], f"wps{s}")

            def epi(m, tt, pp, ptok, s, mi):
                c0, nt = tt
                k = next_et()
                B.act(etmp[k][:, 0:nt], pp, AF.Sigmoid, [ptok], [f"etmp{k}"])
                p2 = next_ps()
                for kc in range(2):
                    B.mm(PS[p2][:, 0:nt], wps[s][:, kc, mi * 128:(mi + 1) * 128], pbt[:, kc, c0:c0 + nt], kc == 0, kc == 1, [f"wps{s}", "pbt"], [f"ps{p2}"])
                B.tt("dve", etmp[k][:, 0:nt], PS[p2][:, 0:nt], etmp[k][:, 0:nt], ALU.mult, [f"ps{p2}", f"etmp{k}"], [f"etmp{k}"])
                xr = xres[:, m, c0:c0 + nt]
                B.stt("dve", xr, xr, ALPHA, etmp[k][:, 0:nt], ALU.mult, ALU.add, [f"xres{m}", f"etmp{k}"], [f"xres{m}"])
            linear(wpg[l], 0, 16, 16, xb, "xb", epi, extra_load=xload, wkey=f"pg{l}")

        def epi_gate(m, tt, pp, ptok, s, mi):
            c0, nt = tt
            etmp = dn["etmp"]
            k = next_et()
            B.act(etmp[k][:, 0:nt], pp, AF.Sigmoid, [ptok], [f"etmp{k}"])
            B.tt("dve", xb[:, m, c0:c0 + nt], hA[:, m, c0:c0 + nt], etmp[k][:, 0:nt], ALU.mult, [f"hA{m}", f"etmp{k}"], [f"xb{m}"])

        def psrc(l, ti, with_s):
            lst = [(slice(0, 1024), pTp[l][ti * 128:(ti + 1) * 128, :].rearrange("p (k t) -> p k t", k=2))]
            if with_s:
                lst.append((slice(1024, 1152), pTs[l].rearrange("p (k t) -> p k t", k=2)))
            return lst

        def kv_proj(ti):
            etmp, wbuf = dn["etmp"], dn["wbuf"]

            def epi_k(m, tt, pp, ptok, s, mi):
                c0, nt = tt
                k = next_et()
                B.cp("dve", etmp[k][:, 0:nt], pp, [ptok], [f"etmp{k}"])
                if c0 < 1024:
                    B.dma("sp", o_kTp[ti * 128:(ti + 1) * 128, m * 1024 + c0:m * 1024 + c0 + nt], etmp[k][:, 0:nt], [f"etmp{k}"], ["o_kTp"], f"okt{k}")
                    B.dma("pool", KTa[m][ti * 128:(ti + 1) * 128, c0:c0 + nt], etmp[k][:, 0:nt], [f"etmp{k}"], ["KTa"], f"okb{k}")
                else:
                    B.dma("sp", o_kTs[:, m * 128:(m + 1) * 128], etmp[k][:, 0:nt], [f"etmp{k}"], ["o_kTs"], f"okt{k}")
                    B.dma("pool", KTs_d[:, m * 128:(m + 1) * 128], etmp[k][:, 0:nt], [f"etmp{k}"], ["KTs_d"], f"okb{k}")
            linear(w_k, 0, 16, 4, xb, "xb", epi_k, wkey="wk")
            for hf in range(2):
                B.dma("pool", wbuf[hf][:], wview(w_v, 0, 16, hf * 256, 256), (), [f"wbuf{hf}"], f"wbuf{hf}")
            nblk = cfg["ncols"] // 128
            P.strict = False
            for tb_ in range(nblk):
                pi_ = next_ps()
                for hf in range(2):
                    for kc in range(KC):
                        B.mm(PS[pi_][:, hf * 256:(hf + 1) * 256], xb[:, kc, tb_ * 128:(tb_ + 1) * 128], wbuf[hf][:, kc, :], kc == 0, kc == KC - 1,
                             [f"xb{kc}", f"wbuf{hf}"], [f"ps{pi_}"])
                k = next_et()
                B.cp("dve", etmp[k][:], PS[pi_][:, :], [f"ps{pi_}"], [f"etmp{k}"])
                if tb_ < 8:
                    B.dma("sp", o_vp[ti * 128:(ti + 1) * 128, tb_ * 512:(tb_ + 1) * 512], etmp[k][:], [f"etmp{k}"], ["o_vp"], f"okt{k}")
                    for kvh in range(4):
                        B.dma("pool", Va[kvh][ti * 128:(ti + 1) * 128, tb_ * 128:(tb_ + 1) * 128], etmp[k][:, kvh * 128:(kvh + 1) * 128],
                              [f"etmp{k}"], ["Va"], f"okb{k}")
                else:
                    B.dma("sp", o_vs, etmp[k][:], [f"etmp{k}"], ["o_vs"], f"okt{k}")
                    B.dma("pool", Vs_d, etmp[k][:], [f"etmp{k}"], ["Vs_d"], f"okb{k}")
            P.strict = True

        NT0 = 8
        for ti in range(NT0):
            if ti == 0:
                cfg["ncols"] = T
                cfg["tts"] = [(0, 512), (512, 512), (1024, 128)]
            else:
                cfg["ncols"] = 1024
                cfg["tts"] = [(0, 512), (512, 512)]
            cfg["wmode"] = "fill" if ti == 0 else "use"
            load_x(ti)
            ssm_phase(ti)
            with ExitStack() as esd:
                dense_scope(esd)
                linear(w_gate, 0, 16, 16, hA, "hA", epi_gate, wkey="wg")
                linear(w_out, 0, 16, 16, xb, "xb", epi_resid(True), wkey="wo")
                layer_norm(1, 2)
                mlp(0)
                layer_norm(3, 4)
                ple(0, psrc(0, ti, ti == 0))
                layer_norm(5, 6)
                kv_proj(ti)
                for hf in range(2):
                    B.dma("sp", X1[hf][ti * 128:(ti + 1) * 128, :], xres[:, 8 * hf:8 * hf + 8, :].rearrange("p a b -> p (a b)"),
                          [f"xres{k}" for k in range(8 * hf, 8 * hf + 8)], ["X1"], f"x1s{hf}")
                if ti == 0:
                    B.dma("sp", X1s.rearrange("p (k t) -> p k t", k=KC), xres[:, :, 1024:1152], [f"xres{k}" for k in range(KC)], ["X1s"], "x1ss")
                P.barrier(new_epoch=True)
        cfg["wmode"] = None
        es_ssm.close()

        cfg["ncols"] = T
        cfg["tts"] = [(0, 512), (512, 512), (1024, 128)]
        ridx_sb = sb("ridx_sb", [128, 8], I32)
        B.dma("sp", ridx_sb[:], ridx, (), ["ridx_sb"], "prm")
        for hf in range(2):
            P.add("pool", lambda e, hf=hf: e.indirect_dma_start(
                out=xres[:, 8 * hf:8 * hf + 8, :].rearrange("p a b -> p (a b)"), out_offset=None, in_=X1[hf],
                in_offset=bass.IndirectOffsetOnAxis(ap=ridx_sb[:, 0:1], axis=0)),
                ["ridx_sb", "X1"], [f"xres{k}" for k in range(8 * hf, 8 * hf + 8)], dma=f"x1g{hf}")
        B.dma("sp", xres[:, :, 1024:1152], X1s.rearrange("p (k t) -> p k t", k=KC), ["X1s"], [f"xres{k}" for k in range(KC)], "x1sl")
        for kc in range(KC):
            B.cp("act", xb[:, kc, :], xres[:, kc, :], [f"xres{kc}"], [f"xb{kc}"])

        with ExitStack() as esd:
            dense_scope(esd)

            def epi_q(m, tt, pp, ptok, s, mi):
                c0, nt = tt
                B.act(hA[:, m, c0:c0 + nt], pp, AF.Copy, [ptok], [f"hA{m}"], scale=128.0 ** -0.5)
            linear(w_q, 0, 16, 16, xb, "xb", epi_q)
            P.barrier()

        if STAGE == 2:
            with ExitStack() as esa:
                def sba(name, shape, dt=F32):
                    return esa.enter_context(sbt(name, list(shape), dt))
                biasT = sba("biasT", [128, 16, 8])
                sbb_sb = sba("sbb_sb", [128, 16])
                cm_sb = sba("cm_sb", [128, 8])
                zerosT = sba("zerosT", [128, 128], BF16)
                def mk_state(nm):
                    return {"nm": nm, "Rf": sba(f"Rf{nm}", [128, 512]), "Rb": [sba(f"Rb{nm}{i}", [128, 512], BF16) for i in range(2)], "r": 0}
                SW = [mk_state("a"), mk_state("b")]
                Ef = [sba(f"Ef{i}", [128, 512]) for i in range(3)]
                Gf = [sba(f"Gf{i}", [128, 512]) for i in range(3)]
                Lpb = [sba(f"Lpb{i}", [128, 512], BF16) for i in range(2)]
                wb = [sba(f"wb{i}", [128, 512], BF16) for i in range(3)]
                B.dma("sp", sbb_sb[:], sbb, (), ["sbb_sb"], "prm")
                B.dma("sp", cm_sb[:], cmask, (), ["cm_sb"], "prm")
                B.memset("dve", zerosT[:], 0.0, ["zerosT"])
                for j in range(8):
                    B.ts("dve", biasT[:, :, j], sbb_sb[:], cm_sb[:, j:j + 1], None, ALU.add, None, ["sbb_sb", "cm_sb"], ["biasT"])
                bc = {"n": 0}

                pend = {"p": None}

                fcnt = {"n": 0}

                def pe_fence():
                    fcnt["n"] += 1
                    tkn = f"fence{fcnt['n']}"
                    B.mm(PS[7][0:8, 0:8], ident[0:8, 0:8], ident[0:8, 0:8], True, True, ["ident"], [tkn])
                    return tkn

                def sb_block(S, nk, N, qk_list, bias_ap, mask_fn, first, roff, wv_list):
                    i = bc["n"] % 2
                    wi = bc["n"] % 3
                    bc["n"] += 1
                    pa = i
                    A = PS[pa]
                    P.strict = (N < 512)
                    nq = len(qk_list)
                    started = False
                    if mask_fn is not None and mask_fn[0] == "pre":
                        for (o_, l_, r_, rt) in mask_fn[1]:
                            B.mm(o_(A), l_, r_, not started, False, rt, [f"ps{pa}"])
                            started = True
                    for qi, (oc, l_, r_, rt) in enumerate(qk_list):
                        last = (qi == nq - 1) and not (mask_fn is not None and mask_fn[0] == "post")
                        B.mm(A[0:nk, oc], l_, r_, not started, last, rt, [f"ps{pa}"])
                        if len(qk_list) == 1:
                            started = True
                    if mask_fn is not None and mask_fn[0] == "post":
                        for (o_, l_, r_, rt) in mask_fn[1]:
                            B.mm(o_(A), l_, r_, False, True, rt, [f"ps{pa}"])
                    E = Ef[wi][0:nk, 0:N]; L = Lpb[i][0:nk, 0:N]
                    fz = []
                    if S.get("fence"):
                        fz = [pe_fence()]
                    if bias_ap is not None:
                        B.act(E, A[0:nk, 0:N], AF.Exp, [f"ps{pa}", "biasT"] + fz, [f"Ef{wi}"], bias=bias_ap)
                    else:
                        B.act(E, A[0:nk, 0:N], AF.Exp, [f"ps{pa}"] + fz, [f"Ef{wi}"])
                    B.act(L, E, AF.Ln, [f"Ef{wi}"], [f"Lpb{i}"], bias=1.0)
                    P.strict = True
                    cur = (S, i, nk, N, first, roff, wv_list, wi)
                    if not S.get("pipe", True):
                        stage2(*cur)
                        stage3(*cur)
                        return
                    q = pend.setdefault("q", [])
                    q.append(cur)
                    if len(q) >= 2:
                        stage2(*q[-2])
                    if len(q) >= 4:
                        stage3(*q[0])
                        q.pop(0)

                def flush_blocks():
                    q = pend.setdefault("q", [])
                    if q:
                        stage2(*q[-1])
                    for c_ in q:
                        stage3(*c_)
                    q.clear()

                def stage2(S, i, nk, N, first, roff, wv_list, wi):
                    pb_ = 2 + i
                    Bk = PS[pb_]
                    P.strict = (N < 512)
                    E = Ef[wi][0:nk, 0:N]; L = Lpb[i][0:nk, 0:N]; G = Gf[wi][0:nk, 0:N]; W = wb[wi][0:nk, 0:N]
                    Rf, Rb, nm = S["Rf"], S["Rb"], S["nm"]
                    rcur = S["r"]
                    B.mm(Bk[0:nk, 0:N], trineg[0:nk, 0:nk], L, True, first, ["trineg", f"Lpb{i}"], [f"ps{pb_}"])
                    if not first:
                        B.mm(Bk[0:nk, 0:N], onesneg[:, 0:nk], Rb[rcur][:, roff:roff + N], False, True, ["onesneg", f"Rb{nm}{rcur}"], [f"ps{pb_}"])
                    rn = 1 - rcur
                    reng_ = S.get("r_eng", "dve")
                    B.tt(reng_, Rf[0:nk, roff:roff + N], Rf[0:nk, roff:roff + N], L, ALU.add, [f"Rf{nm}", f"Lpb{i}"], [f"Rf{nm}"])
                    B.cp(reng_, Rb[rn][:, :], Rf[:, :], [f"Rf{nm}"], [f"Rb{nm}{rn}"])
                    S["r"] = rn
                    fz = []
                    if S.get("fence"):
                        fz = [pe_fence()]
                    B.act(G, Bk[0:nk, 0:N], AF.Exp, [f"ps{pb_}"] + fz, [f"Gf{wi}"])
                    B.tt(S.get("w_eng", "dve"), W, E, G, ALU.mult, [f"Ef{wi}", f"Gf{wi}"], [f"wb{wi}"])
                    P.strict = True

                def stage3(S, i, nk, N, first, roff, wv_list, wi):
                    P.strict = (N < 512)
                    for (po, ptok, l_, wc, st0, sp0, rt) in wv_list:
                        B.mm(po, l_, wb[wi][0:nk, wc], st0, sp0, rt + [f"wb{wi}"], [ptok])
                    P.strict = True

                with ExitStack() as esp:
                    def sbp(name, shape, dt=F32):
                        return esp.enter_context(sbt(name, list(shape), dt))
                    kbuf = [sbp(f"kbuf{i}", [128, 1024], BF16) for i in range(3)]
                    vbuf = [sbp(f"vbuf{i}", [128, 1024], BF16) for i in range(3)]

                    def gather(hj):
                        h_, j_ = hj // 8, hj % 8
                        g_ = hj % 3
                        kvh_ = h_ // 4
                        P.add("pool", lambda e: e.indirect_dma_start(
                            out=kbuf[g_][:], out_offset=None, in_=KTa[kvh_],
                            in_offset=bass.IndirectOffsetOnAxis(ap=ridx_sb[:, j_:j_ + 1], axis=0)),
                            ["ridx_sb", "KTa"], [f"kbuf{g_}"], dma=f"kg{g_}")
                        P.add("pool", lambda e: e.indirect_dma_start(
                            out=vbuf[g_][:], out_offset=None, in_=Va[kvh_],
                            in_offset=bass.IndirectOffsetOnAxis(ap=ridx_sb[:, j_:j_ + 1], axis=0)),
                            ["ridx_sb", "Va"], [f"vbuf{g_}"], dma=f"vg{g_}")
                    gather(0)
                    gcount = {"n": 0}
                    for h in range(16):
                        kvh = h // 4
                        POs = [PS[4], PS[5]]
                        for qt in range(2):
                            B.memset("pool", SW[qt]["Rf"][:], 0.0, [f"Rf{SW[qt]['nm']}"])
                            B.mm(POs[qt][:, 0:512], zerosT[:], hA[:, h, qt * 512:(qt + 1) * 512], True, False, ["zerosT", f"hA{h}"], [f"ps{4 + qt}"])
                        firsts = [True, True]
                        for j in range(8):
                            hj = h * 8 + j
                            g = hj % 3
                            if hj + 1 < 128:
                                gather(hj + 1)
                            for kb in range(7, -1, -1):
                                for qt in ((1, 0) if j == 0 else (0, 1)):
                                    if j == 0 and kb > 4 * qt + 3:
                                        continue
                                    diag = (j == 0 and kb >= 4 * qt)
                                    off = (kb - 4 * qt) * 128 if diag else 0
                                    N = 512 - off
                                    q0 = qt * 512 + off
                                    qk = [(slice(0, N), kbuf[g][:, kb * 128:(kb + 1) * 128], hA[:, h, q0:q0 + N], [f"kbuf{g}", f"hA{h}"])]
                                    mk = None
                                    if diag:
                                        mk = ("post", [(lambda A_: A_[:, 0:128], ident[:], trimask[:], ["ident", "trimask"])])
                                    last_blk = (j == 7 and kb == 0)
                                    wv = [(POs[qt][:, off:512], f"ps{4 + qt}", vbuf[g][:, kb * 128:(kb + 1) * 128], slice(0, N), False, last_blk, [f"vbuf{g}"])]
                                    sb_block(SW[qt], 128, N, qk, biasT[:, h, j:j + 1], mk, firsts[qt], off, wv)
                                    firsts[qt] = False
                        flush_blocks()
                        for qt in range(2):
                            B.cp("act", hA[:, h, qt * 512:(qt + 1) * 512], POs[qt][:, 0:512], [f"ps{4 + qt}"], [f"hA{h}"])
                P.barrier()

                with ExitStack() as ess:
                    def sbq(name, shape, dt=F32):
                        return ess.enter_context(sbt(name, list(shape), dt))
                    xbf = xb[:].rearrange("p a b -> p (a b)")
                    Kraw = xbf[:, 0:8192].rearrange("p (a b) -> p a b", b=512)
                    Vraw = sbq("Vraw", [128, 16, 512], BF16)
                    KTt = xbf[:, 8192:16384].rearrange("p (a b) -> p a b", b=2048)
                    qS = sbq("qS", [128, 16, 128], BF16)
                    kTs = sbq("kTs", [128, 512], BF16)
                    vnew = sbq("vnew", [8, 512], BF16)
                    pt_sb = sbq("pt_sb", [128, 256], I32)
                    idx_all = sbq("idx_all", [128, 256], I32)
                    iota_c = sbq("iota_c", [128, 1])
                    br_f = sbq("br_f", [2, 128]); br_h = sbq("br_h", [2, 128], BF16); br_l = sbq("br_l", [2, 128])
                    br2 = sbq("br2", [2, 128], BF16); m01_sb = sbq("m01_sb", [2, 2]); ones2 = sbq("ones2", [2, 128], BF16)
                    B.dma("sp", pt_sb[:], ptab, (), ["pt_sb"], "prm")
                    B.dma("sp", iota_c[:], c_iota, (), ["iota_c"], "prm")
                    B.dma("sp", br_f[:], brow, (), ["br"], "prm")
                    B.dma("sp", m01_sb[:], m01, (), ["br"], "prm")
                    B.dma("sp", kTs[:], KTs_d, ["KTs_d"], ["kTs"], "prm")
                    B.ts("dve", idx_all[:], pt_sb[:], 128.0, iota_c[:, 0:1], ALU.mult, ALU.add, ["pt_sb", "iota_c"], ["idx_all"])
                    B.cp("dve", br_h[:], br_f[:], ["br"], ["br"])
                    B.tt("dve", br_l[:], br_f[:], br_h[:], ALU.subtract, ["br"], ["br"])
                    B.cp("dve", br_f[:], br_h[:], ["br"], ["br"])
                    B.ts("dve", br_f[:], br_f[:], m01_sb[:, 0:1], None, ALU.mult, None, ["br"], ["br"])
                    B.stt("dve", br2[:], br_l[:], m01_sb[:, 1:2], br_f[:], ALU.mult, ALU.add, ["br"], ["br2"])
                    B.memset("dve", ones2[:], 1.0, ["ones2"])
                    B.cp("dve", qS[:].rearrange("p b (h t) -> p b h t", t=8),
                         hA[:, :, 1024:1152].rearrange("p h (b t) -> p b h t", t=8), [f"hA{h}" for h in range(16)], ["qS"])
                    PTb = PS[6][:].bitcast(BF16)
                    for b in range(16):
                        for pg in range(16):
                            col = b * 16 + pg
                            P.add("pool", lambda e, pg=pg, col=col: e.indirect_dma_start(
                                out=Kraw[:, pg, :], out_offset=None, in_=cache_k,
                                in_offset=bass.IndirectOffsetOnAxis(ap=idx_all[:, col:col + 1], axis=0)),
                                ["idx_all"], [f"Kraw{pg}"], dma=f"ckg{pg % 4}")
                            P.add("pool", lambda e, pg=pg, col=col: e.indirect_dma_start(
                                out=Vraw[:, pg, :], out_offset=None, in_=cache_v,
                                in_offset=bass.IndirectOffsetOnAxis(ap=idx_all[:, col:col + 1], axis=0)),
                                ["idx_all"], [f"Vraw{pg}"], dma=f"cvg{pg % 4}")
                        B.dma("sp", vnew[:], Vs_d[b * 8:(b + 1) * 8, :], ["Vs_d"], ["vnew"], "vnew")
                        for pg in range(16):
                            for kvh in range(4):
                                B.tr(PTb[:, kvh * 128:(kvh + 1) * 128], Kraw[:, pg, kvh * 128:(kvh + 1) * 128], ident[:], [f"Kraw{p_}" for p_ in range(16)] + ["ident"], ["ps6"])
                            ftk = pe_fence()
                            B.cp("dve", KTt[:, :, pg * 128:(pg + 1) * 128], PTb[:, 0:512].rearrange("p (k s) -> p k s", k=4), ["ps6", ftk], ["KTt"])
                        PO = PS[4 + b % 2]
                        potok = f"ps{4 + b % 2}"
                        SS = SW[b % 2]
                        SS["fence"] = False
                        B.memset("pool", SS["Rf"][:], 0.0, [f"Rf{SS['nm']}"])
                        first = True
                        for blk in range(16, -1, -1):
                            nk = 8 if blk == 16 else 128
                            pre = [(lambda A_, nk=nk: A_[0:nk, 0:128], ones2[0:2, 0:nk], br2[0:2, :], ["ones2", "br2"])]
                            if blk == 16:
                                pre.append((lambda A_: A_[0:8, 0:128], ident[0:8, 0:8], smask[0:8, :], ["ident", "smask"]))
                            qk = []
                            wv = []
                            for kvh in range(4):
                                if blk == 16:
                                    kt_ap = kTs[:, kvh * 128 + b * 8:kvh * 128 + b * 8 + 8]
                                    v_ap = vnew[0:8, kvh * 128:(kvh + 1) * 128]
                                    rt = ["kTs"]; vt = ["vnew"]
                                else:
                                    kt_ap = KTt[:, kvh, blk * 128:(blk + 1) * 128]
                                    v_ap = Vraw[:, blk, kvh * 128:(kvh + 1) * 128]
                                    rt = ["KTt"]; vt = [f"Vraw{p_}" for p_ in range(16)]
                                qk.append((slice(kvh * 32, kvh * 32 + 32), kt_ap, qS[:, b, kvh * 32:(kvh + 1) * 32], rt + ["qS"]))
                                wv.append((PO[:, kvh * 32:(kvh + 1) * 32], potok, v_ap, slice(kvh * 32, kvh * 32 + 32), blk == 16, blk == 0, vt))
                            sb_block(SS, nk, 128, qk, None, ("pre", pre), first, 0, wv)
                            first = False
                        flush_blocks()
                        ftk = pe_fence()
                        B.cp("act", hA[:, :, 1024 + b * 8:1024 + b * 8 + 8], PO[:, 0:128].rearrange("p (h t) -> p h t", t=8), [potok, ftk], [f"hA{h}" for h in range(16)])
                P.barrier()

        with ExitStack() as esd:
            dense_scope(esd)
            linear(w_o, 0, 16, 16, hA, "hA", epi_resid(True))
            layer_norm(7, 8)
            mlp(1)
            layer_norm(9, 10)
            pown = esd.enter_context(sbt("pown", [128, 2048], F32))
            P.add("pool", lambda e: e.indirect_dma_start(out=pown[:], out_offset=None, in_=pTp[1],
                                                         in_offset=bass.IndirectOffsetOnAxis(ap=ridx_sb[:, 0:1], axis=0)),
                  ["ridx_sb"], ["pown"], dma="pown")
            B.cp("dve", dn["pbt"][:, :, 0:1024], pown[:].rearrange("p (k t) -> p k t", k=2), ["pown"], ["pbt"])
            ple(1, [(slice(1024, 1152), pTs[1].rearrange("p (k t) -> p k t", k=2))])
            layer_norm(11, 12)
            for kc in range(KC):
                B.dma("sp", o_y[:, kc * T:(kc + 1) * T], xres[:, kc, :], [f"xres{kc}"], ["o_y"], "oy")
        P.add("sp", None, ["o_y", "o_sre", "o_sim", "o_ssre", "o_ssim", "o_kTp", "o_vp", "o_kTs", "o_vs"], ["done"])
        P.emit()
    return nc


_PROG = None


def _get_prog():
    global _PROG
    if _PROG is None:
        _PROG = build_program()
    return _PROG


def _fm(a, nchunk):
    t = a.shape[0]
    return np.ascontiguousarray(a.reshape(t, nchunk, 128).transpose(2, 1, 0).reshape(128, nchunk * t))


def _unfm(a, nchunk, t):
    return np.ascontiguousarray(a.reshape(128, nchunk, t).transpose(2, 1, 0).reshape(t, nchunk * 128))


def kernel(**inp):
    f32 = np.float32
    g = lambda k: np.asarray(inp[k])
    x_prompt = g("x_prompt").astype(f32, copy=False)[0]
    x_sample = g("x_sample").astype(f32, copy=False).reshape(1024, 2048)
    p_prompt = g("p_prompt").astype(f32, copy=False)[:, 0]
    p_sample = g("p_sample").astype(f32, copy=False).reshape(2, 1024, 256)
    sre = g("state_ssm_re")[0]
    sim = g("state_ssm_im")[0]
    ptab_full = g("page_table").astype(np.int32)

    def gp_to_pair(a):
        return np.ascontiguousarray(a.reshape(64, 2, 64).transpose(1, 2, 0).reshape(128, 64)).astype(f32)

    lam_re = g("s5_lam_re")[0]; lam_im = g("s5_lam_im")[0]
    log_dt = g("s5_log_dt")[0]
    ldt_gp = np.repeat(log_dt[:, None], 64, axis=1)
    lr2 = gp_to_pair(lam_re); li2 = gp_to_pair(lam_im); dt2 = gp_to_pair(ldt_gp)

    def feat_layout(a_gp):
        a = a_gp.reshape(16, 4, 2, 64)
        a = a.transpose(1, 2, 0, 3)
        a = np.broadcast_to(a[:, :, None, :, None, :], (4, 2, 16, 16, 2, 64))
        return np.ascontiguousarray(a.reshape(128, 2048)).astype(f32)

    sLR = feat_layout(lam_re); sLI = feat_layout(lam_im); sDT = feat_layout(ldt_gp)

    def bpad(b):
        a = b.reshape(16, 4, 2, 64, 16).transpose(1, 2, 4, 0, 3)
        out = np.zeros((4, 2, 16, 16, 2, 64), f32)
        for g2 in range(2):
            out[:, g2, :, :, g2, :] = a[:, g2]
        return out.reshape(128, 2048)

    sBre = bpad(g("s5_b_re")[0]); sBim = bpad(g("s5_b_im")[0])

    def cpad_fn(cre, cim):
        out = np.zeros((16, 2, 64, 4, 2, 4, 2, 16), f32)
        for ri, c in enumerate((cre, cim)):
            a = c.reshape(16, 4, 2, 16, 64)
            for j in range(4):
                for g2 in range(2):
                    out[:, g2, :, j, ri, j, g2, :] = a[:, j, g2].transpose(0, 2, 1)
        return out.reshape(16, 128, 1024)

    cpad = cpad_fn(g("s5_c_re")[0], g("s5_c_im")[0])

    def pk(v):
        return np.ascontiguousarray(v.reshape(16, 128).T)

    vlist = [g("s5_d")[0]]
    for nm in ("ln_mix", "ln_mlp", "ln_ple"):
        vlist += [g(nm + "_g")[0], g(nm + "_b")[0]]
    for nm in ("ln_mix", "ln_mlp", "ln_ple"):
        vlist += [g(nm + "_g")[1], g(nm + "_b")[1]]
    vecs = np.concatenate([pk(v) for v in vlist], axis=1).astype(f32)

    ii = np.arange(128)
    c_ident = np.eye(128, dtype=f32)
    c_trineg = -(ii[:, None] >= ii[None, :]).astype(f32)
    c_onesneg = -np.ones((128, 128), f32)
    c_onesmean = np.full((128, 128), 1.0 / 2048, f32)
    c_trimask = np.where(ii[:, None] >= ii[None, :], NEG, 0.0).astype(f32)
    c_smask = np.where(np.arange(8)[:, None] >= (np.arange(128) % 8)[None, :], NEG, 0.0).astype(f32)
    c_dsamp = np.broadcast_to((np.arange(128) % 8 != 0).astype(f32)[None, :], (128, 128)).copy()
    sbb = np.broadcast_to(g("sb_bias")[0][None, :], (128, 16)).astype(f32).copy()
    brow = np.broadcast_to(np.repeat(g("sb_bias")[0], 8)[None, :], (2, 128)).astype(f32).copy()
    m01 = np.eye(2, dtype=f32)

    shared = dict(sLR=sLR, sLI=sLI, sDT=sDT, sBre=sBre, sBim=sBim, lr2=lr2, li2=li2, dt2=dt2, cpad=cpad, vecs=vecs,
                  sbb=sbb, brow=brow, m01=m01, c_ident=c_ident, c_trineg=c_trineg, c_onesneg=c_onesneg,
                  c_onesmean=c_onesmean, c_trimask=c_trimask, c_smask=c_smask, c_dsamp=c_dsamp, c_m96=(np.arange(128) >= 96).astype(np.float32).reshape(128, 1),
                  s5_w_gate=g("s5_w_gate")[0], s5_w_out=g("s5_w_out")[0], w_k=g("w_k"), w_v=g("w_v"),
                  sb_w_q=g("sb_w_q")[0], sb_w_o=g("sb_w_o")[0],
                  mlp_w1_0=g("mlp_w1")[0], mlp_w1_1=g("mlp_w1")[1], mlp_w2_0=g("mlp_w2")[0], mlp_w2_1=g("mlp_w2")[1],
                  ple_w_proj_0=g("ple_w_proj")[0], ple_w_proj_1=g("ple_w_proj")[1],
                  ple_w_gate_0=g("ple_w_gate")[0], ple_w_gate_1=g("ple_w_gate")[1],
                  )
    if STAGE == 2:
        shared["cache_k"] = g("cache_k").reshape(2560 * 128, 512)
        shared["cache_v"] = g("cache_v").reshape(2560 * 128, 512)
    shared = {k: np.ascontiguousarray(v, dtype=f32) for k, v in shared.items()}

    shared["xTp"] = np.concatenate([_fm(x_prompt[i * TP:(i + 1) * TP], 16) for i in range(8)], axis=0)
    for l in range(2):
        shared[f"pTp{l}"] = np.concatenate([_fm(p_prompt[l, i * TP:(i + 1) * TP], 2) for i in range(8)], axis=0)
    shared["c_iota"] = np.arange(128, dtype=f32).reshape(128, 1)
    in_maps = []
    for c in range(NC):
        m = dict(shared)
        m["xTs"] = _fm(x_sample[c * TS:(c + 1) * TS], 16)
        for l in range(2):
            m[f"pTs{l}"] = _fm(p_sample[l, c * TS:(c + 1) * TS], 2)
        for nm, s in (("h0re", sre), ("h0im", sim)):
            a = s[c * 16:(c + 1) * 16].reshape(16, 64, 2, 64).transpose(2, 3, 1, 0)
            m[nm] = np.ascontiguousarray(a.reshape(128, 1024)).astype(f32)
        m["ptab"] = np.ascontiguousarray(np.broadcast_to(ptab_full[c * 16:(c + 1) * 16].reshape(1, 256), (128, 256))).astype(np.int32)
        r = np.array([((c - j) % NC) for j in range(8)], np.int32)
        m["ridx"] = (r[None, :] * 128 + np.arange(128, dtype=np.int32)[:, None]).astype(np.int32)
        cm = np.array([0.0 if (c - j) >= 0 else NEG for j in range(8)], f32)
        m["cmask"] = np.broadcast_to(cm[None, :], (128, 8)).copy()
        in_maps.append(m)

    nc = _get_prog()
    res = run_bass_kernel_spmd(nc, in_maps, core_ids=list(range(NC)))
    R = res.results

    y_all = [_unfm(np.asarray(R[c]["o_y"]), 16, T) for c in range(NC)]
    y_prompt = np.concatenate([y[:TP] for y in y_all], axis=0)[None]
    y_sample = np.concatenate([y[TP:] for y in y_all], axis=0).reshape(128, 8, 2048)

    def st_prompt(a):
        return np.ascontiguousarray(np.asarray(a).reshape(2, 64, 64).transpose(2, 0, 1).reshape(1, 1, 128, 64))
    sre_p = st_prompt(R[0]["o_sre"]); sim_p = st_prompt(R[0]["o_sim"])

    def st_sample(nm):
        outs = []
        for c in range(NC):
            a = np.asarray(R[c][nm]).reshape(2, 64, 64, 16).transpose(3, 2, 0, 1)
            outs.append(a.reshape(16, 128, 64))
        return np.ascontiguousarray(np.concatenate(outs, axis=0)[None])
    sre_s = st_sample("o_ssre"); sim_s = st_sample("o_ssim")

    kp = np.asarray(R[0]["o_kTp"]).reshape(8, 128, 4, 1024).transpose(0, 3, 2, 1)
    k_p = np.ascontiguousarray(kp.reshape(1, 8192, 4, 128))
    vp = np.asarray(R[0]["o_vp"]).reshape(8, 128, 8, 512).transpose(0, 2, 1, 3)
    v_p = np.ascontiguousarray(vp.reshape(1, 8192, 4, 128))
    k_s = np.concatenate([np.asarray(R[c]["o_kTs"]).reshape(128, 4, 16, 8).transpose(2, 3, 1, 0) for c in range(NC)], axis=0)
    v_s = np.concatenate([np.asarray(R[c]["o_vs"]).reshape(16, 8, 4, 128) for c in range(NC)], axis=0)
    outs = (y_prompt, y_sample, sre_p, sim_p, k_p, v_p, sre_s, sim_s, np.ascontiguousarray(k_s), np.ascontiguousarray(v_s))
    return tuple(np.ascontiguousarray(o, dtype=f32) for o in outs)
```
